# Optimizing a Trainium2 kernel written in Bass

```python
import math
import jax
import jax.numpy as jnp
from jax import lax
import numpy as np

D_MODEL = 2048
BATCH = 4
SEQ = 2048
DEPTH = 2
DEC_BATCH = 8
DEC_SEQ = 4096
PAST_LEN = 128

N_EVEN = (DEPTH + 1) // 2
N_ODD = DEPTH // 2
NORM_EPS = 1e-6
L2_EPS = 1e-6

RET_HEADS = 8
RET_DK = 64
RET_DV = 128
RET_CHUNK = 128
ROPE_BASE = 10000.0
RET_QK_W = RET_HEADS * RET_DK
RET_V_W = RET_HEADS * RET_DV
GM_GROUPS = 8
GM_GROUP_DIM = 128
GM_CHUNK = 128
GM_W = GM_GROUPS * GM_GROUP_DIM
EVEN_SPLITS = [RET_QK_W, 2 * RET_QK_W, 2 * RET_QK_W + RET_V_W, 2 * RET_QK_W + 2 * RET_V_W,
               2 * RET_QK_W + 2 * RET_V_W + GM_W]
EVEN_IN_W = 2 * RET_QK_W + 2 * RET_V_W + 2 * GM_W
EVEN_MIX_W = RET_V_W + GM_W
GDN_K_HEADS = 16
GDN_V_HEADS = 32
GDN_DK = 128
GDN_DV = 128
GDN_CHUNK = 64
GDN_CONV = 5
GDN_QK_W = GDN_K_HEADS * GDN_DK
GDN_V_W = GDN_V_HEADS * GDN_DV
GDN_CONV_W = 2 * GDN_QK_W + GDN_V_W
ODD_SPLITS = [GDN_CONV_W, GDN_CONV_W + GDN_V_W, GDN_CONV_W + GDN_V_W + GDN_V_HEADS,
              GDN_CONV_W + GDN_V_W + 2 * GDN_V_HEADS, GDN_CONV_W + GDN_V_W + 3 * GDN_V_HEADS]
ODD_IN_W = GDN_CONV_W + GDN_V_W + 4 * GDN_V_HEADS
D_FF = -(-8 * D_MODEL // (3 * 256)) * 256

kernel_name = 'hybrid_bidir_retention_gmlp_gdn_encoder'


def rms_norm(x, w):
    xf = x.astype(jnp.float32)
    y = xf * lax.rsqrt(jnp.mean(xf * xf, axis=-1, keepdims=True) + NORM_EPS)
    return (y * w.astype(jnp.float32)).astype(x.dtype)


def standardize(x):
    xf = x.astype(jnp.float32)
    xc = xf - jnp.mean(xf, axis=-1, keepdims=True)
    return xc * lax.rsqrt(jnp.mean(xc * xc, axis=-1, keepdims=True) + NORM_EPS)


def layer_norm(x, w, b):
    return (standardize(x) * w.astype(jnp.float32) + b.astype(jnp.float32)).astype(x.dtype)


def l2_normalize(x):
    xf = x.astype(jnp.float32)
    return xf * lax.rsqrt(jnp.sum(xf * xf, axis=-1, keepdims=True) + L2_EPS)


def apply_rotary(x, pos):
    half = x.shape[-1] // 2
    inv = ROPE_BASE ** (-jnp.arange(half, dtype=jnp.float32) / half)
    ang = pos.astype(jnp.float32)[:, None] * inv[None, :]
    cos = jnp.cos(ang)[None, :, None, :]
    sin = jnp.sin(ang)[None, :, None, :]
    xf = x.astype(jnp.float32)
    x1, x2 = xf[..., :half], xf[..., half:]
    return jnp.concatenate([x1 * cos - x2 * sin, x1 * sin + x2 * cos], axis=-1)


def exclusive_chunk_scan(u, decay, reverse):
    def step(s, u_n):
        return s * decay[:, None, None] + u_n, s
    _, states = lax.scan(step, jnp.zeros_like(u[0]), u, reverse=reverse)
    return states


def bidir_retention(q, k, v, log_gamma):
    B, S, H, dk = q.shape
    dv = v.shape[-1]
    C = RET_CHUNK
    N = S // C
    q = q.reshape(B, N, C, H, dk)
    k = k.reshape(B, N, C, H, dk)
    v = v.reshape(B, N, C, H, dv)
    lf, lb = log_gamma[0], log_gamma[1]
    idx = jnp.arange(C, dtype=jnp.float32)
    diff = idx[:, None] - idx[None, :]
    adiff = jnp.abs(diff)[None]
    dmat = jnp.where((diff >= 0)[None], jnp.exp(adiff * lf[:, None, None]),
                     jnp.exp(adiff * lb[:, None, None]))
    scores = jnp.einsum('bnihd,bnjhd->bnhij', q, k) * dmat
    out = jnp.einsum('bnhij,bnjhe->bnihe', scores, v)
    kf = k * jnp.exp((C - 1 - idx)[:, None] * lf[None, :])[None, None, :, :, None]
    s_f = exclusive_chunk_scan(jnp.einsum('bnjhd,bnjhe->nbhde', kf, v), jnp.exp(C * lf), False)
    qf = q * jnp.exp((idx + 1)[:, None] * lf[None, :])[None, None, :, :, None]
    out = out + jnp.einsum('bnihd,nbhde->bnihe', qf, s_f)
    kb = k * jnp.exp((idx + 1)[:, None] * lb[None, :])[None, None, :, :, None]
    s_b = exclusive_chunk_scan(jnp.einsum('bnjhd,bnjhe->nbhde', kb, v), jnp.exp(C * lb), True)
    qb = q * jnp.exp((C - 1 - idx)[:, None] * lb[None, :])[None, None, :, :, None]
    out = out + jnp.einsum('bnihd,nbhde->bnihe', qb, s_b)
    return out.reshape(B, S, H, dv)


def retention_gmlp_mixer(h, w_in, w_out, ret_decay_logit, ret_gn_w, gm_ln_w, gm_ln_b, gm_ws, gm_bs):
    B, S, _ = h.shape
    proj = h @ w_in
    q, k, v, g, gu, gv = jnp.split(proj, EVEN_SPLITS, axis=-1)
    pos = jnp.arange(S)
    q = apply_rotary(q.reshape(B, S, RET_HEADS, RET_DK), pos) * (RET_DK ** -0.5)
    k = apply_rotary(k.reshape(B, S, RET_HEADS, RET_DK), pos)
    v = v.reshape(B, S, RET_HEADS, RET_DV).astype(jnp.float32)
    log_gamma = jax.nn.log_sigmoid(ret_decay_logit.astype(jnp.float32))
    y = bidir_retention(q, k, v, log_gamma)
    y = standardize(y).reshape(B, S, RET_V_W) * ret_gn_w.astype(jnp.float32)
    y_a = (jax.nn.silu(g.astype(jnp.float32)) * y).astype(h.dtype)
    gu = jax.nn.gelu(gu)
    gv = layer_norm(jax.nn.gelu(gv), gm_ln_w, gm_ln_b)
    gv = gv.reshape(B, S // GM_CHUNK, GM_CHUNK, GM_GROUPS, GM_GROUP_DIM)
    mixed = jnp.einsum('gij,bnjgc->bnigc', gm_ws, gv) + gm_bs.T[None, None, :, :, None]
    y_b = gu * mixed.reshape(B, S, GM_W)
    return jnp.concatenate([y_a, y_b], axis=-1) @ w_out


def gdn_chunk(q, k, v, g, beta):
    B, S, H, dk = q.shape
    dv = v.shape[-1]
    C = GDN_CHUNK
    N = S // C

    def chunks(x):
        return x.astype(jnp.float32).reshape(B, N, C, H, -1).transpose(0, 3, 1, 2, 4)

    q, k, v = chunks(q), chunks(k), chunks(v)
    g = g.astype(jnp.float32).reshape(B, N, C, H).transpose(0, 3, 1, 2)
    beta = beta.astype(jnp.float32).reshape(B, N, C, H).transpose(0, 3, 1, 2)
    gc = jnp.cumsum(g, axis=-1)
    lower = jnp.tril(jnp.ones((C, C), dtype=bool))
    strict = jnp.tril(jnp.ones((C, C), dtype=bool), -1)
    diff = gc[..., :, None] - gc[..., None, :]
    decay = jnp.where(lower, jnp.exp(jnp.where(lower, diff, 0.0)), 0.0)
    kk = jnp.einsum('bhncd,bhnsd->bhncs', k, k)
    m = jnp.eye(C, dtype=jnp.float32) + jnp.where(strict, beta[..., :, None] * kk * decay, 0.0)

    def solve(rhs):
        return lax.linalg.triangular_solve(m, rhs, left_side=True, lower=True, unit_diagonal=True)

    u = solve(v * beta[..., None])
    w = solve(k * (beta * jnp.exp(gc))[..., None])
    qk = jnp.where(lower, jnp.einsum('bhncd,bhnsd->bhncs', q, k) * decay, 0.0)
    qg = q * jnp.exp(gc)[..., None]
    g_last = gc[..., -1:]
    kd = k * jnp.exp(g_last - gc)[..., None]
    eg = jnp.exp(g_last[..., 0])
    xs = tuple(jnp.moveaxis(t, 2, 0) for t in (u, w, qg, qk, kd, eg))

    def step(s, inp):
        u_n, w_n, qg_n, qk_n, kd_n, eg_n = inp
        v_new = u_n - jnp.einsum('bhcd,bhde->bhce', w_n, s)
        o_n = jnp.einsum('bhcd,bhde->bhce', qg_n, s) + jnp.einsum('bhcs,bhse->bhce', qk_n, v_new)
        s = s * eg_n[..., None, None] + jnp.einsum('bhcd,bhce->bhde', kd_n, v_new)
        return s, o_n

    _, o = lax.scan(step, jnp.zeros((B, H, dk, dv), jnp.float32), xs)
    return o.transpose(1, 0, 3, 2, 4).reshape(B, S, H, dv)


def gated_deltanet_mixer(h, w_in, conv_w, a_log, dt_bias, norm_w, w_out):
    B, S, _ = h.shape
    proj = h @ w_in
    qkv, z, b_f, b_b, a_f, a_b = jnp.split(proj, ODD_SPLITS, axis=-1)
    qkv = lax.conv_general_dilated(qkv, conv_w[:, None, :].astype(qkv.dtype), window_strides=(1,),
                                   padding=[(GDN_CONV // 2, GDN_CONV // 2)],
                                   dimension_numbers=('NWC', 'WIO', 'NWC'),
                                   feature_group_count=GDN_CONV_W)
    qkv = jax.nn.silu(qkv)
    q, k, v = jnp.split(qkv, [GDN_QK_W, 2 * GDN_QK_W], axis=-1)
    rep = GDN_V_HEADS // GDN_K_HEADS
    q = jnp.repeat(l2_normalize(q.reshape(B, S, GDN_K_HEADS, GDN_DK)) * (GDN_DK ** -0.5), rep, axis=2)
    k = jnp.repeat(l2_normalize(k.reshape(B, S, GDN_K_HEADS, GDN_DK)), rep, axis=2)
    v = v.reshape(B, S, GDN_V_HEADS, GDN_DV)
    rate = jnp.exp(a_log.astype(jnp.float32))
    dtb = dt_bias.astype(jnp.float32)
    g_f = -rate[0] * jax.nn.softplus(a_f.astype(jnp.float32) + dtb[0])
    g_b = -rate[1] * jax.nn.softplus(a_b.astype(jnp.float32) + dtb[1])
    beta_f = jax.nn.sigmoid(b_f.astype(jnp.float32))
    beta_b = jax.nn.sigmoid(b_b.astype(jnp.float32))
    o_f = gdn_chunk(q, k, v, g_f, beta_f)
    o_b = gdn_chunk(jnp.flip(q, 1), jnp.flip(k, 1), jnp.flip(v, 1), jnp.flip(g_b, 1), jnp.flip(beta_b, 1))
    o = o_f + jnp.flip(o_b, 1)
    z = z.reshape(B, S, GDN_V_HEADS, GDN_DV).astype(jnp.float32)
    o = rms_norm(o, norm_w) * jax.nn.silu(z)
    return o.reshape(B, S, GDN_V_W).astype(h.dtype) @ w_out


def swiglu_ffn(h, w_gate, w_up, w_down):
    return (jax.nn.silu(h @ w_gate) * (h @ w_up)) @ w_down


def encoder(x, norm1_w, norm2_w, final_norm_w, ev_w_in, ev_w_out, ev_ret_decay_logit, ev_ret_gn_w,
            ev_gm_ln_w, ev_gm_ln_b, ev_gm_ws, ev_gm_bs, od_w_in, od_conv_w, od_a_log, od_dt_bias,
            od_norm_w, od_w_out, ffn_w_gate, ffn_w_up, ffn_w_down):
    h = x
    for layer in range(DEPTH):
        i = layer // 2
        hn = rms_norm(h, norm1_w[layer])
        if layer % 2 == 0:
            mix = retention_gmlp_mixer(hn, ev_w_in[i], ev_w_out[i], ev_ret_decay_logit[i], ev_ret_gn_w[i],
                                       ev_gm_ln_w[i], ev_gm_ln_b[i], ev_gm_ws[i], ev_gm_bs[i])
        else:
            mix = gated_deltanet_mixer(hn, od_w_in[i], od_conv_w[i], od_a_log[i], od_dt_bias[i],
                                       od_norm_w[i], od_w_out[i])
        h = h + mix
        h = h + swiglu_ffn(rms_norm(h, norm2_w[layer]), ffn_w_gate[layer], ffn_w_up[layer], ffn_w_down[layer])
    return rms_norm(h, final_norm_w)


def setup_inputs(seed: int = 0) -> dict:
    key = jax.random.key(seed)
    ks = jax.random.split(key, 24)
    f32 = jnp.float32

    def nrm(k, shape, scale):
        return jax.random.normal(k, shape, f32) * scale

    ret_base = jnp.log(2.0 ** (5.0 + jnp.arange(RET_HEADS, dtype=f32)) - 1.0)
    dt = jnp.exp(jax.random.uniform(ks[16], (N_ODD, 2, GDN_V_HEADS), f32,
                                    minval=math.log(1e-3), maxval=math.log(1e-1)))
    return {
        'x_prompt': nrm(ks[0], (BATCH, SEQ, D_MODEL), 1.0),
        'x_sample': nrm(ks[1], (DEC_BATCH, DEC_SEQ, D_MODEL), 1.0),
        'norm1_w': 1.0 + nrm(ks[2], (DEPTH, D_MODEL), 0.02),
        'norm2_w': 1.0 + nrm(ks[3], (DEPTH, D_MODEL), 0.02),
        'final_norm_w': 1.0 + nrm(ks[4], (D_MODEL,), 0.02),
        'ev_w_in': nrm(ks[5], (N_EVEN, D_MODEL, EVEN_IN_W), D_MODEL ** -0.5),
        'ev_w_out': nrm(ks[6], (N_EVEN, EVEN_MIX_W, D_MODEL), EVEN_MIX_W ** -0.5),
        'ev_ret_decay_logit': ret_base[None, None, :] + nrm(ks[7], (N_EVEN, 2, RET_HEADS), 0.05),
        'ev_ret_gn_w': 1.0 + nrm(ks[8], (N_EVEN, RET_V_W), 0.02),
        'ev_gm_ln_w': 1.0 + nrm(ks[9], (N_EVEN, GM_W), 0.02),
        'ev_gm_ln_b': nrm(ks[10], (N_EVEN, GM_W), 0.02),
        'ev_gm_ws': nrm(ks[11], (N_EVEN, GM_GROUPS, GM_CHUNK, GM_CHUNK), GM_CHUNK ** -0.5),
        'ev_gm_bs': 1.0 + nrm(ks[12], (N_EVEN, GM_GROUPS, GM_CHUNK), 0.1),
        'od_w_in': nrm(ks[13], (N_ODD, D_MODEL, ODD_IN_W), D_MODEL ** -0.5),
        'od_conv_w': nrm(ks[14], (N_ODD, GDN_CONV, GDN_CONV_W), GDN_CONV ** -0.5),
        'od_a_log': jnp.log(jax.random.uniform(ks[15], (N_ODD, 2, GDN_V_HEADS), f32, minval=1.0, maxval=16.0)),
        'od_dt_bias': dt + jnp.log(-jnp.expm1(-dt)),
        'od_norm_w': 1.0 + nrm(ks[17], (N_ODD, GDN_DV), 0.02),
        'od_w_out': nrm(ks[18], (N_ODD, GDN_V_W, D_MODEL), GDN_V_W ** -0.5),
        'ffn_w_gate': nrm(ks[19], (DEPTH, D_MODEL, D_FF), D_MODEL ** -0.5),
        'ffn_w_up': nrm(ks[20], (DEPTH, D_MODEL, D_FF), D_MODEL ** -0.5),
        'ffn_w_down': nrm(ks[21], (DEPTH, D_FF, D_MODEL), D_FF ** -0.5),
    }


def reference(x_prompt, x_sample, norm1_w, norm2_w, final_norm_w, ev_w_in, ev_w_out, ev_ret_decay_logit,
              ev_ret_gn_w, ev_gm_ln_w, ev_gm_ln_b, ev_gm_ws, ev_gm_bs, od_w_in, od_conv_w, od_a_log,
              od_dt_bias, od_norm_w, od_w_out, ffn_w_gate, ffn_w_up, ffn_w_down):
    y_prompt = encoder(x_prompt, norm1_w, norm2_w, final_norm_w, ev_w_in, ev_w_out, ev_ret_decay_logit,
                       ev_ret_gn_w, ev_gm_ln_w, ev_gm_ln_b, ev_gm_ws, ev_gm_bs, od_w_in, od_conv_w, od_a_log,
                       od_dt_bias, od_norm_w, od_w_out, ffn_w_gate, ffn_w_up, ffn_w_down)
    y_sample = encoder(x_sample, norm1_w, norm2_w, final_norm_w, ev_w_in, ev_w_out, ev_ret_decay_logit,
                       ev_ret_gn_w, ev_gm_ln_w, ev_gm_ln_b, ev_gm_ws, ev_gm_bs, od_w_in, od_conv_w, od_a_log,
                       od_dt_bias, od_norm_w, od_w_out, ffn_w_gate, ffn_w_up, ffn_w_down)
    return (y_prompt, y_sample)
```

```python
from contextlib import ExitStack
import numpy as np
import ml_dtypes
import concourse.bass as bass
import concourse.mybir as mybir
from concourse.bass_utils import run_bass_kernel_spmd

F32 = mybir.dt.float32
BF16 = mybir.dt.bfloat16
AF = mybir.ActivationFunctionType
ALU = mybir.AluOpType
AX = mybir.AxisListType

D = 2048
DFF = 5632
NORM_EPS = 1e-6
L2_EPS = 1e-6
SAME_ENGINE_SYNC = True
import os as _os
DBG_LEVEL = int(_os.environ.get('K_DBG', '9'))
K_SUB = float(_os.environ.get('K_SUB', '9'))
NDSEM = 20


class Buf:
    __slots__ = ("name", "writers", "readers")

    def __init__(self, name):
        self.name = name
        self.writers = []
        self.readers = []


class Op:
    __slots__ = ("eng", "fn", "deps", "sig", "tick", "dma", "dsem", "dcount", "prev_slot")

    def __init__(self, eng, fn, dma=False):
        self.eng = eng
        self.fn = fn
        self.deps = []
        self.sig = False
        self.tick = 0
        self.dma = dma
        self.dsem = None
        self.dcount = 0
        self.prev_slot = None


class Prog:
    def __init__(self, nc):
        self.nc = nc
        self.es = ExitStack()
        self.eh = {"pe": nc.tensor, "act": nc.scalar, "dve": nc.vector, "pool": nc.gpsimd, "sp": nc.sync}
        self.ops = {e: [] for e in self.eh}
        self.sem = {e: self.es.enter_context(nc.semaphore("sem_" + e)) for e in self.eh}
        self.dsems = {q: [self.es.enter_context(nc.semaphore(f"dsem_{q}{i}")) for i in range(NDSEM)]
                      for q in ("sp", "pool", "act")}
        self.dn = {"sp": 0, "pool": 0, "act": 0}
        self.dlast = {q: [None] * NDSEM for q in self.dn}
        self.n_alloc = 0
        self.out_dmas = []

    def sb(self, shape, dtype, name=None):
        self.n_alloc += 1
        name = name or f"sb{self.n_alloc}"
        t = self.es.enter_context(self.nc.sbuf_tensor(f"{name}_{self.n_alloc}", list(shape), dtype))
        return TT(t, Buf(name))

    def ps(self, name=None):
        self.n_alloc += 1
        t = self.es.enter_context(self.nc.psum_tensor(f"ps_{self.n_alloc}", [128, 512], F32))
        return TT(t, Buf(name or f"ps{self.n_alloc}"))

    def dram(self, name, shape, dtype):
        return self.nc.dram_tensor(name, list(shape), dtype, kind="Internal")

    def _reg(self, o, r, w):
        deps = []
        seen = set()

        def add(d):
            if d is o or id(d) in seen:
                return
            seen.add(id(d))
            if (not d.dma) and d.eng == o.eng and not o.dma:
                if o.eng == "pe" or not SAME_ENGINE_SYNC:
                    return
            deps.append(d)
            if not d.dma:
                d.sig = True

        rb = [x.b if isinstance(x, TT) else x for x in r]
        wb = [x.b if isinstance(x, TT) else x for x in w]
        for b in rb:
            for d in b.writers:
                add(d)
            if b.name.startswith("ps"):
                for d in b.readers:
                    if d.eng != o.eng:
                        add(d)
        for b in wb:
            for d in b.writers:
                add(d)
            for d in b.readers:
                add(d)
        o.deps = deps
        for b in wb:
            b.writers = [o]
            b.readers = []
        for b in rb:
            if b in wb:
                continue
            if not o.dma:
                b.readers = [x for x in b.readers if x.dma or x.eng != o.eng]
            b.readers.append(o)
        self.ops[o.eng].append(o)
        return o

    def op(self, eng, fn, r=(), w=()):
        return self._reg(Op(eng, fn), r, w)

    def dma(self, q, out, in_, r=(), w=(), is_out=False):
        o = Op(q, lambda e: e.dma_start(out=out, in_=in_), dma=True)
        n = self.dn[q]
        slot = n % NDSEM
        o.dsem = self.dsems[q][slot]
        o.dcount = 16 * (n // NDSEM + 1)
        o.prev_slot = self.dlast[q][slot]
        self.dlast[q][slot] = o
        self.dn[q] = n + 1
        self._reg(o, r, w)
        if o.prev_slot is not None:
            o.deps.append(o.prev_slot)
        if is_out:
            self.out_dmas.append(o)
        return o

    def emit(self):
        nc = self.nc
        fin = Op("sp", lambda e: e.nop() if hasattr(e, "nop") else None)
        fin.deps = list(self.out_dmas)
        for q in self.dn:
            for o in self.dlast[q]:
                if o is not None and o not in fin.deps:
                    fin.deps.append(o)
        self.ops["sp"].append(fin)
        for e, lst in self.ops.items():
            t = 0
            for o in lst:
                if o.sig and not o.dma:
                    t += 1
                    o.tick = t
        with nc.Block() as block:
            def body(ename):
                def f(eng):
                    waited = {}
                    for o in self.ops[ename]:
                        for d in o.deps:
                            if d.dma:
                                key, sem, val = id(d.dsem), d.dsem, d.dcount
                            else:
                                key, sem, val = d.eng, self.sem[d.eng], d.tick
                            if waited.get(key, 0) < val:
                                eng.wait_ge(sem, val)
                                waited[key] = val
                        if o is fin or o.fn is None:
                            continue
                        ins = o.fn(eng)
                        if o.dma:
                            ins.then_inc(o.dsem, 16)
                        elif o.sig:
                            ins.then_inc(self.sem[ename], 1)
                return f
            block.sync(body("sp"))
            block.tensor(body("pe"))
            block.scalar(body("act"))
            block.vector(body("dve"))
            block.gpsimd(body("pool"))


class TT:
    __slots__ = ("t", "b")

    def __init__(self, t, b):
        self.t = t
        self.b = b

    def __getitem__(self, k):
        return self.t[k]


class Ring:
    def __init__(self, items):
        self.items = items
        self.i = 0

    def next(self):
        x = self.items[self.i % len(self.items)]
        self.i += 1
        return x


def bc(ap, shape):
    return ap.to_broadcast(list(shape))


def group_w(w):
    K, N = w.shape
    gw = min(N, 512)
    G = N // gw
    return np.ascontiguousarray(w.reshape(K // 128, 128, G, gw).transpose(2, 1, 0, 3))


def host_consts(smax):
    c = {}
    c["ident"] = np.eye(128, dtype=np.float32)
    hm = np.zeros((128, 2), np.float32)
    hm[:64, 0] = 1.0
    hm[64:, 1] = 1.0
    c["half_mask"] = hm
    half = 32
    inv = (10000.0 ** (-np.arange(half, dtype=np.float32) / half)).astype(np.float32)
    pos = np.arange(smax, dtype=np.float32)
    ang = (pos[:, None] * inv[None, :]).astype(np.float32)
    cos = np.cos(ang).astype(np.float32).T
    sin = np.sin(ang).astype(np.float32).T
    c["rope_cos"] = np.ascontiguousarray(np.concatenate([cos, cos, cos, cos], 0))
    c["rope_sin"] = np.ascontiguousarray(np.concatenate([-sin, sin, -sin, sin], 0))
    i = np.arange(128, dtype=np.float32)
    c["iota_p"] = np.stack([127.0 - i, i + 1.0], 1).astype(np.float32)
    c["iota_f"] = np.ascontiguousarray(np.stack([np.tile(i + 1.0, (128, 1)), np.tile(127.0 - i, (128, 1))], 1)).astype(np.float32)
    jj, ii = np.meshgrid(i, i, indexing="ij")
    c["ret_pos"] = np.ascontiguousarray(np.stack([np.maximum(ii - jj, 0), np.maximum(jj - ii, 0)], 1)).astype(np.float32)
    c["ret_msk"] = np.ascontiguousarray(np.stack([(ii >= jj), (ii < jj)], 1)).astype(np.float32)
    pp, ff = jj, ii
    c["gdn_cum"] = np.ascontiguousarray(np.stack([(pp <= ff), (pp >= ff), np.ones_like(pp)], 1)).astype(np.float32)
    NEG = -30000.0
    m = np.zeros((128, 2, 2, 128), np.float32)
    m[:, 0, 0, :] = np.where(ff < pp, 0.0, NEG)
    m[:, 0, 1, :] = np.where(ff <= pp, 0.0, NEG)
    m[:, 1, 0, :] = np.where(ff > pp, 0.0, NEG)
    m[:, 1, 1, :] = np.where(ff >= pp, 0.0, NEG)
    c["gdn_msk"] = m
    return c


class Arena:
    def __init__(self, P, nbytes):
        self.P = P
        self.nbytes = nbytes
        self.t = P.es.enter_context(P.nc.sbuf_tensor("arena", [128, nbytes // 2], BF16))
        self.off = 0
        self.n = 0

    def reset(self):
        self.off = 0

    def alloc(self, free_shape, dtype, name="t"):
        esz = 4 if dtype == F32 else 2
        n = int(np.prod(free_shape))
        nb = (n * esz + 63) // 64 * 64
        assert self.off + nb <= self.nbytes, f"arena overflow {name} {self.off}+{nb}"
        ap = self.t[:, self.off // 2:(self.off + n * esz) // 2]
        if dtype == F32:
            ap = ap.bitcast(F32)
        if len(free_shape) == 2:
            ap = ap.rearrange("p (a b) -> p a b", a=free_shape[0])
        elif len(free_shape) == 3:
            ap = ap.rearrange("p (a b c) -> p a b c", a=free_shape[0], b=free_shape[1])
        self.off += nb
        self.n += 1
        return TT(ap, Buf(f"{name}{self.n}"))


def build_program(SS, SP, nlayers=2, dbg=False):
    nc = bass.Bass("TRN2", target_bir_lowering=False)
    P = Prog(nc)
    SMAX = max(SS, SP)

    def din(name, shape, dt=F32):
        return nc.dram_tensor(name, list(shape), dt, kind="ExternalInput")

    I = {}
    I["x_s"] = din("x_s", [SS, D])
    I["x_p"] = din("x_p", [SP, D])
    for nm, shp in (("norm1_w", [2, D]), ("norm2_w", [2, D]), ("final_norm_w", [1, D]),
                    ("w_ev_in", [12, 128, 16, 512]), ("w_ev_out", [4, 128, 16, 512]),
                    ("ret_logit", [1, 16]), ("ret_logit_pair", [2, 2, 4]),
                    ("ret_gn_w", [1, 1024]), ("gm_ln_w", [1, 1024]), ("gm_ln_b", [1, 1024]),
                    ("gm_wsT", [128, 8, 128]), ("gm_bs", [128, 8]),
                    ("w_od_qkv", [16, 128, 16, 512]), ("w_od_z", [8, 128, 16, 512]), ("w_od_g", [1, 128, 16, 128]),
                    ("od_conv", [128, 64, 5]), ("od_a_log", [1, 64]), ("od_dt_bias", [1, 64]), ("od_norm_w", [1, 128]),
                    ("w_od_out", [4, 128, 32, 512]),
                    ("w_gate0", [11, 128, 16, 512]), ("w_up0", [11, 128, 16, 512]), ("w_down0", [4, 128, 44, 512]),
                    ("w_gate1", [11, 128, 16, 512]), ("w_up1", [11, 128, 16, 512]), ("w_down1", [4, 128, 44, 512]),
                    ("ident", [128, 128]), ("half_mask", [128, 2]), ("rope_cos", [128, SMAX]), ("rope_sin", [128, SMAX]),
                    ("iota_p", [128, 2]), ("iota_f", [128, 2, 128]), ("ret_pos", [128, 2, 128]), ("ret_msk", [128, 2, 128]),
                    ("gdn_cum", [128, 3, 128]), ("gdn_msk", [128, 2, 2, 128])):
        I[nm] = din(nm, shp)
    O = {"s": nc.dram_tensor("y_s", [SS, D], F32, kind="ExternalOutput"),
         "p": nc.dram_tensor("y_p", [SP, D], F32, kind="ExternalOutput")}

    WB = {}
    WBUF = {}
    big = ["w_ev_in", "w_ev_out", "w_gate0", "w_up0", "w_down0"]
    if nlayers > 1:
        big += ["w_od_qkv", "w_od_z", "w_od_g", "w_od_out", "w_gate1", "w_up1", "w_down1"]
    if _os.environ.get('K_NOCAST'):
        big = []
    for nm in big:
        shp = list(I[nm].shape)
        WB[nm] = P.dram(nm + "_bf", shp, BF16)
        for g in range(shp[0]):
            b = Buf(f"{nm}_{g}")
            WBUF[(nm, g)] = b
            kc = shp[2]
            for k0 in range(0, kc, 16):
                k1 = min(kc, k0 + 16)
                P.dma("pool", WB[nm][g, :, k0:k1, :], I[nm][g, :, k0:k1, :], w=[b])

    AR = Arena(P, 164 * 1024)
    CA = Arena.__new__(Arena)
    CA.P = P
    CA.nbytes = 24 * 1024
    CA.t = P.es.enter_context(nc.sbuf_tensor("carena", [128, CA.nbytes // 2], BF16))
    CA.off = 0
    CA.n = 0
    PSB = [P.ps() for _ in range(8)]
    PSR = Ring(PSB)

    def cload(name, free_shape, src_ap):
        t = CA.alloc(free_shape, F32, name)
        P.dma("sp", t.t, src_ap, w=[t])
        return t

    identf = cload("identf", [128], I["ident"][:, :])
    identb = CA.alloc([128], BF16, "identb")
    P.op("dve", lambda e: e.tensor_copy(out=identb.t, in_=identf.t), r=[identf], w=[identb])
    epsc = CA.alloc([2], F32, "eps")
    P.op("dve", lambda e: e.memset(epsc.t[:, 0:1], NORM_EPS), w=[epsc])
    P.op("dve", lambda e: e.memset(epsc.t[:, 1:2], 1.0), r=[epsc], w=[epsc])

    def barrier():
        lasts = []
        for e in P.ops:
            for o in reversed(P.ops[e]):
                if not o.dma and o.fn is not None:
                    lasts.append(o)
                    break
        dm = [o for q in P.dn for o in P.dlast[q] if o is not None]
        for e in ("pe", "act", "dve", "pool", "sp"):
            o = Op(e, None)
            for d in lasts:
                if d.eng != e:
                    o.deps.append(d)
                    d.sig = True
            o.deps += dm
            P.ops[e].append(o)

    evac_flip = [0]

    def evac_eng():
        evac_flip[0] ^= 1
        return "act" if evac_flip[0] else "dve"

    def copy_op(eng, out, in_, r, w):
        if eng == "act":
            return P.op("act", lambda e: e.copy(out=out, in_=in_), r=r, w=w)
        return P.op(eng, lambda e: e.tensor_copy(out=out, in_=in_), r=r, w=w)

    def transposes(src_fn, srcbufs, n, dst_fn, dstbufs):
        for k0 in range(0, n, 8):
            nb = min(8, n - k0)
            bank = PSR.next()
            pb = bank.t.bitcast(BF16)
            for k in range(nb):
                P.op("pe", lambda e, k=k, pb=pb, k0=k0: e.transpose(out=pb[:, k * 128:(k + 1) * 128], in_=src_fn(k0 + k), identity=identb.t),
                     r=list(srcbufs) + [identb], w=[bank])
            copy_op(evac_eng(), dst_fn(k0, nb), pb[:, 0:nb * 128].rearrange("p (a b) -> p a b", a=nb), [bank], dstbufs)

    def load_bc(dst, src_row_ap):
        P.dma("sp", dst.t, src_row_ap.partition_broadcast(128), w=[dst])

    def norm_to_fm(src_dram, t0, T, wbc, hnT, xin_ring, xn_ring, junk, stat_ring):
        for j in range(T // 128):
            xin = xin_ring.next()
            xn = xn_ring.next()
            st = stat_ring.next()
            P.dma("sp", xin.t, src_dram[t0 + j * 128:t0 + (j + 1) * 128, :], w=[xin])
            P.op("act", lambda e, xin=xin, st=st: e.activation(out=junk.t, in_=xin.t, func=AF.Square, accum_out=st.t[:, 0:1]),
                 r=[xin], w=[junk, st])
            P.op("act", lambda e, st=st: e.activation(out=st.t[:, 1:2], in_=st.t[:, 0:1], func=AF.Sqrt, bias=epsc.t[:, 0:1], scale=1.0 / D), r=[st, epsc], w=[st])
            P.op("dve", lambda e, st=st: e.reciprocal(out=st.t[:, 2:3], in_=st.t[:, 1:2]), r=[st], w=[st])
            P.op("dve", lambda e, xin=xin, xn=xn, st=st: e.scalar_tensor_tensor(out=xn.t, in0=xin.t, scalar=st.t[:, 2:3], in1=wbc.t,
                                                                              op0=ALU.mult, op1=ALU.mult), r=[xin, st, wbc], w=[xn])
            transposes(lambda k, xn=xn: xn.t[:, k * 128:(k + 1) * 128], [xn], 16,
                       lambda k0, nb, j=j: hnT.t[:, k0:k0 + nb, j * 128:(j + 1) * 128], [hnT])

    def load_w(wring, nm, g, k0=0, k1=None):
        wt = wring.next()
        shp = WB[nm].shape
        k1 = shp[2] if k1 is None else k1
        P.dma("sp", wt.t[:, 0:k1 - k0, 0:shp[3]], WB[nm][g, :, k0:k1, :], r=[WBUF[(nm, g)]], w=[wt])
        return wt

    def mm_tm(bank, actT, abufs, j, wt, kcn, ncols, first=True, last=True, kc_off=0):
        for kc in range(kcn):
            P.op("pe", lambda e, kc=kc: e.matmul(bank.t[:, 0:ncols], lhsT=actT.t[:, kc_off + kc, j * 128:(j + 1) * 128],
                                                 rhs=wt.t[:, kc, 0:ncols], start=(first and kc == 0), stop=(last and kc == kcn - 1)),
                 r=list(abufs) + [wt], w=[bank])

    def mm_fm(bank, actT, abufs, T, wt, c, kcn):
        for kc in range(kcn):
            P.op("pe", lambda e, kc=kc: e.matmul(bank.t[:, 0:T], lhsT=wt.t[:, kc, c * 128:(c + 1) * 128], rhs=actT.t[:, kc, 0:T],
                                                 start=(kc == 0), stop=(kc == kcn - 1)),
                 r=list(abufs) + [wt], w=[bank])

    def gelu_tanh(out_ap, ps_ap, T_, tmp1, tmp2, r, w):
        P.op("act", lambda e: e.activation(out=tmp1.t[:, 0:T_], in_=ps_ap, func=AF.Square), r=r, w=[tmp1])
        P.op("dve", lambda e: e.tensor_scalar(out=tmp1.t[:, 0:T_], in0=tmp1.t[:, 0:T_], scalar1=0.044715, scalar2=1.0, op0=ALU.mult, op1=ALU.add),
             r=[tmp1], w=[tmp1])
        P.op("dve", lambda e: e.tensor_tensor(out=tmp1.t[:, 0:T_], in0=tmp1.t[:, 0:T_], in1=ps_ap, op=ALU.mult), r=[tmp1] + r, w=[tmp1])
        P.op("act", lambda e: e.activation(out=tmp2.t[:, 0:T_], in_=tmp1.t[:, 0:T_], func=AF.Sigmoid, scale=1.5957691216057308), r=[tmp1], w=[tmp2])
        P.op("dve", lambda e: e.tensor_tensor(out=out_ap, in0=tmp2.t[:, 0:T_], in1=ps_ap, op=ALU.mult), r=[tmp2] + r, w=w)

    def ffn_phase(src, dst, S, T, layer, final=False):
        barrier()
        AR.reset()
        wring = Ring([AR.alloc([16, 512], BF16, "w") for _ in range(3)])
        hnT = AR.alloc([16, T], BF16, "hnT")
        aT = AR.alloc([44, T], BF16, "aT")
        xin_ring = Ring([AR.alloc([D], F32, "xin") for _ in range(2)])
        xn_ring = Ring([AR.alloc([D], BF16, "xn") for _ in range(2)])
        junk = AR.alloc([D], BF16, "junk")
        stat_ring = Ring([AR.alloc([4], F32, "st") for _ in range(2)])
        wbc = AR.alloc([D], F32, "wbc")
        load_bc(wbc, I["norm2_w"][layer:layer + 1, :])
        sg_ring = Ring([AR.alloc([512], F32, "sg") for _ in range(2)])
        res_ring = Ring([AR.alloc([512], F32, "res") for _ in range(3)])
        out_ring = Ring([AR.alloc([512], F32, "out") for _ in range(3)])
        NS = T // 128
        for t0 in range(0, S, T):
            norm_to_fm(src, t0, T, wbc, hnT, xin_ring, xn_ring, junk, stat_ring)
            for g in range(11):
                wg = load_w(wring, f"w_gate{layer}", g)
                wu = load_w(wring, f"w_up{layer}", g)
                for c in range(4):
                    bg = PSR.next()
                    bu = PSR.next()
                    mm_fm(bg, hnT, [hnT], T, wg, c, 16)
                    mm_fm(bu, hnT, [hnT], T, wu, c, 16)
                    sg = sg_ring.next()
                    P.op("act", lambda e, bg=bg, sg=sg: e.activation(out=sg.t[:, 0:T], in_=bg.t[:, 0:T], func=AF.Silu), r=[bg], w=[sg])
                    P.op("dve", lambda e, bu=bu, sg=sg, g=g, c=c: e.tensor_tensor(out=aT.t[:, g * 4 + c, :], in0=sg.t[:, 0:T], in1=bu.t[:, 0:T], op=ALU.mult),
                         r=[bu, sg], w=[aT])
            for og in range(4):
                banks = [PSR.next() for _ in range(NS)]
                pieces = [(0, 16), (16, 32), (32, 44)]
                for pi, (k0, k1) in enumerate(pieces):
                    wd = load_w(wring, f"w_down{layer}", og, k0, k1)
                    for j in range(NS):
                        mm_tm(banks[j], aT, [aT], j, wd, k1 - k0, 512, first=(pi == 0), last=(pi == len(pieces) - 1), kc_off=k0)
                for j in range(NS):
                    res = res_ring.next()
                    ot = out_ring.next()
                    rows = slice(t0 + j * 128, t0 + (j + 1) * 128)
                    P.dma("sp", res.t, src[rows, og * 512:(og + 1) * 512], w=[res])
                    P.op("dve", lambda e, b=banks[j], res=res, ot=ot: e.tensor_tensor(out=ot.t, in0=b.t[:, :], in1=res.t, op=ALU.add),
                         r=[banks[j], res], w=[ot])
                    P.dma("pool", dst[rows, og * 512:(og + 1) * 512], ot.t, r=[ot])

    def final_norm_phase(src, dst, S):
        barrier()
        AR.reset()
        wbc = AR.alloc([D], F32, "wbc")
        load_bc(wbc, I["final_norm_w"][0:1, :])
        xin_ring = Ring([AR.alloc([D], F32, "xin") for _ in range(3)])
        out_ring = Ring([AR.alloc([D], F32, "xo") for _ in range(3)])
        junk = AR.alloc([D], BF16, "junk")
        stat_ring = Ring([AR.alloc([4], F32, "st") for _ in range(3)])
        for t0 in range(0, S, 128):
            xin = xin_ring.next()
            xo = out_ring.next()
            st = stat_ring.next()
            P.dma("sp", xin.t, src[t0:t0 + 128, :], w=[xin])
            P.op("act", lambda e, xin=xin, st=st: e.activation(out=junk.t, in_=xin.t, func=AF.Square, accum_out=st.t[:, 0:1]), r=[xin], w=[junk, st])
            P.op("act", lambda e, st=st: e.activation(out=st.t[:, 1:2], in_=st.t[:, 0:1], func=AF.Sqrt, bias=epsc.t[:, 0:1], scale=1.0 / D), r=[st, epsc], w=[st])
            P.op("dve", lambda e, st=st: e.reciprocal(out=st.t[:, 2:3], in_=st.t[:, 1:2]), r=[st], w=[st])
            P.op("dve", lambda e, xin=xin, xo=xo, st=st: e.scalar_tensor_tensor(out=xo.t, in0=xin.t, scalar=st.t[:, 2:3], in1=wbc.t, op0=ALU.mult, op1=ALU.mult),
                 r=[xin, st, wbc], w=[xo])
            P.dma("pool", dst[t0:t0 + 128, :], xo.t, r=[xo], is_out=True)

    PH = {"P": P, "I": I, "O": O, "AR": AR, "CA": CA, "PSR": PSR, "WB": WB, "WBUF": WBUF, "barrier": barrier,
          "transposes": transposes, "load_bc": load_bc, "norm_to_fm": norm_to_fm, "load_w": load_w, "mm_tm": mm_tm, "mm_fm": mm_fm,
          "gelu_tanh": gelu_tanh, "epsc": epsc, "copy_op": copy_op, "evac_eng": evac_eng, "identb": identb, "identf": identf, "nc": nc}

    for tag, S in (("s", SS), ("p", SP)):
        T = min(512, S)
        x = I["x_" + tag]
        mid1 = P.dram(f"mid1_{tag}", [S, D], F32)
        h1 = P.dram(f"h1_{tag}", [S, D], F32)
        if DBG_LEVEL < 4:
            if DBG_LEVEL >= 1:
                layer0_mixer(PH, tag, x, mid1, S, T)
            final_norm_phase(mid1 if DBG_LEVEL == 3 else x, O[tag], S)
            continue
        layer0_mixer(PH, tag, x, mid1, S, T)
        ffn_phase(mid1, h1, S, T, 0)
        if nlayers > 1:
            mid2 = P.dram(f"mid2_{tag}", [S, D], F32)
            h2 = P.dram(f"h2_{tag}", [S, D], F32)
            layer1_mixer(PH, tag, h1, mid2, S, T)
            ffn_phase(mid2, h2, S, T, 1)
            final_norm_phase(h2, O[tag], S)
        else:
            final_norm_phase(h1, O[tag], S)
    P.emit()
    return nc


def layer0_mixer(PH, tag, x, mid1, S, T):
    P, I, AR, CA, PSR, nc = PH["P"], PH["I"], PH["AR"], PH["CA"], PH["PSR"], PH["nc"]
    barrier, transposes, load_bc, norm_to_fm, load_w, mm_tm, mm_fm = (PH[k] for k in ("barrier", "transposes", "load_bc", "norm_to_fm", "load_w", "mm_tm", "mm_fm"))
    gelu_tanh, copy_op, evac_eng, identb = PH["gelu_tanh"], PH["copy_op"], PH["evac_eng"], PH["identb"]
    epsc = PH["epsc"]
    N = S // 128
    NS = T // 128
    qT_d = P.dram(f"qT_{tag}", [4, 128, S], BF16)
    kT_d = P.dram(f"kT_{tag}", [4, 128, S], BF16)
    v_d = P.dram(f"v0_{tag}", [S, 1024], BF16)
    sg_d = P.dram(f"sg_{tag}", [S, 1024], F32)
    gu_d = P.dram(f"gu_{tag}", [S, 1024], F32)
    gv_d = P.dram(f"gv_{tag}", [S, 1024], F32)
    sF_d = P.dram(f"sF_{tag}", [N, 128, 512], BF16)
    sB_d = P.dram(f"sB_{tag}", [N, 128, 512], BF16)
    DB = {k: [Buf(f"{k}{n}") for n in range(N)] for k in ("qT", "kT", "v", "sg", "gu", "gv", "sF", "sB")}

    barrier()
    AR.reset()
    wring = Ring([AR.alloc([16, 512], BF16, "w") for _ in range(4)])
    hnT = AR.alloc([16, T], BF16, "hnT")
    xin_ring = Ring([AR.alloc([D], F32, "xin") for _ in range(2)])
    xn_ring = Ring([AR.alloc([D], BF16, "xn") for _ in range(2)])
    junk = AR.alloc([D], BF16, "junk")
    stat_ring = Ring([AR.alloc([4], F32, "st") for _ in range(2)])
    wbc = AR.alloc([D], F32, "wbc")
    load_bc(wbc, I["norm1_w"][0:1, :])
    gnw = AR.alloc([1024], F32, "gnw")
    load_bc(gnw, I["ret_gn_w"][0:1, :])
    cos_t = AR.alloc([T], F32, "cos")
    sin_t = AR.alloc([T], F32, "sin")
    ta_ring = Ring([AR.alloc([512], F32, "ta") for _ in range(2)])
    tb_ring = Ring([AR.alloc([512], F32, "tb") for _ in range(2)])
    qk_ring = Ring([AR.alloc([4, T], BF16, "qk") for _ in range(2)])
    o16_ring = Ring([AR.alloc([512], BF16, "o16") for _ in range(2)])
    o32_ring = Ring([AR.alloc([512], F32, "o32") for _ in range(3)])
    for t0 in range(0, S, T):
        norm_to_fm(x, t0, T, wbc, hnT, xin_ring, xn_ring, junk, stat_ring)
        P.dma("sp", cos_t.t, I["rope_cos"][:, t0:t0 + T], w=[cos_t])
        P.dma("sp", sin_t.t, I["rope_sin"][:, t0:t0 + T], w=[sin_t])
        for which, dst_d, key in ((0, qT_d, "qT"), (2, kT_d, "kT")):
            w0 = load_w(wring, "w_ev_in", which)
            w1 = load_w(wring, "w_ev_in", which + 1)
            qk = qk_ring.next()
            for c in range(4):
                b0 = PSR.next()
                b1 = PSR.next()
                mm_fm(b0, hnT, [hnT], T, w0, c, 16)
                mm_fm(b1, hnT, [hnT], T, w1, c, 16)
                ta = ta_ring.next()
                tb = tb_ring.next()
                P.op("dve", lambda e, b0=b0, ta=ta: e.tensor_tensor(out=ta.t[:, 0:T], in0=b0.t[:, 0:T], in1=cos_t.t, op=ALU.mult), r=[b0, cos_t], w=[ta])
                P.op("dve", lambda e, b1=b1, tb=tb: e.tensor_tensor(out=tb.t[:, 0:T], in0=b1.t[:, 0:T], in1=sin_t.t, op=ALU.mult), r=[b1, sin_t], w=[tb])
                P.op("pool", lambda e, ta=ta, tb=tb, qk=qk, c=c: e.tensor_tensor(out=qk.t[:, c, :], in0=ta.t[:, 0:T], in1=tb.t[:, 0:T], op=ALU.add), r=[ta, tb], w=[qk])
            P.dma("pool", dst_d[:, :, t0:t0 + T].rearrange("c p t -> p c t"), qk.t, r=[qk], w=[DB[key][(t0 + j * 128) // 128] for j in range(NS)])
        for g in range(4, 12):
            wt = load_w(wring, "w_ev_in", g)
            col = ((g - 4) % 2) * 512
            kind = (g - 4) // 2
            for j in range(NS):
                bank = PSR.next()
                mm_tm(bank, hnT, [hnT], j, wt, 16, 512)
                rows = slice(t0 + j * 128, t0 + (j + 1) * 128)
                n = (t0 + j * 128) // 128
                if kind == 0:
                    o = o16_ring.next()
                    copy_op(evac_eng(), o.t, bank.t[:, :], [bank], [o])
                    P.dma("pool", v_d[rows, col:col + 512], o.t, r=[o], w=[DB["v"][n]])
                elif kind == 1:
                    o = o32_ring.next()
                    ta = ta_ring.next()
                    P.op("act", lambda e, bank=bank, ta=ta: e.activation(out=ta.t, in_=bank.t[:, :], func=AF.Silu), r=[bank], w=[ta])
                    P.op("dve", lambda e, ta=ta, o=o, col=col: e.tensor_tensor(out=o.t, in0=ta.t, in1=gnw.t[:, col:col + 512], op=ALU.mult), r=[ta, gnw], w=[o])
                    P.dma("pool", sg_d[rows, col:col + 512], o.t, r=[o], w=[DB["sg"][n]])
                else:
                    o = o32_ring.next()
                    gelu_tanh(o.t, bank.t[:, :], 512, ta_ring.next(), tb_ring.next(), [bank], [o])
                    P.dma("pool", (gu_d if kind == 2 else gv_d)[rows, col:col + 512], o.t, r=[o], w=[DB["gu" if kind == 2 else "gv"][n]])

    if DBG_LEVEL < 2:
        return
    barrier()
    AR.reset()
    lg = AR.alloc([16], F32, "lg")
    load_bc(lg, I["ret_logit"][0:1, :])
    lgp = AR.alloc([2, 4], F32, "lgp")
    for hh in range(2):
        for dr in range(2):
            P.dma("sp", lgp.t[hh * 64:(hh + 1) * 64, dr, :], I["ret_logit_pair"][dr, hh:hh + 1, :].partition_broadcast(64), r=[lgp], w=[lgp])
    iop = AR.alloc([2], F32, "iop")
    P.dma("sp", iop.t, I["iota_p"][:, :], w=[iop])
    iof = AR.alloc([2, 128], F32, "iof")
    P.dma("sp", iof.t, I["iota_f"][:, :, :], w=[iof])
    rpos = AR.alloc([2, 128], F32, "rpos")
    P.dma("sp", rpos.t, I["ret_pos"][:, :, :], w=[rpos])
    rmsk = AR.alloc([2, 128], F32, "rmsk")
    P.dma("sp", rmsk.t, I["ret_msk"][:, :, :], w=[rmsk])
    for t_ in (lg, lgp):
        P.op("act", lambda e, t_=t_: e.activation(out=t_.t, in_=t_.t, func=AF.Sigmoid), r=[t_], w=[t_])
        P.op("act", lambda e, t_=t_: e.activation(out=t_.t, in_=t_.t, func=AF.Ln), r=[t_], w=[t_])
    wk = CA.alloc([2, 8], F32, "wk") if not hasattr(CA, "l0") else CA.l0["wk"]
    dmat = CA.alloc([8, 128], F32, "dmat") if not hasattr(CA, "l0") else CA.l0["dmat"]
    wq = CA.alloc([2, 4, 128], F32, "wq") if not hasattr(CA, "l0") else CA.l0["wq"]
    wqm = CA.alloc([4, 4, 128], F32, "wqm") if not hasattr(CA, "l0") else CA.l0["wqm"]
    hmask = CA.alloc([2], F32, "hmask") if not hasattr(CA, "l0") else CA.l0["hmask"]
    dec = CA.alloc([2, 4], F32, "dec") if not hasattr(CA, "l0") else CA.l0["dec"]
    wsT = CA.alloc([8, 128], BF16, "wsT") if not hasattr(CA, "l0") else CA.l0["wsT"]
    first = not hasattr(CA, "l0")
    CA.l0 = {"wk": wk, "dmat": dmat, "wq": wq, "dec": dec, "wsT": wsT, "wqm": wqm, "hmask": hmask}
    if first:
        for dr in range(2):
            P.op("dve", lambda e, dr=dr: e.tensor_scalar(out=wk.t[:, dr, :], in0=lg.t[:, dr * 8:(dr + 1) * 8], scalar1=iop.t[:, dr:dr + 1], scalar2=None, op0=ALU.mult), r=[lg, iop], w=[wk])
        P.op("act", lambda e: e.activation(out=wk.t, in_=wk.t, func=AF.Exp), r=[wk], w=[wk])
        tmpd = AR.alloc([2, 128], F32, "tmpd")
        for h in range(8):
            for dr in range(2):
                P.op("act", lambda e, h=h, dr=dr: e.activation(out=tmpd.t[:, dr, :], in_=rpos.t[:, dr, :], func=AF.Exp, scale=lg.t[:, dr * 8 + h:dr * 8 + h + 1]), r=[rpos, lg, tmpd], w=[tmpd])
            P.op("dve", lambda e: e.tensor_tensor(out=tmpd.t, in0=tmpd.t, in1=rmsk.t, op=ALU.mult), r=[tmpd, rmsk], w=[tmpd])
            P.op("dve", lambda e, h=h: e.tensor_tensor(out=dmat.t[:, h, :], in0=tmpd.t[:, 0, :], in1=tmpd.t[:, 1, :], op=ALU.add), r=[tmpd, dmat], w=[dmat])
        P.op("dve", lambda e: e.tensor_scalar(out=dmat.t, in0=dmat.t, scalar1=0.125, scalar2=None, op0=ALU.mult), r=[dmat], w=[dmat])
        for dr in range(2):
            for p_ in range(4):
                P.op("act", lambda e, dr=dr, p_=p_: e.activation(out=wq.t[:, dr, p_, :], in_=iof.t[:, dr, :], func=AF.Exp, scale=lgp.t[:, dr, p_:p_ + 1]), r=[iof, lgp, wq], w=[wq])
        P.op("dve", lambda e: e.tensor_scalar(out=wq.t, in0=wq.t, scalar1=0.125, scalar2=None, op0=ALU.mult), r=[wq], w=[wq])
        P.op("act", lambda e: e.activation(out=dec.t, in_=lgp.t, func=AF.Exp, scale=128.0), r=[lgp], w=[dec])
        P.dma("sp", hmask.t, I["half_mask"][:, :], w=[hmask])
        for dr in range(2):
            for hh in range(2):
                P.op("dve", lambda e, dr=dr, hh=hh: e.tensor_scalar(out=wqm.t[:, dr * 2 + hh, :, :], in0=wq.t[:, dr, :, :], scalar1=hmask.t[:, hh:hh + 1], scalar2=None, op0=ALU.mult), r=[wq, hmask, wqm], w=[wqm])
        wsf = AR.alloc([8, 128], F32, "wsf")
        P.dma("sp", wsf.t, I["gm_wsT"][:, :, :], w=[wsf])
        P.op("dve", lambda e: e.tensor_copy(out=wsT.t, in_=wsf.t), r=[wsf], w=[wsT])

    sF = AR.alloc([4, 128], F32, "sF")
    kT_ring = Ring([AR.alloc([4, 128], BF16, "kTc") for _ in range(2)])
    v_ring = Ring([AR.alloc([1024], BF16, "vc") for _ in range(2)])
    kf_ring = Ring([AR.alloc([512], BF16, "kf") for _ in range(2)])
    sb16_ring = Ring([AR.alloc([4, 128], BF16, "s16") for _ in range(3)])
    for dr in range(2):
        P.op("dve", lambda e: e.memset(sF.t, 0.0), r=[sF], w=[sF])
        order = range(N) if dr == 0 else range(N - 1, -1, -1)
        s_d = sF_d if dr == 0 else sB_d
        skey = "sF" if dr == 0 else "sB"
        for n in order:
            kTc = kT_ring.next()
            vc = v_ring.next()
            P.dma("sp", kTc.t, kT_d[:, :, n * 128:(n + 1) * 128].rearrange("c p t -> p c t"), r=[DB["kT"][n]], w=[kTc])
            P.dma("sp", vc.t, v_d[n * 128:(n + 1) * 128, :], r=[DB["v"][n]], w=[vc])
            s16 = sb16_ring.next()
            copy_op("act", s16.t, sF.t, [sF], [s16])
            P.dma("pool", s_d[n, :, :], s16.t.rearrange("p a b -> p (a b)"), r=[s16], w=[DB[skey][n]])
            bank = PSR.next()
            pb = bank.t.bitcast(BF16)
            for c in range(4):
                P.op("pe", lambda e, c=c, pb=pb, kTc=kTc: e.transpose(out=pb[:, c * 128:(c + 1) * 128], in_=kTc.t[:, c, :], identity=identb.t), r=[kTc, identb], w=[bank])
            kf = kf_ring.next()
            P.op("dve", lambda e, pb=pb, kf=kf, dr=dr: e.tensor_tensor(out=kf.t.rearrange("p (h d) -> p h d", h=8), in0=pb[:, 0:512].rearrange("p (h d) -> p h d", h=8),
                                                                 in1=bc(wk.t[:, dr, :].unsqueeze(2), [128, 8, 64]), op=ALU.mult), r=[bank, wk], w=[kf])
            ub = [PSR.next(), PSR.next()]
            for p_ in range(4):
                P.op("pe", lambda e, p_=p_, kf=kf, vc=vc, ub=ub: e.matmul(ub[p_ // 2].t[:, (p_ % 2) * 256:(p_ % 2 + 1) * 256], lhsT=kf.t[:, p_ * 128:(p_ + 1) * 128],
                                                                      rhs=vc.t[:, p_ * 256:(p_ + 1) * 256], start=True, stop=True), r=[kf, vc], w=[ub[p_ // 2]])
            for p_ in range(4):
                for hh in range(2):
                    rows = slice(hh * 64, (hh + 1) * 64)
                    c0 = (p_ % 2) * 256 + hh * 128
                    P.op("dve", lambda e, p_=p_, rows=rows, c0=c0, ub=ub, dr=dr: e.scalar_tensor_tensor(out=sF.t[rows, p_, :], in0=sF.t[rows, p_, :], scalar=dec.t[rows, dr, p_:p_ + 1],
                                                                                                 in1=ub[p_ // 2].t[rows, c0:c0 + 128], op0=ALU.mult, op1=ALU.add),
                         r=[sF, dec, ub[p_ // 2]], w=[sF])

    if DBG_LEVEL < 3:
        return
    barrier()
    AR.reset()
    wring = Ring([AR.alloc([16, 512], BF16, "w") for _ in range(3)])
    gm_w = AR.alloc([1024], F32, "gmw")
    load_bc(gm_w, I["gm_ln_w"][0:1, :])
    gm_b = AR.alloc([1024], F32, "gmb")
    load_bc(gm_b, I["gm_ln_b"][0:1, :])
    gbs = AR.alloc([8], F32, "gbs")
    P.dma("sp", gbs.t, I["gm_bs"][:, :], w=[gbs])
    ymixT = AR.alloc([16, T], BF16, "ymixT")
    qT_ring = Ring([AR.alloc([4, 128], BF16, "qTc") for _ in range(2)])
    kT_ring = Ring([AR.alloc([4, 128], BF16, "kTc") for _ in range(2)])
    v_ring = Ring([AR.alloc([1024], BF16, "vc") for _ in range(2)])
    sF_ring = Ring([AR.alloc([4, 128], BF16, "sFc") for _ in range(2)])
    sB_ring = Ring([AR.alloc([4, 128], BF16, "sBc") for _ in range(2)])
    sg_ring = Ring([AR.alloc([1024], F32, "sgc") for _ in range(2)])
    gu_ring = Ring([AR.alloc([1024], F32, "guc") for _ in range(2)])
    gv_ring = Ring([AR.alloc([1024], F32, "gvc") for _ in range(2)])
    SM = AR.alloc([8, 128], BF16, "SM")
    qfm = AR.alloc([4, 4, 128], BF16, "qfm")
    kTm = AR.alloc([2, 4, 128], BF16, "kTm")
    sq = AR.alloc([1024], F32, "sq")
    yn = AR.alloc([1024], F32, "yn")
    st8 = AR.alloc([6, 8], F32, "st8")
    ymix = AR.alloc([2048], BF16, "ymix")
    gvn = AR.alloc([1024], BF16, "gvn")
    gtmp = AR.alloc([1024], F32, "gtmp")
    lnst = AR.alloc([8], F32, "lnst")
    res_ring = Ring([AR.alloc([512], F32, "res") for _ in range(3)])
    out_ring = Ring([AR.alloc([512], F32, "out") for _ in range(3)])
    for t0 in range(0, S, T):
        for j in range(NS):
            n = t0 // 128 + j
            cs = slice(n * 128, (n + 1) * 128)
            qTc, kTc, vc, sFc, sBc, sgc, guc, gvc = (r_.next() for r_ in (qT_ring, kT_ring, v_ring, sF_ring, sB_ring, sg_ring, gu_ring, gv_ring))
            P.dma("sp", qTc.t, qT_d[:, :, cs].rearrange("c p t -> p c t"), r=[DB["qT"][n]], w=[qTc])
            P.dma("sp", kTc.t, kT_d[:, :, cs].rearrange("c p t -> p c t"), r=[DB["kT"][n]], w=[kTc])
            P.dma("sp", vc.t, v_d[cs, :], r=[DB["v"][n]], w=[vc])
            P.dma("sp", sFc.t.rearrange("p a b -> p (a b)"), sF_d[n, :, :], r=[DB["sF"][n]], w=[sFc])
            P.dma("sp", sBc.t.rearrange("p a b -> p (a b)"), sB_d[n, :, :], r=[DB["sB"][n]], w=[sBc])
            P.dma("sp", sgc.t, sg_d[cs, :], r=[DB["sg"][n]], w=[sgc])
            P.dma("sp", guc.t, gu_d[cs, :], r=[DB["gu"][n]], w=[guc])
            P.dma("sp", gvc.t, gv_d[cs, :], r=[DB["gv"][n]], w=[gvc])
            if K_SUB <= 1:
                continue
            sb_ = [PSR.next(), PSR.next()]
            for hh in range(2):
                P.op("dve", lambda e, hh=hh, kTc=kTc: e.tensor_scalar(out=kTm.t[:, hh, :, :], in0=kTc.t, scalar1=hmask.t[:, hh:hh + 1], scalar2=None, op0=ALU.mult), r=[kTc, hmask, kTm], w=[kTm])
            for h in range(8):
                P.op("pe", lambda e, h=h, sb_=sb_, qTc=qTc: e.matmul(sb_[h // 4].t[:, (h % 4) * 128:(h % 4 + 1) * 128], lhsT=kTm.t[:, h % 2, h // 2, :], rhs=qTc.t[:, h // 2, :], start=True, stop=True),
                     r=[kTm, qTc], w=[sb_[h // 4]])
            for b_ in range(2):
                P.op("dve", lambda e, b_=b_, sb_=sb_: e.tensor_tensor(out=SM.t[:, b_ * 4:(b_ + 1) * 4, :], in0=sb_[b_].t[:, :].rearrange("p (h i) -> p h i", h=4), in1=dmat.t[:, b_ * 4:(b_ + 1) * 4, :], op=ALU.mult),
                     r=[sb_[b_], dmat], w=[SM])
            for m_ in range(4):
                P.op("dve", lambda e, m_=m_, qTc=qTc: e.tensor_tensor(out=qfm.t[:, m_, :, :], in0=qTc.t, in1=wqm.t[:, m_, :, :], op=ALU.mult), r=[qTc, wqm, qfm], w=[qfm])
            if K_SUB <= 2:
                continue
            yb_ = [PSR.next(), PSR.next()]
            for h in range(8):
                rows = slice((h % 2) * 64, (h % 2 + 1) * 64)
                oc = slice((h % 4) * 128, (h % 4 + 1) * 128)
                P.op("pe", lambda e, h=h, oc=oc, yb_=yb_, vc=vc: e.matmul(yb_[h // 4].t[:, oc], lhsT=SM.t[:, h, :], rhs=vc.t[:, h * 128:(h + 1) * 128], start=True, stop=False), r=[SM, vc], w=[yb_[h // 4]])
                P.op("pe", lambda e, h=h, oc=oc, yb_=yb_, sFc=sFc: e.matmul(yb_[h // 4].t[:, oc], lhsT=qfm.t[:, h % 2, h // 2, :], rhs=sFc.t[:, h // 2, :], start=False, stop=False), r=[qfm, sFc], w=[yb_[h // 4]])
                P.op("pe", lambda e, h=h, oc=oc, yb_=yb_, sBc=sBc: e.matmul(yb_[h // 4].t[:, oc], lhsT=qfm.t[:, 2 + h % 2, h // 2, :], rhs=sBc.t[:, h // 2, :], start=False, stop=True), r=[qfm, sBc], w=[yb_[h // 4]])
            if K_SUB <= 2.5:
                continue
            for b_ in range(2):
                y3 = yb_[b_].t[:, :].rearrange("p (h e) -> p h e", h=4)
                P.op("dve", lambda e, b_=b_, y3=y3: e.tensor_reduce(out=st8.t[:, 0, b_ * 4:(b_ + 1) * 4], in_=y3, axis=AX.X, op=ALU.add), r=[yb_[b_], st8], w=[st8])
                P.op("act", lambda e, b_=b_, yb_=yb_: e.activation(out=sq.t[:, b_ * 512:(b_ + 1) * 512], in_=yb_[b_].t[:, :], func=AF.Square), r=[yb_[b_], sq], w=[sq])
            P.op("dve", lambda e: e.tensor_reduce(out=st8.t[:, 1, :], in_=sq.t.rearrange("p (h e) -> p h e", h=8), axis=AX.X, op=ALU.add), r=[sq, st8], w=[st8])
            if K_SUB <= 2.7:
                continue
            P.op("dve", lambda e: e.tensor_scalar(out=st8.t[:, 2, :], in0=st8.t[:, 0, :], scalar1=1.0 / 128, scalar2=None, op0=ALU.mult), r=[st8], w=[st8])
            P.op("dve", lambda e: e.tensor_tensor(out=st8.t[:, 3, :], in0=st8.t[:, 2, :], in1=st8.t[:, 2, :], op=ALU.mult), r=[st8], w=[st8])
            P.op("dve", lambda e: e.scalar_tensor_tensor(out=st8.t[:, 4, :], in0=st8.t[:, 1, :], scalar=1.0 / 128, in1=st8.t[:, 3, :], op0=ALU.mult, op1=ALU.subtract), r=[st8], w=[st8])
            P.op("act", lambda e: e.activation(out=st8.t[:, 4, :], in_=st8.t[:, 4, :], func=AF.Sqrt, bias=epsc.t[:, 0:1], scale=1.0), r=[st8, epsc], w=[st8])
            P.op("dve", lambda e: e.reciprocal(out=st8.t[:, 4, :], in_=st8.t[:, 4, :]), r=[st8], w=[st8])
            P.op("dve", lambda e: e.scalar_tensor_tensor(out=st8.t[:, 5, :], in0=st8.t[:, 2, :], scalar=-1.0, in1=st8.t[:, 4, :], op0=ALU.mult, op1=ALU.mult), r=[st8], w=[st8])
            for b_ in range(2):
                y3 = yb_[b_].t[:, :].rearrange("p (h e) -> p h e", h=4)
                yn3 = yn.t[:, b_ * 512:(b_ + 1) * 512].rearrange("p (h e) -> p h e", h=4)
                P.op("dve", lambda e, b_=b_, y3=y3, yn3=yn3: e.tensor_tensor(out=yn3, in0=y3, in1=bc(st8.t[:, 4, b_ * 4:(b_ + 1) * 4].unsqueeze(2), [128, 4, 128]), op=ALU.mult), r=[yb_[b_], st8, yn], w=[yn])
                P.op("dve", lambda e, b_=b_, yn3=yn3: e.tensor_tensor(out=yn3, in0=yn3, in1=bc(st8.t[:, 5, b_ * 4:(b_ + 1) * 4].unsqueeze(2), [128, 4, 128]), op=ALU.add), r=[yn, st8], w=[yn])
            P.op("dve", lambda e, sgc=sgc: e.tensor_tensor(out=ymix.t[:, 0:1024], in0=yn.t, in1=sgc.t, op=ALU.mult), r=[yn, sgc], w=[ymix])
            if K_SUB <= 3:
                continue
            P.op("dve", lambda e, gvc=gvc: e.tensor_reduce(out=lnst.t[:, 0:1], in_=gvc.t, axis=AX.X, op=ALU.add), r=[gvc, lnst], w=[lnst])
            P.op("act", lambda e, gvc=gvc: e.activation(out=gtmp.t, in_=gvc.t, func=AF.Square, accum_out=lnst.t[:, 1:2]), r=[gvc, lnst], w=[gtmp, lnst])
            P.op("dve", lambda e: e.tensor_scalar(out=lnst.t[:, 2:3], in0=lnst.t[:, 0:1], scalar1=1.0 / 1024, scalar2=None, op0=ALU.mult), r=[lnst], w=[lnst])
            P.op("dve", lambda e: e.tensor_tensor(out=lnst.t[:, 3:4], in0=lnst.t[:, 2:3], in1=lnst.t[:, 2:3], op=ALU.mult), r=[lnst], w=[lnst])
            P.op("dve", lambda e: e.scalar_tensor_tensor(out=lnst.t[:, 4:5], in0=lnst.t[:, 1:2], scalar=1.0 / 1024, in1=lnst.t[:, 3:4], op0=ALU.mult, op1=ALU.subtract), r=[lnst], w=[lnst])
            P.op("act", lambda e: e.activation(out=lnst.t[:, 4:5], in_=lnst.t[:, 4:5], func=AF.Sqrt, bias=epsc.t[:, 0:1], scale=1.0), r=[lnst, epsc], w=[lnst])
            P.op("dve", lambda e: e.reciprocal(out=lnst.t[:, 4:5], in_=lnst.t[:, 4:5]), r=[lnst], w=[lnst])
            P.op("dve", lambda e: e.scalar_tensor_tensor(out=lnst.t[:, 5:6], in0=lnst.t[:, 2:3], scalar=-1.0, in1=lnst.t[:, 4:5], op0=ALU.mult, op1=ALU.mult), r=[lnst], w=[lnst])
            P.op("dve", lambda e, gvc=gvc: e.tensor_scalar(out=gtmp.t, in0=gvc.t, scalar1=lnst.t[:, 4:5], scalar2=lnst.t[:, 5:6], op0=ALU.mult, op1=ALU.add), r=[gvc, lnst, gtmp], w=[gtmp])
            P.op("dve", lambda e: e.tensor_tensor(out=gtmp.t, in0=gtmp.t, in1=gm_w.t, op=ALU.mult), r=[gtmp, gm_w], w=[gtmp])
            P.op("dve", lambda e: e.tensor_tensor(out=gvn.t, in0=gtmp.t, in1=gm_b.t, op=ALU.add), r=[gtmp, gm_b], w=[gvn])
            mb_ = [PSR.next(), PSR.next()]
            for g in range(8):
                P.op("pe", lambda e, g=g, mb_=mb_: e.matmul(mb_[g // 4].t[:, (g % 4) * 128:(g % 4 + 1) * 128], lhsT=wsT.t[:, g, :], rhs=gvn.t[:, g * 128:(g + 1) * 128], start=True, stop=True), r=[wsT, gvn], w=[mb_[g // 4]])
            for b_ in range(2):
                P.op("dve", lambda e, b_=b_, mb_=mb_: e.tensor_tensor(out=gtmp.t[:, b_ * 512:(b_ + 1) * 512].rearrange("p (g c) -> p g c", g=4), in0=mb_[b_].t[:, :].rearrange("p (g c) -> p g c", g=4),
                                                                in1=bc(gbs.t[:, b_ * 4:(b_ + 1) * 4].unsqueeze(2), [128, 4, 128]), op=ALU.add), r=[mb_[b_], gbs, gtmp], w=[gtmp])
            P.op("dve", lambda e, guc=guc: e.tensor_tensor(out=ymix.t[:, 1024:2048], in0=gtmp.t, in1=guc.t, op=ALU.mult), r=[gtmp, guc, ymix], w=[ymix])
            if K_SUB <= 4:
                continue
            transposes(lambda k: ymix.t[:, k * 128:(k + 1) * 128], [ymix], 16, lambda k0, nb, j=j: ymixT.t[:, k0:k0 + nb, j * 128:(j + 1) * 128], [ymixT])
        if K_SUB <= 5:
            continue
        for og in range(4):
            wt = load_w(wring, "w_ev_out", og)
            for j in range(NS):
                bank = PSR.next()
                mm_tm(bank, ymixT, [ymixT], j, wt, 16, 512)
                res = res_ring.next()
                ot = out_ring.next()
                rows = slice(t0 + j * 128, t0 + (j + 1) * 128)
                P.dma("sp", res.t, x[rows, og * 512:(og + 1) * 512], w=[res])
                P.op("dve", lambda e, bank=bank, res=res, ot=ot: e.tensor_tensor(out=ot.t, in0=bank.t[:, :], in1=res.t, op=ALU.add), r=[bank, res], w=[ot])
                P.dma("pool", mid1[rows, og * 512:(og + 1) * 512], ot.t, r=[ot])


def layer1_mixer(PH, tag, h1, mid2, S, T):
    P, I, AR, CA, PSR, nc = PH["P"], PH["I"], PH["AR"], PH["CA"], PH["PSR"], PH["nc"]
    barrier, transposes, load_bc, norm_to_fm, load_w, mm_tm, mm_fm = (PH[k] for k in ("barrier", "transposes", "load_bc", "norm_to_fm", "load_w", "mm_tm", "mm_fm"))
    copy_op, evac_eng, identb, identf, epsc = PH["copy_op"], PH["evac_eng"], PH["identb"], PH["identf"], PH["epsc"]
    N = S // 128
    NS = T // 128
    raw_d = P.dram(f"raw_{tag}", [64, 128, S], BF16)
    sz_d = P.dram(f"sz_{tag}", [S, 4096], F32)
    G_d = P.dram(f"G_{tag}", [S, 8, 64], F32)
    GT_d = P.dram(f"GT_{tag}", [N, 64, 128], F32)
    qT_d = P.dram(f"q1T_{tag}", [16, 128, S], BF16)
    kT_d = P.dram(f"k1T_{tag}", [16, 128, S], BF16)
    k_d = P.dram(f"k1_{tag}", [S, 2048], BF16)
    v_d = P.dram(f"v1_{tag}", [S, 4096], BF16)
    o_d = [P.dram(f"o{d}_{tag}", [S, 4096], F32) for d in range(2)]
    om_d = P.dram(f"om_{tag}", [S, 4096], BF16)

    barrier()
    AR.reset()
    wring = Ring([AR.alloc([16, 512], BF16, "w") for _ in range(3)])
    hnT = AR.alloc([16, T], BF16, "hnT")
    xin_ring = Ring([AR.alloc([D], F32, "xin") for _ in range(2)])
    xn_ring = Ring([AR.alloc([D], BF16, "xn") for _ in range(2)])
    junk = AR.alloc([D], BF16, "junk")
    stat_ring = Ring([AR.alloc([4], F32, "st") for _ in range(2)])
    wbc = AR.alloc([D], F32, "wbc")
    load_bc(wbc, I["norm1_w"][1:2, :])
    dtb = AR.alloc([64], F32, "dtb")
    load_bc(dtb, I["od_dt_bias"][0:1, :])
    nrate = AR.alloc([64], F32, "nrate")
    load_bc(nrate, I["od_a_log"][0:1, :])
    P.op("act", lambda e: e.activation(out=nrate.t, in_=nrate.t, func=AF.Exp), r=[nrate], w=[nrate])
    P.op("dve", lambda e: e.tensor_scalar(out=nrate.t, in0=nrate.t, scalar1=-1.0, scalar2=None, op0=ALU.mult), r=[nrate], w=[nrate])
    cum = AR.alloc([3, 128], F32, "cum")
    P.dma("sp", cum.t, I["gdn_cum"][:, :, :], w=[cum])
    raw_ring = Ring([AR.alloc([4, T], BF16, "raw") for _ in range(2)])
    o32_ring = Ring([AR.alloc([512], F32, "o32") for _ in range(3)])
    Gt_ring = Ring([AR.alloc([8, 64], F32, "Gt") for _ in range(2)])
    gx = AR.alloc([6, 64], F32, "gx")
    gT_ring = Ring([AR.alloc([128], F32, "gT") for _ in range(2)])
    for t0 in range(0, S, T):
        norm_to_fm(h1, t0, T, wbc, hnT, xin_ring, xn_ring, junk, stat_ring)
        for g in range(16):
            wt = load_w(wring, "w_od_qkv", g)
            raw = raw_ring.next()
            for c in range(4):
                bank = PSR.next()
                mm_fm(bank, hnT, [hnT], T, wt, c, 16)
                copy_op(evac_eng(), raw.t[:, c, :], bank.t[:, 0:T], [bank], [raw])
            P.dma("pool", raw_d[g * 4:(g + 1) * 4, :, t0:t0 + T].rearrange("c p t -> p c t"), raw.t, r=[raw])
        for g in range(8):
            wt = load_w(wring, "w_od_z", g)
            for j in range(NS):
                bank = PSR.next()
                mm_tm(bank, hnT, [hnT], j, wt, 16, 512)
                o = o32_ring.next()
                P.op("act", lambda e, bank=bank, o=o: e.activation(out=o.t, in_=bank.t[:, :], func=AF.Silu), r=[bank], w=[o])
                P.dma("pool", sz_d[t0 + j * 128:t0 + (j + 1) * 128, g * 512:(g + 1) * 512], o.t, r=[o])
        wt = load_w(wring, "w_od_g", 0)
        for j in range(NS):
            n = t0 // 128 + j
            bank = PSR.next()
            mm_tm(bank, hnT, [hnT], j, wt, 16, 128)
            Gt = Gt_ring.next()
            P.op("act", lambda e, bank=bank, Gt=Gt: e.activation(out=Gt.t[:, 0, :], in_=bank.t[:, 0:64], func=AF.Sigmoid), r=[bank], w=[Gt])
            P.op("dve", lambda e, bank=bank: e.tensor_tensor(out=gx.t[:, 0, :], in0=bank.t[:, 64:128], in1=dtb.t, op=ALU.add), r=[bank, dtb, gx], w=[gx])
            P.op("dve", lambda e: e.scalar_tensor_tensor(out=gx.t[:, 1, :], in0=gx.t[:, 0, :], scalar=-1.0, in1=gx.t[:, 0, :], op0=ALU.mult, op1=ALU.max), r=[gx], w=[gx])
            P.op("act", lambda e: e.activation(out=gx.t[:, 1, :], in_=gx.t[:, 1, :], func=AF.Exp, scale=-1.0), r=[gx], w=[gx])
            P.op("act", lambda e: e.activation(out=gx.t[:, 1, :], in_=gx.t[:, 1, :], func=AF.Ln, bias=epsc.t[:, 1:2], scale=1.0), r=[gx, epsc], w=[gx])
            P.op("dve", lambda e: e.scalar_tensor_tensor(out=gx.t[:, 2, :], in0=gx.t[:, 0, :], scalar=0.0, in1=gx.t[:, 1, :], op0=ALU.max, op1=ALU.add), r=[gx], w=[gx])
            P.op("dve", lambda e: e.tensor_tensor(out=gx.t[:, 3, :], in0=gx.t[:, 2, :], in1=nrate.t, op=ALU.mult), r=[gx, nrate], w=[gx])
            P.op("act", lambda e, Gt=Gt: e.activation(out=gx.t[:, 4, :], in_=Gt.t[:, 0, :], func=AF.Ln), r=[Gt, gx], w=[gx])
            cb = PSR.next()
            cb2 = PSR.next()
            for dr in range(2):
                hs = slice(dr * 32, (dr + 1) * 32)
                P.op("pe", lambda e, dr=dr, hs=hs, cb=cb: e.matmul(cb.t[:, dr * 32:(dr + 1) * 32], lhsT=cum.t[:, dr, :], rhs=gx.t[:, 3, hs], start=True, stop=True), r=[cum, gx], w=[cb])
                P.op("pe", lambda e, dr=dr, hs=hs, cb=cb: e.matmul(cb.t[:, 64 + dr * 32:64 + (dr + 1) * 32], lhsT=cum.t[:, 2, :], rhs=gx.t[:, 3, hs], start=True, stop=True), r=[cum, gx], w=[cb])
                P.op("pe", lambda e, dr=dr, hs=hs, cb2=cb2: e.matmul(cb2.t[0:32, dr * 128:(dr + 1) * 128], lhsT=gx.t[:, 3, hs], rhs=cum.t[:, dr, :], start=True, stop=True), r=[cum, gx], w=[cb2])
            gT = gT_ring.next()
            for dr in range(2):
                P.op("dve", lambda e, dr=dr, cb2=cb2, gT=gT: e.tensor_copy(out=gT.t[0:32, :], in_=cb2.t[0:32, dr * 128:(dr + 1) * 128]), r=[cb2, gT], w=[gT])
                P.dma("pool", GT_d[n, dr * 32:(dr + 1) * 32, :], gT.t[0:32, :], r=[gT])
                gT = gT_ring.next() if dr == 0 else gT
            P.op("dve", lambda e, cb=cb, Gt=Gt: e.tensor_copy(out=Gt.t[:, 1, :], in_=cb.t[:, 0:64]), r=[cb, Gt], w=[Gt])
            P.op("dve", lambda e, Gt=Gt: e.tensor_tensor(out=Gt.t[:, 2, :], in0=Gt.t[:, 1, :], in1=gx.t[:, 4, :], op=ALU.add), r=[Gt, gx], w=[Gt])
            P.op("act", lambda e, Gt=Gt: e.activation(out=Gt.t[:, 4, :], in_=Gt.t[:, 1, :], func=AF.Exp), r=[Gt], w=[Gt])
            P.op("dve", lambda e, Gt=Gt: e.scalar_tensor_tensor(out=Gt.t[:, 3, :], in0=Gt.t[:, 0, :], scalar=-1.0, in1=Gt.t[:, 4, :], op0=ALU.mult, op1=ALU.mult), r=[Gt], w=[Gt])
            P.op("dve", lambda e, cb=cb, Gt=Gt: e.tensor_tensor(out=Gt.t[:, 5, :], in0=cb.t[:, 64:128], in1=Gt.t[:, 1, :], op=ALU.subtract), r=[cb, Gt], w=[Gt])
            P.op("act", lambda e, Gt=Gt: e.activation(out=Gt.t[:, 5, :], in_=Gt.t[:, 5, :], func=AF.Exp), r=[Gt], w=[Gt])
            P.op("act", lambda e, cb=cb, Gt=Gt: e.activation(out=Gt.t[:, 6, :], in_=cb.t[:, 64:128], func=AF.Exp), r=[cb, Gt], w=[Gt])
            P.op("dve", lambda e, Gt=Gt: e.tensor_scalar(out=Gt.t[:, 7, :], in0=Gt.t[:, 1, :], scalar1=-1.0, scalar2=None, op0=ALU.mult), r=[Gt], w=[Gt])
            P.dma("pool", G_d[n * 128:(n + 1) * 128, :, :], Gt.t, r=[Gt])

    barrier()
    AR.reset()
    cw = AR.alloc([64, 5], F32, "cw")
    P.dma("sp", cw.t, I["od_conv"][:, :, :], w=[cw])
    onesb = AR.alloc([128], BF16, "ones")
    P.op("dve", lambda e: e.memset(onesb.t, 1.0), w=[onesb])
    l2e = AR.alloc([1], F32, "l2e")
    P.op("dve", lambda e: e.memset(l2e.t, L2_EPS), w=[l2e])
    diag_ring = Ring([AR.alloc([5, 128], BF16, "diag") for _ in range(2)])
    win_ring = Ring([AR.alloc([T + 4], BF16, "win") for _ in range(3)])
    xs_ring = Ring([AR.alloc([T], F32, "xs") for _ in range(2)])
    sq_ring = Ring([AR.alloc([T], BF16, "sq") for _ in range(2)])
    rn_ring = Ring([AR.alloc([T], F32, "rn") for _ in range(2)])
    xb_ring = Ring([AR.alloc([T], BF16, "xb") for _ in range(3)])
    tm_ring = Ring([AR.alloc([NS, 128], BF16, "tm") for _ in range(3)])
    for cc in range(64):
        dg = diag_ring.next()
        for w_ in range(5):
            P.op("dve", lambda e, w_=w_, dg=dg, cc=cc: e.tensor_scalar(out=dg.t[:, w_, :], in0=identf.t, scalar1=cw.t[:, cc, w_:w_ + 1], scalar2=None, op0=ALU.mult), r=[identf, cw, dg], w=[dg])
        for t0 in range(0, S, T):
            win = win_ring.next()
            lo, hi = max(t0 - 2, 0), min(t0 + T + 2, S)
            if lo != t0 - 2 or hi != t0 + T + 2:
                P.op("dve", lambda e, win=win: e.memset(win.t, 0.0), w=[win])
            P.dma("sp", win.t[:, lo - (t0 - 2):hi - (t0 - 2)], raw_d[cc, :, lo:hi], r=[win], w=[win])
            bank = PSR.next()
            for w_ in range(5):
                P.op("pe", lambda e, w_=w_, dg=dg, win=win, bank=bank: e.matmul(bank.t[:, 0:T], lhsT=dg.t[:, w_, :], rhs=win.t[:, w_:w_ + T], start=(w_ == 0), stop=(w_ == 4)), r=[dg, win], w=[bank])
            xb = xb_ring.next()
            if cc < 32:
                xs = xs_ring.next()
                sq = sq_ring.next()
                rn = rn_ring.next()
                P.op("act", lambda e, bank=bank, xs=xs: e.activation(out=xs.t, in_=bank.t[:, 0:T], func=AF.Silu), r=[bank], w=[xs])
                P.op("dve", lambda e, xs=xs, sq=sq: e.tensor_tensor(out=sq.t, in0=xs.t, in1=xs.t, op=ALU.mult), r=[xs], w=[sq])
                b2 = PSR.next()
                P.op("pe", lambda e, sq=sq, b2=b2: e.matmul(b2.t[:, 0:T], lhsT=onesb.t, rhs=sq.t, start=True, stop=True), r=[onesb, sq], w=[b2])
                P.op("act", lambda e, b2=b2, rn=rn: e.activation(out=rn.t, in_=b2.t[:, 0:T], func=AF.Sqrt, bias=l2e.t[:, 0:1], scale=1.0), r=[b2, l2e], w=[rn])
                P.op("dve", lambda e, rn=rn: e.reciprocal(out=rn.t, in_=rn.t), r=[rn], w=[rn])
                sc = 128 ** -0.5 if cc < 16 else 1.0
                P.op("dve", lambda e, xs=xs, rn=rn, xb=xb, sc=sc: e.scalar_tensor_tensor(out=xb.t, in0=xs.t, scalar=sc, in1=rn.t, op0=ALU.mult, op1=ALU.mult), r=[xs, rn], w=[xb])
                dst = qT_d if cc < 16 else kT_d
                P.dma("pool", dst[cc % 16, :, t0:t0 + T], xb.t, r=[xb])
            else:
                P.op("act", lambda e, bank=bank, xb=xb: e.activation(out=xb.t, in_=bank.t[:, 0:T], func=AF.Silu), r=[bank], w=[xb])
            if cc >= 16:
                tm = tm_ring.next()
                transposes(lambda k, xb=xb: xb.t[:, k * 128:(k + 1) * 128], [xb], NS, lambda k0, nb, tm=tm: tm.t[:, k0:k0 + nb, :], [tm])
                dd, c0 = (k_d, (cc - 16) * 128) if cc < 32 else (v_d, (cc - 32) * 128)
                P.dma("pool", dd[t0:t0 + T, c0:c0 + 128].rearrange("(j p) e -> p j e", p=128), tm.t, r=[tm])

    barrier()
    AR.reset()
    msk = AR.alloc([2, 2, 128], F32, "msk")
    P.dma("sp", msk.t, I["gdn_msk"][:, :, :, :], w=[msk])
    Sst = AR.alloc([32, 128], F32, "S")
    Sb = AR.alloc([32, 128], BF16, "Sb")
    GB = AR.alloc([32, 128], F32, "GB")
    bv = AR.alloc([32, 128], F32, "bv")
    osb = AR.alloc([32, 128], F32, "osb")
    kT_ring = Ring([AR.alloc([16, 128], BF16, "kT") for _ in range(2)])
    qT_ring = Ring([AR.alloc([16, 128], BF16, "qT") for _ in range(2)])
    ktm_ring = Ring([AR.alloc([16, 128], BF16, "ktm") for _ in range(1)])
    v_ring = Ring([AR.alloc([32, 128], BF16, "v") for _ in range(1)])
    Gt_ring = Ring([AR.alloc([8, 64], F32, "Gt") for _ in range(2)])
    X_ring = Ring([AR.alloc([4, 128], F32, "X") for _ in range(4)])
    h16 = lambda nm, n: Ring([AR.alloc([4, 128], BF16, nm) for _ in range(n)])
    h32 = lambda nm, n: Ring([AR.alloc([4, 128], F32, nm) for _ in range(n)])
    A_ring, QK_ring, P_ring, Q_ring, Z_ring = h32("A", 2), h16("QK", 2), h32("P", 3), h32("Q", 3), h32("Z", 3)
    B_ring, QKT_ring = h32("B", 2), h16("QKT", 2)
    r_ring, vn_ring, kd_ring = h32("r", 2), h16("vn", 2), h16("kd", 2)
    t32_ring = Ring([AR.alloc([4, 128], F32, "t32") for _ in range(2)])
    b4 = lambda ap: bc(ap.unsqueeze(2), [128, 4, 128])
    for dr in range(2):
        P.op("dve", lambda e: e.memset(Sst.t, 0.0), r=[Sst], w=[Sst])
        P.op("dve", lambda e: e.memset(Sb.t, 0.0), r=[Sb], w=[Sb])
        order = range(N) if dr == 0 else range(N - 1, -1, -1)
        for n in order:
            cs = slice(n * 128, (n + 1) * 128)
            kT, qT, ktm, vv, Gt = kT_ring.next(), qT_ring.next(), ktm_ring.next(), v_ring.next(), Gt_ring.next()
            P.dma("sp", kT.t, kT_d[:, :, cs].rearrange("c p t -> p c t"), w=[kT])
            P.dma("sp", qT.t, qT_d[:, :, cs].rearrange("c p t -> p c t"), w=[qT])
            P.dma("sp", ktm.t.rearrange("p a b -> p (a b)"), k_d[cs, :], w=[ktm])
            P.dma("sp", vv.t.rearrange("p a b -> p (a b)"), v_d[cs, :], w=[vv])
            P.dma("sp", Gt.t, G_d[cs, :, :], w=[Gt])
            P.dma("sp", GB.t.rearrange("p a b -> p (a b)"), GT_d[n:n + 1, dr * 32:(dr + 1) * 32, :].rearrange("o h j -> o (h j)").partition_broadcast(128), w=[GB])
            G = lambda f, hs, Gt=Gt, dr=dr: Gt.t[:, f, dr * 32 + hs.start:dr * 32 + hs.stop]
            allh = slice(0, 32)
            P.op("dve", lambda e, vv=vv, Gt=Gt, dr=dr: e.tensor_tensor(out=bv.t, in0=vv.t, in1=bc(Gt.t[:, 0, dr * 32:(dr + 1) * 32].unsqueeze(2), [128, 32, 128]), op=ALU.mult), r=[vv, Gt, bv], w=[bv])
            for hg in range(8):
                hs = slice(hg * 4, hg * 4 + 4)
                XA, XE = X_ring.next(), X_ring.next()
                P.op("dve", lambda e, XA=XA, hs=hs, G=G: e.scalar_tensor_tensor(out=XA.t, in0=GB.t[:, hs, :], scalar=-1.0, in1=b4(G(2, hs)), op0=ALU.mult, op1=ALU.add), r=[GB, Gt], w=[XA])
                P.op("pool", lambda e, XA=XA, dr=dr: e.tensor_tensor(out=XA.t, in0=XA.t, in1=bc(msk.t[:, dr, 0:1, :], [128, 4, 128]), op=ALU.add), r=[XA, msk], w=[XA])
                P.op("act", lambda e, XA=XA: e.activation(out=XA.t, in_=XA.t, func=AF.Exp), r=[XA], w=[XA])
                P.op("dve", lambda e, XE=XE, hs=hs, G=G: e.scalar_tensor_tensor(out=XE.t, in0=GB.t[:, hs, :], scalar=-1.0, in1=b4(G(1, hs)), op0=ALU.mult, op1=ALU.add), r=[GB, Gt], w=[XE])
                P.op("pool", lambda e, XE=XE, dr=dr: e.tensor_tensor(out=XE.t, in0=XE.t, in1=bc(msk.t[:, dr, 1:2, :], [128, 4, 128]), op=ALU.add), r=[XE, msk], w=[XE])
                P.op("act", lambda e, XE=XE: e.activation(out=XE.t, in_=XE.t, func=AF.Exp), r=[XE], w=[XE])
                kb = PSR.next()
                for q_ in range(2):
                    kh = hg * 2 + q_
                    P.op("pe", lambda e, q_=q_, kh=kh, kb=kb, kT=kT: e.matmul(kb.t[:, q_ * 128:(q_ + 1) * 128], lhsT=kT.t[:, kh, :], rhs=kT.t[:, kh, :], start=True, stop=True), r=[kT], w=[kb])
                    P.op("pe", lambda e, q_=q_, kh=kh, kb=kb, kT=kT, qT=qT: e.matmul(kb.t[:, 256 + q_ * 128:256 + (q_ + 1) * 128], lhsT=qT.t[:, kh, :], rhs=kT.t[:, kh, :], start=True, stop=True), r=[kT, qT], w=[kb])
                A, QK = A_ring.next(), QK_ring.next()
                k2 = lambda ap: bc(ap.rearrange("p (k j) -> p k j", k=2).unsqueeze(2), [128, 2, 2, 128])
                v4 = lambda t_: t_.t.rearrange("p (k r) j -> p k r j", k=2)
                P.op("dve", lambda e, A=A, XA=XA, kb=kb: e.tensor_tensor(out=v4(A), in0=k2(kb.t[:, 0:256]), in1=v4(XA), op=ALU.mult), r=[kb, XA], w=[A])
                P.op("dve", lambda e, QK=QK, XE=XE, kb=kb: e.tensor_tensor(out=v4(QK), in0=k2(kb.t[:, 256:512]), in1=v4(XE), op=ALU.mult), r=[kb, XE], w=[QK])
                Bm, QKT = B_ring.next(), QKT_ring.next()
                tb_ = PSR.next()
                for k in range(4):
                    P.op("pe", lambda e, k=k, tb_=tb_, A=A: e.transpose(out=tb_.t[:, k * 128:(k + 1) * 128], in_=A.t[:, k, :], identity=identf.t), r=[A, identf], w=[tb_])
                copy_op("act", Bm.t, tb_.t[:, :].rearrange("p (h j) -> p h j", h=4), [tb_], [Bm])
                transposes(lambda k, QK=QK: QK.t[:, k, :], [QK], 4, lambda k0, nb, QKT=QKT: QKT.t[:, k0:k0 + nb, :], [QKT])
                Z = Z_ring.next()
                P.op("dve", lambda e, Z=Z, Bm=Bm: e.tensor_tensor(out=Z.t, in0=bc(identf.t.unsqueeze(1), [128, 4, 128]), in1=Bm.t, op=ALU.subtract), r=[identf, Bm], w=[Z])
                BQ = Bm
                Pm, Qm = A, None
                Qap = lambda hh, Bm=Bm: Bm.t[:, hh, :]
                for lvl in range(6):
                    pb_, Pn = PSR.next(), P_ring.next()
                    for hh in range(4):
                        P.op("pe", lambda e, hh=hh, pb_=pb_, Pm=Pm, Qap=Qap: e.matmul(pb_.t[:, hh * 128:(hh + 1) * 128], lhsT=Qap(hh), rhs=Pm.t[:, hh, :], start=True, stop=True), r=[Pm, BQ] + ([Qm] if Qm else []), w=[pb_])
                    copy_op("act", Pn.t, pb_.t[:, :].rearrange("p (h j) -> p h j", h=4), [pb_], [Pn])
                    if lvl < 5:
                        qb_, Qn = PSR.next(), Q_ring.next()
                        for hh in range(4):
                            P.op("pe", lambda e, hh=hh, qb_=qb_, Pm=Pm, Qap=Qap: e.matmul(qb_.t[:, hh * 128:(hh + 1) * 128], lhsT=Pm.t[:, hh, :], rhs=Qap(hh), start=True, stop=True), r=[Pm, BQ] + ([Qm] if Qm else []), w=[qb_])
                        copy_op("dve", Qn.t, qb_.t[:, :].rearrange("p (h j) -> p h j", h=4), [qb_], [Qn])
                    zb_, Zn = PSR.next(), Z_ring.next()
                    for hh in range(4):
                        P.op("pe", lambda e, hh=hh, zb_=zb_, Pn=Pn, Z=Z: e.matmul(zb_.t[:, hh * 128:(hh + 1) * 128], lhsT=Pn.t[:, hh, :], rhs=Z.t[:, hh, :], start=True, stop=True), r=[Pn, Z], w=[zb_])
                    P.op("dve", lambda e, zb_=zb_, Zn=Zn, Z=Z: e.tensor_tensor(out=Zn.t, in0=zb_.t[:, :].rearrange("p (h j) -> p h j", h=4), in1=Z.t, op=ALU.add), r=[zb_, Z], w=[Zn])
                    Z, Pm = Zn, Pn
                    if lvl < 5:
                        Qm = Qn
                        Qap = lambda hh, Qn=Qn: Qn.t[:, hh, :]
                kd = kd_ring.next()
                P.op("pool", lambda e, kd=kd, ktm=ktm, hs=hs, G=G, hg=hg: e.tensor_tensor(out=v4(kd), in0=bc(ktm.t[:, hg * 2:hg * 2 + 2, :].unsqueeze(2), [128, 2, 2, 128]),
                                                                                 in1=bc(G(5, hs).rearrange("p (k r) -> p k r", k=2).unsqueeze(3), [128, 2, 2, 128]), op=ALU.mult), r=[ktm, Gt], w=[kd])
                ab = PSR.next()
                for hh in range(4):
                    h = hg * 4 + hh
                    P.op("pe", lambda e, hh=hh, h=h, ab=ab, kT=kT: e.matmul(ab.t[:, hh * 128:(hh + 1) * 128], lhsT=kT.t[:, h // 2, :], rhs=Sb.t[:, h, :], start=True, stop=True), r=[kT, Sb], w=[ab])
                t32 = t32_ring.next()
                r_ = r_ring.next()
                P.op("dve", lambda e, ab=ab, t32=t32, hs=hs, G=G: e.tensor_tensor(out=t32.t, in0=ab.t[:, :].rearrange("p (h j) -> p h j", h=4), in1=b4(G(3, hs)), op=ALU.mult), r=[ab, Gt], w=[t32])
                P.op("dve", lambda e, t32=t32, r_=r_, hs=hs: e.tensor_tensor(out=r_.t, in0=t32.t, in1=bv.t[:, hs, :], op=ALU.add), r=[t32, bv], w=[r_])
                cb_ = PSR.next()
                for hh in range(4):
                    P.op("pe", lambda e, hh=hh, cb_=cb_, Z=Z, r_=r_: e.matmul(cb_.t[:, hh * 128:(hh + 1) * 128], lhsT=Z.t[:, hh, :], rhs=r_.t[:, hh, :], start=True, stop=True), r=[Z, r_], w=[cb_])
                vn = vn_ring.next()
                copy_op("act", vn.t, cb_.t[:, :].rearrange("p (h j) -> p h j", h=4), [cb_], [vn])
                d1, d2 = PSR.next(), PSR.next()
                for hh in range(4):
                    h = hg * 4 + hh
                    P.op("pe", lambda e, hh=hh, h=h, d1=d1, qT=qT: e.matmul(d1.t[:, hh * 128:(hh + 1) * 128], lhsT=qT.t[:, h // 2, :], rhs=Sb.t[:, h, :], start=True, stop=True), r=[qT, Sb], w=[d1])
                    P.op("pe", lambda e, hh=hh, d2=d2, QKT=QKT, vn=vn: e.matmul(d2.t[:, hh * 128:(hh + 1) * 128], lhsT=QKT.t[:, hh, :], rhs=vn.t[:, hh, :], start=True, stop=True), r=[QKT, vn], w=[d2])
                t32b = t32_ring.next()
                P.op("dve", lambda e, d1=d1, t32b=t32b, hs=hs, G=G: e.tensor_tensor(out=t32b.t, in0=d1.t[:, :].rearrange("p (h j) -> p h j", h=4), in1=b4(G(4, hs)), op=ALU.mult), r=[d1, Gt], w=[t32b])
                P.op("dve", lambda e, d2=d2, t32b=t32b, hs=hs: e.tensor_tensor(out=osb.t[:, hs, :], in0=d2.t[:, :].rearrange("p (h j) -> p h j", h=4), in1=t32b.t, op=ALU.add), r=[d2, t32b, osb], w=[osb])
                eb = PSR.next()
                for hh in range(4):
                    P.op("pe", lambda e, hh=hh, eb=eb, kd=kd, vn=vn: e.matmul(eb.t[:, hh * 128:(hh + 1) * 128], lhsT=kd.t[:, hh, :], rhs=vn.t[:, hh, :], start=True, stop=True), r=[kd, vn], w=[eb])
                P.op("pool", lambda e, hs=hs, G=G: e.tensor_tensor(out=Sst.t[:, hs, :], in0=Sst.t[:, hs, :], in1=b4(G(6, hs)), op=ALU.mult), r=[Sst, Gt], w=[Sst])
                P.op("dve", lambda e, hs=hs, eb=eb: e.tensor_tensor(out=Sst.t[:, hs, :], in0=Sst.t[:, hs, :], in1=eb.t[:, :].rearrange("p (h j) -> p h j", h=4), op=ALU.add), r=[Sst, eb], w=[Sst])
                copy_op("act", Sb.t[:, hs, :], Sst.t[:, hs, :], [Sst, Sb], [Sb])
            P.dma("pool", o_d[dr][cs, :], osb.t.rearrange("p a b -> p (a b)"), r=[osb])

    barrier()
    AR.reset()
    nw = AR.alloc([128], F32, "nw")
    load_bc(nw, I["od_norm_w"][0:1, :])
    oa_ring = Ring([AR.alloc([32, 128], F32, "oa") for _ in range(2)])
    ob_ring = Ring([AR.alloc([32, 128], F32, "ob") for _ in range(2)])
    sz_ring = Ring([AR.alloc([32, 128], F32, "sz") for _ in range(2)])
    sqt = AR.alloc([32, 128], F32, "sqt")
    om_ring = Ring([AR.alloc([32, 128], BF16, "om") for _ in range(2)])
    rs_ring = Ring([AR.alloc([32], F32, "rs") for _ in range(2)])
    for n in range(N):
        cs = slice(n * 128, (n + 1) * 128)
        oa, ob, sz, om, rs = oa_ring.next(), ob_ring.next(), sz_ring.next(), om_ring.next(), rs_ring.next()
        P.dma("sp", oa.t.rearrange("p a b -> p (a b)"), o_d[0][cs, :], w=[oa])
        P.dma("sp", ob.t.rearrange("p a b -> p (a b)"), o_d[1][cs, :], w=[ob])
        P.dma("sp", sz.t.rearrange("p a b -> p (a b)"), sz_d[cs, :], w=[sz])
        P.op("dve", lambda e, oa=oa, ob=ob: e.tensor_tensor(out=oa.t, in0=oa.t, in1=ob.t, op=ALU.add), r=[oa, ob], w=[oa])
        P.op("act", lambda e, oa=oa: e.activation(out=sqt.t, in_=oa.t, func=AF.Square), r=[oa, sqt], w=[sqt])
        P.op("dve", lambda e, rs=rs: e.tensor_reduce(out=rs.t, in_=sqt.t, axis=AX.X, op=ALU.add), r=[sqt], w=[rs])
        P.op("act", lambda e, rs=rs: e.activation(out=rs.t, in_=rs.t, func=AF.Sqrt, bias=epsc.t[:, 0:1], scale=1.0 / 128), r=[rs, epsc], w=[rs])
        P.op("dve", lambda e, rs=rs: e.reciprocal(out=rs.t, in_=rs.t), r=[rs], w=[rs])
        P.op("dve", lambda e, oa=oa, rs=rs: e.tensor_tensor(out=oa.t, in0=oa.t, in1=bc(rs.t.unsqueeze(2), [128, 32, 128]), op=ALU.mult), r=[oa, rs], w=[oa])
        P.op("pool", lambda e, oa=oa: e.tensor_tensor(out=oa.t, in0=oa.t, in1=bc(nw.t.unsqueeze(1), [128, 32, 128]), op=ALU.mult), r=[oa, nw], w=[oa])
        P.op("dve", lambda e, oa=oa, sz=sz, om=om: e.tensor_tensor(out=om.t, in0=oa.t, in1=sz.t, op=ALU.mult), r=[oa, sz], w=[om])
        P.dma("pool", om_d[cs, :], om.t.rearrange("p a b -> p (a b)"), r=[om])

    barrier()
    AR.reset()
    wring = Ring([AR.alloc([16, 512], BF16, "w") for _ in range(3)])
    omT = AR.alloc([32, T], BF16, "omT")
    om_ring = Ring([AR.alloc([4096], BF16, "om") for _ in range(2)])
    res_ring = Ring([AR.alloc([512], F32, "res") for _ in range(3)])
    out_ring = Ring([AR.alloc([512], F32, "out") for _ in range(3)])
    for t0 in range(0, S, T):
        for j in range(NS):
            om = om_ring.next()
            P.dma("sp", om.t, om_d[t0 + j * 128:t0 + (j + 1) * 128, :], w=[om])
            transposes(lambda k, om=om: om.t[:, k * 128:(k + 1) * 128], [om], 32, lambda k0, nb, j=j: omT.t[:, k0:k0 + nb, j * 128:(j + 1) * 128], [omT])
        for og in range(4):
            banks = [PSR.next() for _ in range(NS)]
            for pi, (k0, k1) in enumerate(((0, 16), (16, 32))):
                wd = load_w(wring, "w_od_out", og, k0, k1)
                for j in range(NS):
                    mm_tm(banks[j], omT, [omT], j, wd, 16, 512, first=(pi == 0), last=(pi == 1), kc_off=k0)
            for j in range(NS):
                res, ot = res_ring.next(), out_ring.next()
                rows = slice(t0 + j * 128, t0 + (j + 1) * 128)
                P.dma("sp", res.t, h1[rows, og * 512:(og + 1) * 512], w=[res])
                P.op("dve", lambda e, b=banks[j], res=res, ot=ot: e.tensor_tensor(out=ot.t, in0=b.t[:, :], in1=res.t, op=ALU.add), r=[banks[j], res], w=[ot])
                P.dma("pool", mid2[rows, og * 512:(og + 1) * 512], ot.t, r=[ot])


def prep_inputs(inp, SS, SP, ncores=8):
    f = lambda a: np.ascontiguousarray(np.asarray(a, dtype=np.float32))
    sh = {}
    sh["norm1_w"] = f(inp["norm1_w"])
    sh["norm2_w"] = f(inp["norm2_w"])
    sh["final_norm_w"] = f(inp["final_norm_w"]).reshape(1, D)
    w = f(inp["ev_w_in"])[0]
    perm = np.array([h * 64 + (d + 32) % 64 for h in range(8) for d in range(64)])
    wq, wk = w[:, 0:512], w[:, 512:1024]
    w2 = np.concatenate([wq, wq[:, perm], wk, wk[:, perm], w[:, 1024:]], axis=1)
    sh["w_ev_in"] = group_w(w2)
    sh["w_ev_out"] = group_w(f(inp["ev_w_out"])[0])
    lg = f(inp["ev_ret_decay_logit"])[0]
    sh["ret_logit"] = lg.reshape(1, 16)
    sh["ret_logit_pair"] = np.ascontiguousarray(np.stack([lg[d].reshape(4, 2).T for d in range(2)], 0))
    sh["ret_gn_w"] = f(inp["ev_ret_gn_w"]).reshape(1, 1024)
    sh["gm_ln_w"] = f(inp["ev_gm_ln_w"]).reshape(1, 1024)
    sh["gm_ln_b"] = f(inp["ev_gm_ln_b"]).reshape(1, 1024)
    sh["gm_wsT"] = np.ascontiguousarray(f(inp["ev_gm_ws"])[0].transpose(2, 0, 1))
    sh["gm_bs"] = np.ascontiguousarray(f(inp["ev_gm_bs"])[0].T)
    wo = f(inp["od_w_in"])[0]
    sh["w_od_qkv"] = group_w(wo[:, 0:8192])
    sh["w_od_z"] = group_w(wo[:, 8192:12288])
    sh["w_od_g"] = group_w(wo[:, 12288:12416])
    sh["od_conv"] = np.ascontiguousarray(f(inp["od_conv_w"])[0].reshape(5, 64, 128).transpose(2, 1, 0))
    sh["od_a_log"] = f(inp["od_a_log"]).reshape(1, 64)
    sh["od_dt_bias"] = f(inp["od_dt_bias"]).reshape(1, 64)
    sh["od_norm_w"] = f(inp["od_norm_w"]).reshape(1, 128)
    sh["w_od_out"] = group_w(f(inp["od_w_out"])[0])
    for l in range(2):
        sh[f"w_gate{l}"] = group_w(f(inp["ffn_w_gate"])[l])
        sh[f"w_up{l}"] = group_w(f(inp["ffn_w_up"])[l])
        sh[f"w_down{l}"] = group_w(f(inp["ffn_w_down"])[l])
    sh.update(host_consts(max(SS, SP)))
    xs = np.asarray(inp["x_sample"], dtype=np.float32)
    xp = np.asarray(inp["x_prompt"], dtype=np.float32)
    maps = []
    for c in range(ncores):
        m = dict(sh)
        m["x_s"] = np.ascontiguousarray(xs[c % xs.shape[0]])
        m["x_p"] = np.ascontiguousarray(xp[c % xp.shape[0]])
        maps.append(m)
    return maps


_NC_CACHE = {}


def run(inp, SS, SP, nlayers=2, trace=False):
    key = (SS, SP, nlayers)
    if key not in _NC_CACHE:
        _NC_CACHE[key] = build_program(SS, SP, nlayers)
    nc = _NC_CACHE[key]
    maps = prep_inputs(inp, SS, SP)
    res = run_bass_kernel_spmd(nc, maps, core_ids=list(range(8)), trace=trace)
    ys = np.stack([res.results[c]["y_s"] for c in range(8)], 0)
    yp = np.stack([res.results[c]["y_p"] for c in range(4)], 0)
    return yp, ys, res


def kernel(**inputs):
    SS = inputs["x_sample"].shape[1]
    SP = inputs["x_prompt"].shape[1]
    yp, ys, _ = run(inputs, SS, SP)
    return (yp.astype(np.float32), ys.astype(np.float32))
```

```python
from contextlib import ExitStack
import numpy as np
import ml_dtypes
import concourse.bass as bass
import concourse.mybir as mybir
from concourse.bass_utils import run_bass_kernel_spmd

F32 = mybir.dt.float32
BF16 = mybir.dt.bfloat16
AF = mybir.ActivationFunctionType
ALU = mybir.AluOpType
AX = mybir.AxisListType

D = 2048
DFF = 5632
NORM_EPS = 1e-6
L2_EPS = 1e-6
import os as _os
SAME_ENGINE_SYNC = bool(int(_os.environ.get('K_SES', '1')))
DBG_LEVEL = int(_os.environ.get('K_DBG', '9'))
K_SUB = float(_os.environ.get('K_SUB', '9'))
NDSEM = 20


class Buf:
    __slots__ = ("name", "writers", "readers")

    def __init__(self, name):
        self.name = name
        self.writers = []
        self.readers = []


class Op:
    __slots__ = ("eng", "fn", "deps", "sig", "tick", "dma", "dsem", "dcount", "prev_slot")

    def __init__(self, eng, fn, dma=False):
        self.eng = eng
        self.fn = fn
        self.deps = []
        self.sig = False
        self.tick = 0
        self.dma = dma
        self.dsem = None
        self.dcount = 0
        self.prev_slot = None


class Prog:
    def __init__(self, nc):
        self.nc = nc
        self.es = ExitStack()
        self.eh = {"pe": nc.tensor, "act": nc.scalar, "dve": nc.vector, "pool": nc.gpsimd, "sp": nc.sync}
        self.ops = {e: [] for e in self.eh}
        self.sem = {e: self.es.enter_context(nc.semaphore("sem_" + e)) for e in self.eh}
        self.dsems = {q: [self.es.enter_context(nc.semaphore(f"dsem_{q}{i}")) for i in range(NDSEM)]
                      for q in ("sp", "pool", "act")}
        self.dn = {"sp": 0, "pool": 0, "act": 0}
        self.dlast = {q: [None] * NDSEM for q in self.dn}
        self.n_alloc = 0
        self.out_dmas = []

    def sb(self, shape, dtype, name=None):
        self.n_alloc += 1
        name = name or f"sb{self.n_alloc}"
        t = self.es.enter_context(self.nc.sbuf_tensor(f"{name}_{self.n_alloc}", list(shape), dtype))
        return TT(t, Buf(name))

    def ps(self, name=None):
        self.n_alloc += 1
        t = self.es.enter_context(self.nc.psum_tensor(f"ps_{self.n_alloc}", [128, 512], F32))
        return TT(t, Buf(name or f"ps{self.n_alloc}"))

    def dram(self, name, shape, dtype):
        return self.nc.dram_tensor(name, list(shape), dtype, kind="Internal")

    def _reg(self, o, r, w):
        deps = []
        seen = set()

        def add(d):
            if d is o or id(d) in seen:
                return
            seen.add(id(d))
            if (not d.dma) and d.eng == o.eng and not o.dma:
                if o.eng == "pe" or not SAME_ENGINE_SYNC:
                    return
            deps.append(d)
            if not d.dma:
                d.sig = True

        rb = [x.b if isinstance(x, TT) else x for x in r]
        wb = [x.b if isinstance(x, TT) else x for x in w]
        for b in rb:
            for d in b.writers:
                add(d)
            if b.name.startswith("ps"):
                for d in b.readers:
                    if d.eng != o.eng:
                        add(d)
        for b in wb:
            for d in b.writers:
                add(d)
            for d in b.readers:
                add(d)
        o.deps = deps
        for b in wb:
            b.writers = [o]
            b.readers = []
        for b in rb:
            if b in wb:
                continue
            if not o.dma:
                b.readers = [x for x in b.readers if x.dma or x.eng != o.eng]
            b.readers.append(o)
        self.ops[o.eng].append(o)
        return o

    def op(self, eng, fn, r=(), w=()):
        return self._reg(Op(eng, fn), r, w)

    def dma(self, q, out, in_, r=(), w=(), is_out=False):
        o = Op(q, lambda e: e.dma_start(out=out, in_=in_), dma=True)
        n = self.dn[q]
        slot = n % NDSEM
        o.dsem = self.dsems[q][slot]
        o.dcount = 16 * (n // NDSEM + 1)
        o.prev_slot = self.dlast[q][slot]
        self.dlast[q][slot] = o
        self.dn[q] = n + 1
        self._reg(o, r, w)
        if o.prev_slot is not None:
            o.deps.append(o.prev_slot)
        if is_out:
            self.out_dmas.append(o)
        return o

    def emit(self):
        nc = self.nc
        fin = Op("sp", lambda e: e.nop() if hasattr(e, "nop") else None)
        fin.deps = list(self.out_dmas)
        for q in self.dn:
            for o in self.dlast[q]:
                if o is not None and o not in fin.deps:
                    fin.deps.append(o)
        self.ops["sp"].append(fin)
        for e, lst in self.ops.items():
            t = 0
            for o in lst:
                if o.sig and not o.dma:
                    t += 1
                    o.tick = t
        with nc.Block() as block:
            def body(ename):
                def f(eng):
                    waited = {}
                    for o in self.ops[ename]:
                        for d in o.deps:
                            if d.dma:
                                key, sem, val = id(d.dsem), d.dsem, d.dcount
                            else:
                                key, sem, val = d.eng, self.sem[d.eng], d.tick
                            if waited.get(key, 0) < val:
                                eng.wait_ge(sem, val)
                                waited[key] = val
                        if o is fin or o.fn is None:
                            continue
                        ins = o.fn(eng)
                        if o.dma:
                            ins.then_inc(o.dsem, 16)
                        elif o.sig:
                            ins.then_inc(self.sem[ename], 1)
                return f
            block.sync(body("sp"))
            block.tensor(body("pe"))
            block.scalar(body("act"))
            block.vector(body("dve"))
            block.gpsimd(body("pool"))


class TT:
    __slots__ = ("t", "b")

    def __init__(self, t, b):
        self.t = t
        self.b = b

    def __getitem__(self, k):
        return self.t[k]


class Ring:
    def __init__(self, items):
        self.items = items
        self.i = 0

    def next(self):
        x = self.items[self.i % len(self.items)]
        self.i += 1
        return x


def bc(ap, shape):
    return ap.to_broadcast(list(shape))


def group_w(w):
    K, N = w.shape
    gw = min(N, 512)
    G = N // gw
    return np.ascontiguousarray(w.reshape(K // 128, 128, G, gw).transpose(2, 1, 0, 3))


def host_consts(smax):
    c = {}
    c["ident"] = np.eye(128, dtype=np.float32)
    hm = np.zeros((128, 2), np.float32)
    hm[:64, 0] = 1.0
    hm[64:, 1] = 1.0
    c["half_mask"] = hm
    half = 32
    inv = (10000.0 ** (-np.arange(half, dtype=np.float32) / half)).astype(np.float32)
    pos = np.arange(smax, dtype=np.float32)
    ang = (pos[:, None] * inv[None, :]).astype(np.float32)
    cos = np.cos(ang).astype(np.float32).T
    sin = np.sin(ang).astype(np.float32).T
    c["rope_cos"] = np.ascontiguousarray(np.concatenate([cos, cos, cos, cos], 0))
    c["rope_sin"] = np.ascontiguousarray(np.concatenate([-sin, sin, -sin, sin], 0))
    i = np.arange(128, dtype=np.float32)
    c["iota_p"] = np.stack([127.0 - i, i + 1.0], 1).astype(np.float32)
    c["iota_f"] = np.ascontiguousarray(np.stack([np.tile(i + 1.0, (128, 1)), np.tile(127.0 - i, (128, 1))], 1)).astype(np.float32)
    jj, ii = np.meshgrid(i, i, indexing="ij")
    c["ret_pos"] = np.ascontiguousarray(np.stack([np.maximum(ii - jj, 0), np.maximum(jj - ii, 0)], 1)).astype(np.float32)
    c["ret_msk"] = np.ascontiguousarray(np.stack([(ii >= jj), (ii < jj)], 1)).astype(np.float32)
    pp, ff = jj, ii
    c["gdn_cum"] = np.ascontiguousarray(np.stack([(pp <= ff), (pp >= ff), np.ones_like(pp)], 1)).astype(np.float32)
    NEG = -30000.0
    m = np.zeros((128, 2, 2, 128), np.float32)
    m[:, 0, 0, :] = np.where(ff < pp, 0.0, NEG)
    m[:, 0, 1, :] = np.where(ff <= pp, 0.0, NEG)
    m[:, 1, 0, :] = np.where(ff > pp, 0.0, NEG)
    m[:, 1, 1, :] = np.where(ff >= pp, 0.0, NEG)
    c["gdn_msk"] = m
    return c


class Arena:
    def __init__(self, P, nbytes):
        self.P = P
        self.nbytes = nbytes
        self.t = P.es.enter_context(P.nc.sbuf_tensor("arena", [128, nbytes // 2], BF16))
        self.off = 0
        self.n = 0

    def reset(self):
        self.off = 0

    def alloc(self, free_shape, dtype, name="t"):
        esz = 4 if dtype == F32 else 2
        n = int(np.prod(free_shape))
        nb = (n * esz + 63) // 64 * 64
        assert self.off + nb <= self.nbytes, f"arena overflow {name} {self.off}+{nb}"
        ap = self.t[:, self.off // 2:(self.off + n * esz) // 2]
        if dtype == F32:
            ap = ap.bitcast(F32)
        if len(free_shape) == 2:
            ap = ap.rearrange("p (a b) -> p a b", a=free_shape[0])
        elif len(free_shape) == 3:
            ap = ap.rearrange("p (a b c) -> p a b c", a=free_shape[0], b=free_shape[1])
        self.off += nb
        self.n += 1
        return TT(ap, Buf(f"{name}{self.n}"))


def build_program(SS, SP, nlayers=2, dbg=False):
    nc = bass.Bass("TRN2", target_bir_lowering=False)
    P = Prog(nc)
    SMAX = max(SS, SP)

    def din(name, shape, dt=F32):
        return nc.dram_tensor(name, list(shape), dt, kind="ExternalInput")

    I = {}
    I["x_s"] = din("x_s", [SS, D])
    I["x_p"] = din("x_p", [SP, D])
    for nm, shp in (("norm1_w", [2, D]), ("norm2_w", [2, D]), ("final_norm_w", [1, D]),
                    ("w_ev_in", [12, 128, 16, 512]), ("w_ev_out", [4, 128, 16, 512]),
                    ("ret_logit", [1, 16]), ("ret_logit_pair", [2, 2, 4]),
                    ("ret_gn_w", [1, 1024]), ("gm_ln_w", [1, 1024]), ("gm_ln_b", [1, 1024]),
                    ("gm_wsT", [128, 8, 128]), ("gm_bs", [128, 8]),
                    ("w_od_qkv", [16, 128, 16, 512]), ("w_od_z", [8, 128, 16, 512]), ("w_od_g", [1, 128, 16, 128]),
                    ("od_conv", [128, 64, 5]), ("od_a_log", [1, 64]), ("od_dt_bias", [1, 64]), ("od_norm_w", [1, 128]),
                    ("w_od_out", [4, 128, 32, 512]),
                    ("w_gate0", [11, 128, 16, 512]), ("w_up0", [11, 128, 16, 512]), ("w_down0", [4, 128, 44, 512]),
                    ("w_gate1", [11, 128, 16, 512]), ("w_up1", [11, 128, 16, 512]), ("w_down1", [4, 128, 44, 512]),
                    ("ident", [128, 128]), ("half_mask", [128, 2]), ("rope_cos", [128, SMAX]), ("rope_sin", [128, SMAX]),
                    ("iota_p", [128, 2]), ("iota_f", [128, 2, 128]), ("ret_pos", [128, 2, 128]), ("ret_msk", [128, 2, 128]),
                    ("gdn_cum", [128, 3, 128]), ("gdn_msk", [128, 2, 2, 128])):
        I[nm] = din(nm, shp)
    O = {"s": nc.dram_tensor("y_s", [SS, D], F32, kind="ExternalOutput"),
         "p": nc.dram_tensor("y_p", [SP, D], F32, kind="ExternalOutput")}

    WB = {}
    WBUF = {}
    big = ["w_ev_in", "w_ev_out", "w_gate0", "w_up0", "w_down0"]
    if nlayers > 1:
        big += ["w_od_qkv", "w_od_z", "w_od_g", "w_od_out", "w_gate1", "w_up1", "w_down1"]
    if _os.environ.get('K_NOCAST'):
        big = []
    for nm in big:
        shp = list(I[nm].shape)
        WB[nm] = P.dram(nm + "_bf", shp, BF16)
        for g in range(shp[0]):
            b = Buf(f"{nm}_{g}")
            WBUF[(nm, g)] = b
            kc = shp[2]
            for k0 in range(0, kc, 16):
                k1 = min(kc, k0 + 16)
                P.dma("pool", WB[nm][g, :, k0:k1, :], I[nm][g, :, k0:k1, :], w=[b])

    AR = Arena(P, 180 * 1024)
    CA = Arena.__new__(Arena)
    CA.P = P
    CA.nbytes = 24 * 1024
    CA.t = P.es.enter_context(nc.sbuf_tensor("carena", [128, CA.nbytes // 2], BF16))
    CA.off = 0
    CA.n = 0
    PSB = [P.ps() for _ in range(8)]
    PSR = Ring(PSB)

    def cload(name, free_shape, src_ap):
        t = CA.alloc(free_shape, F32, name)
        P.dma("sp", t.t, src_ap, w=[t])
        return t

    identf = cload("identf", [128], I["ident"][:, :])
    identb = CA.alloc([128], BF16, "identb")
    P.op("dve", lambda e: e.tensor_copy(out=identb.t, in_=identf.t), r=[identf], w=[identb])
    epsc = CA.alloc([2], F32, "eps")
    P.op("dve", lambda e: e.memset(epsc.t[:, 0:1], NORM_EPS), w=[epsc])
    P.op("dve", lambda e: e.memset(epsc.t[:, 1:2], 1.0), r=[epsc], w=[epsc])

    def barrier():
        lasts = []
        for e in P.ops:
            for o in reversed(P.ops[e]):
                if not o.dma and o.fn is not None:
                    lasts.append(o)
                    break
        dm = [o for q in P.dn for o in P.dlast[q] if o is not None]
        for e in ("pe", "act", "dve", "pool", "sp"):
            o = Op(e, None)
            for d in lasts:
                if d.eng != e:
                    o.deps.append(d)
                    d.sig = True
            o.deps += dm
            P.ops[e].append(o)

    evac_flip = [0]

    def evac_eng():
        evac_flip[0] ^= 1
        return "act" if evac_flip[0] else "dve"

    def copy_op(eng, out, in_, r, w):
        if eng == "act":
            return P.op("act", lambda e: e.copy(out=out, in_=in_), r=r, w=w)
        return P.op(eng, lambda e: e.tensor_copy(out=out, in_=in_), r=r, w=w)

    def transposes(src_fn, srcbufs, n, dst_fn, dstbufs):
        for k0 in range(0, n, 8):
            nb = min(8, n - k0)
            bank = PSR.next()
            pb = bank.t.bitcast(BF16)
            for k in range(nb):
                P.op("pe", lambda e, k=k, pb=pb, k0=k0: e.transpose(out=pb[:, k * 128:(k + 1) * 128], in_=src_fn(k0 + k), identity=identb.t),
                     r=list(srcbufs) + [identb], w=[bank])
            copy_op(evac_eng(), dst_fn(k0, nb), pb[:, 0:nb * 128].rearrange("p (a b) -> p a b", a=nb), [bank], dstbufs)

    def load_bc(dst, src_row_ap):
        P.dma("sp", dst.t, src_row_ap.partition_broadcast(128), w=[dst])

    def norm_to_fm(src_dram, t0, T, wbc, hnT, xin_ring, xn_ring, junk, stat_ring):
        for j in range(T // 128):
            xin = xin_ring.next()
            xn = xn_ring.next()
            st = stat_ring.next()
            P.dma("sp", xin.t, src_dram[t0 + j * 128:t0 + (j + 1) * 128, :], w=[xin])
            P.op("act", lambda e, xin=xin, st=st: e.activation(out=junk.t, in_=xin.t, func=AF.Square, accum_out=st.t[:, 0:1]),
                 r=[xin], w=[junk, st])
            P.op("act", lambda e, st=st: e.activation(out=st.t[:, 1:2], in_=st.t[:, 0:1], func=AF.Sqrt, bias=epsc.t[:, 0:1], scale=1.0 / D), r=[st, epsc], w=[st])
            P.op("dve", lambda e, st=st: e.reciprocal(out=st.t[:, 2:3], in_=st.t[:, 1:2]), r=[st], w=[st])
            P.op("dve", lambda e, xin=xin, xn=xn, st=st: e.scalar_tensor_tensor(out=xn.t, in0=xin.t, scalar=st.t[:, 2:3], in1=wbc.t,
                                                                              op0=ALU.mult, op1=ALU.mult), r=[xin, st, wbc], w=[xn])
            transposes(lambda k, xn=xn: xn.t[:, k * 128:(k + 1) * 128], [xn], 16,
                       lambda k0, nb, j=j: hnT.t[:, k0:k0 + nb, j * 128:(j + 1) * 128], [hnT])

    def load_w(wring, nm, g, k0=0, k1=None):
        wt = wring.next()
        shp = WB[nm].shape
        k1 = shp[2] if k1 is None else k1
        P.dma("sp", wt.t[:, 0:k1 - k0, 0:shp[3]], WB[nm][g, :, k0:k1, :], r=[WBUF[(nm, g)]], w=[wt])
        return wt

    def mm_tm(bank, actT, abufs, j, wt, kcn, ncols, first=True, last=True, kc_off=0):
        for kc in range(kcn):
            P.op("pe", lambda e, kc=kc: e.matmul(bank.t[:, 0:ncols], lhsT=actT.t[:, kc_off + kc, j * 128:(j + 1) * 128],
                                                 rhs=wt.t[:, kc, 0:ncols], start=(first and kc == 0), stop=(last and kc == kcn - 1)),
                 r=list(abufs) + [wt], w=[bank])

    def mm_fm(bank, actT, abufs, T, wt, c, kcn):
        for kc in range(kcn):
            P.op("pe", lambda e, kc=kc: e.matmul(bank.t[:, 0:T], lhsT=wt.t[:, kc, c * 128:(c + 1) * 128], rhs=actT.t[:, kc, 0:T],
                                                 start=(kc == 0), stop=(kc == kcn - 1)),
                 r=list(abufs) + [wt], w=[bank])

    def gelu_tanh(out_ap, ps_ap, T_, tmp1, tmp2, r, w):
        P.op("act", lambda e: e.activation(out=tmp1.t[:, 0:T_], in_=ps_ap, func=AF.Square), r=r, w=[tmp1])
        P.op("dve", lambda e: e.tensor_scalar(out=tmp1.t[:, 0:T_], in0=tmp1.t[:, 0:T_], scalar1=0.044715, scalar2=1.0, op0=ALU.mult, op1=ALU.add),
             r=[tmp1], w=[tmp1])
        P.op("dve", lambda e: e.tensor_tensor(out=tmp1.t[:, 0:T_], in0=tmp1.t[:, 0:T_], in1=ps_ap, op=ALU.mult), r=[tmp1] + r, w=[tmp1])
        P.op("act", lambda e: e.activation(out=tmp2.t[:, 0:T_], in_=tmp1.t[:, 0:T_], func=AF.Sigmoid, scale=1.5957691216057308), r=[tmp1], w=[tmp2])
        P.op("dve", lambda e: e.tensor_tensor(out=out_ap, in0=tmp2.t[:, 0:T_], in1=ps_ap, op=ALU.mult), r=[tmp2] + r, w=w)

    def ffn_phase(src, dst, S, T, layer, final=False):
        barrier()
        AR.reset()
        wring = Ring([AR.alloc([16, 512], BF16, "w") for _ in range(3)])
        hnT = AR.alloc([16, T], BF16, "hnT")
        aT = AR.alloc([44, T], BF16, "aT")
        xin_ring = Ring([AR.alloc([D], F32, "xin") for _ in range(2)])
        xn_ring = Ring([AR.alloc([D], BF16, "xn") for _ in range(2)])
        junk = AR.alloc([D], BF16, "junk")
        stat_ring = Ring([AR.alloc([4], F32, "st") for _ in range(2)])
        wbc = AR.alloc([D], F32, "wbc")
        load_bc(wbc, I["norm2_w"][layer:layer + 1, :])
        sg_ring = Ring([AR.alloc([512], F32, "sg") for _ in range(2)])
        res_ring = Ring([AR.alloc([512], F32, "res") for _ in range(3)])
        out_ring = Ring([AR.alloc([512], F32, "out") for _ in range(3)])
        NS = T // 128
        for t0 in range(0, S, T):
            norm_to_fm(src, t0, T, wbc, hnT, xin_ring, xn_ring, junk, stat_ring)
            for g in range(11):
                wg = load_w(wring, f"w_gate{layer}", g)
                wu = load_w(wring, f"w_up{layer}", g)
                for c in range(4):
                    bg = PSR.next()
                    bu = PSR.next()
                    mm_fm(bg, hnT, [hnT], T, wg, c, 16)
                    mm_fm(bu, hnT, [hnT], T, wu, c, 16)
                    sg = sg_ring.next()
                    P.op("act", lambda e, bg=bg, sg=sg: e.activation(out=sg.t[:, 0:T], in_=bg.t[:, 0:T], func=AF.Silu), r=[bg], w=[sg])
                    P.op("dve", lambda e, bu=bu, sg=sg, g=g, c=c: e.tensor_tensor(out=aT.t[:, g * 4 + c, :], in0=sg.t[:, 0:T], in1=bu.t[:, 0:T], op=ALU.mult),
                         r=[bu, sg], w=[aT])
            for og in range(4):
                banks = [PSR.next() for _ in range(NS)]
                pieces = [(0, 16), (16, 32), (32, 44)]
                for pi, (k0, k1) in enumerate(pieces):
                    wd = load_w(wring, f"w_down{layer}", og, k0, k1)
                    for j in range(NS):
                        mm_tm(banks[j], aT, [aT], j, wd, k1 - k0, 512, first=(pi == 0), last=(pi == len(pieces) - 1), kc_off=k0)
                for j in range(NS):
                    res = res_ring.next()
                    ot = out_ring.next()
                    rows = slice(t0 + j * 128, t0 + (j + 1) * 128)
                    P.dma("sp", res.t, src[rows, og * 512:(og + 1) * 512], w=[res])
                    P.op("dve", lambda e, b=banks[j], res=res, ot=ot: e.tensor_tensor(out=ot.t, in0=b.t[:, :], in1=res.t, op=ALU.add),
                         r=[banks[j], res], w=[ot])
                    P.dma("pool", dst[rows, og * 512:(og + 1) * 512], ot.t, r=[ot])

    def final_norm_phase(src, dst, S):
        barrier()
        AR.reset()
        wbc = AR.alloc([D], F32, "wbc")
        load_bc(wbc, I["final_norm_w"][0:1, :])
        xin_ring = Ring([AR.alloc([D], F32, "xin") for _ in range(3)])
        out_ring = Ring([AR.alloc([D], F32, "xo") for _ in range(3)])
        junk = AR.alloc([D], BF16, "junk")
        stat_ring = Ring([AR.alloc([4], F32, "st") for _ in range(3)])
        for t0 in range(0, S, 128):
            xin = xin_ring.next()
            xo = out_ring.next()
            st = stat_ring.next()
            P.dma("sp", xin.t, src[t0:t0 + 128, :], w=[xin])
            P.op("act", lambda e, xin=xin, st=st: e.activation(out=junk.t, in_=xin.t, func=AF.Square, accum_out=st.t[:, 0:1]), r=[xin], w=[junk, st])
            P.op("act", lambda e, st=st: e.activation(out=st.t[:, 1:2], in_=st.t[:, 0:1], func=AF.Sqrt, bias=epsc.t[:, 0:1], scale=1.0 / D), r=[st, epsc], w=[st])
            P.op("dve", lambda e, st=st: e.reciprocal(out=st.t[:, 2:3], in_=st.t[:, 1:2]), r=[st], w=[st])
            P.op("dve", lambda e, xin=xin, xo=xo, st=st: e.scalar_tensor_tensor(out=xo.t, in0=xin.t, scalar=st.t[:, 2:3], in1=wbc.t, op0=ALU.mult, op1=ALU.mult),
                 r=[xin, st, wbc], w=[xo])
            P.dma("pool", dst[t0:t0 + 128, :], xo.t, r=[xo], is_out=True)

    PH = {"P": P, "I": I, "O": O, "AR": AR, "CA": CA, "PSR": PSR, "WB": WB, "WBUF": WBUF, "barrier": barrier,
          "transposes": transposes, "load_bc": load_bc, "norm_to_fm": norm_to_fm, "load_w": load_w, "mm_tm": mm_tm, "mm_fm": mm_fm,
          "gelu_tanh": gelu_tanh, "epsc": epsc, "copy_op": copy_op, "evac_eng": evac_eng, "identb": identb, "identf": identf, "nc": nc}

    for tag, S in (("s", SS), ("p", SP)):
        T = min(512, S)
        x = I["x_" + tag]
        mid1 = P.dram(f"mid1_{tag}", [S, D], F32)
        h1 = P.dram(f"h1_{tag}", [S, D], F32)
        if DBG_LEVEL < 4:
            if DBG_LEVEL >= 1:
                layer0_mixer(PH, tag, x, mid1, S, T)
            final_norm_phase(mid1 if DBG_LEVEL == 3 else x, O[tag], S)
            continue
        layer0_mixer(PH, tag, x, mid1, S, T)
        ffn_phase(mid1, h1, S, T, 0)
        if nlayers > 1:
            mid2 = P.dram(f"mid2_{tag}", [S, D], F32)
            h2 = P.dram(f"h2_{tag}", [S, D], F32)
            layer1_mixer(PH, tag, h1, mid2, S, T)
            ffn_phase(mid2, h2, S, T, 1)
            final_norm_phase(h2, O[tag], S)
        else:
            final_norm_phase(h1, O[tag], S)
    P.emit()
    return nc


def layer0_mixer(PH, tag, x, mid1, S, T):
    P, I, AR, CA, PSR, nc = PH["P"], PH["I"], PH["AR"], PH["CA"], PH["PSR"], PH["nc"]
    barrier, transposes, load_bc, norm_to_fm, load_w, mm_tm, mm_fm = (PH[k] for k in ("barrier", "transposes", "load_bc", "norm_to_fm", "load_w", "mm_tm", "mm_fm"))
    gelu_tanh, copy_op, evac_eng, identb = PH["gelu_tanh"], PH["copy_op"], PH["evac_eng"], PH["identb"]
    epsc = PH["epsc"]
    N = S // 128
    NS = T // 128
    qT_d = P.dram(f"qT_{tag}", [4, 128, S], BF16)
    kT_d = P.dram(f"kT_{tag}", [4, 128, S], BF16)
    v_d = P.dram(f"v0_{tag}", [S, 1024], BF16)
    sg_d = P.dram(f"sg_{tag}", [S, 1024], F32)
    gu_d = P.dram(f"gu_{tag}", [S, 1024], F32)
    gv_d = P.dram(f"gv_{tag}", [S, 1024], F32)
    sF_d = P.dram(f"sF_{tag}", [N, 128, 512], BF16)
    sB_d = P.dram(f"sB_{tag}", [N, 128, 512], BF16)
    DB = {k: [Buf(f"{k}{n}") for n in range(N)] for k in ("qT", "kT", "v", "sg", "gu", "gv", "sF", "sB")}

    barrier()
    AR.reset()
    wring = Ring([AR.alloc([16, 512], BF16, "w") for _ in range(4)])
    hnT = AR.alloc([16, T], BF16, "hnT")
    xin_ring = Ring([AR.alloc([D], F32, "xin") for _ in range(2)])
    xn_ring = Ring([AR.alloc([D], BF16, "xn") for _ in range(2)])
    junk = AR.alloc([D], BF16, "junk")
    stat_ring = Ring([AR.alloc([4], F32, "st") for _ in range(2)])
    wbc = AR.alloc([D], F32, "wbc")
    load_bc(wbc, I["norm1_w"][0:1, :])
    gnw = AR.alloc([1024], F32, "gnw")
    load_bc(gnw, I["ret_gn_w"][0:1, :])
    cos_t = AR.alloc([T], F32, "cos")
    sin_t = AR.alloc([T], F32, "sin")
    ta_ring = Ring([AR.alloc([512], F32, "ta") for _ in range(2)])
    tb_ring = Ring([AR.alloc([512], F32, "tb") for _ in range(2)])
    qk_ring = Ring([AR.alloc([4, T], BF16, "qk") for _ in range(2)])
    o16_ring = Ring([AR.alloc([512], BF16, "o16") for _ in range(2)])
    o32_ring = Ring([AR.alloc([512], F32, "o32") for _ in range(3)])
    for t0 in range(0, S, T):
        norm_to_fm(x, t0, T, wbc, hnT, xin_ring, xn_ring, junk, stat_ring)
        P.dma("sp", cos_t.t, I["rope_cos"][:, t0:t0 + T], w=[cos_t])
        P.dma("sp", sin_t.t, I["rope_sin"][:, t0:t0 + T], w=[sin_t])
        for which, dst_d, key in ((0, qT_d, "qT"), (2, kT_d, "kT")):
            w0 = load_w(wring, "w_ev_in", which)
            w1 = load_w(wring, "w_ev_in", which + 1)
            qk = qk_ring.next()
            for c in range(4):
                b0 = PSR.next()
                b1 = PSR.next()
                mm_fm(b0, hnT, [hnT], T, w0, c, 16)
                mm_fm(b1, hnT, [hnT], T, w1, c, 16)
                ta = ta_ring.next()
                tb = tb_ring.next()
                P.op("dve", lambda e, b0=b0, ta=ta: e.tensor_tensor(out=ta.t[:, 0:T], in0=b0.t[:, 0:T], in1=cos_t.t, op=ALU.mult), r=[b0, cos_t], w=[ta])
                P.op("dve", lambda e, b1=b1, tb=tb: e.tensor_tensor(out=tb.t[:, 0:T], in0=b1.t[:, 0:T], in1=sin_t.t, op=ALU.mult), r=[b1, sin_t], w=[tb])
                P.op("pool", lambda e, ta=ta, tb=tb, qk=qk, c=c: e.tensor_tensor(out=qk.t[:, c, :], in0=ta.t[:, 0:T], in1=tb.t[:, 0:T], op=ALU.add), r=[ta, tb], w=[qk])
            P.dma("pool", dst_d[:, :, t0:t0 + T].rearrange("c p t -> p c t"), qk.t, r=[qk], w=[DB[key][(t0 + j * 128) // 128] for j in range(NS)])
        for g in range(4, 12):
            wt = load_w(wring, "w_ev_in", g)
            col = ((g - 4) % 2) * 512
            kind = (g - 4) // 2
            for j in range(NS):
                bank = PSR.next()
                mm_tm(bank, hnT, [hnT], j, wt, 16, 512)
                rows = slice(t0 + j * 128, t0 + (j + 1) * 128)
                n = (t0 + j * 128) // 128
                if kind == 0:
                    o = o16_ring.next()
                    copy_op(evac_eng(), o.t, bank.t[:, :], [bank], [o])
                    P.dma("pool", v_d[rows, col:col + 512], o.t, r=[o], w=[DB["v"][n]])
                elif kind == 1:
                    o = o32_ring.next()
                    ta = ta_ring.next()
                    P.op("act", lambda e, bank=bank, ta=ta: e.activation(out=ta.t, in_=bank.t[:, :], func=AF.Silu), r=[bank], w=[ta])
                    P.op("dve", lambda e, ta=ta, o=o, col=col: e.tensor_tensor(out=o.t, in0=ta.t, in1=gnw.t[:, col:col + 512], op=ALU.mult), r=[ta, gnw], w=[o])
                    P.dma("pool", sg_d[rows, col:col + 512], o.t, r=[o], w=[DB["sg"][n]])
                else:
                    o = o32_ring.next()
                    gelu_tanh(o.t, bank.t[:, :], 512, ta_ring.next(), tb_ring.next(), [bank], [o])
                    P.dma("pool", (gu_d if kind == 2 else gv_d)[rows, col:col + 512], o.t, r=[o], w=[DB["gu" if kind == 2 else "gv"][n]])

    if DBG_LEVEL < 2:
        return
    barrier()
    AR.reset()
    lg = AR.alloc([16], F32, "lg")
    load_bc(lg, I["ret_logit"][0:1, :])
    lgp = AR.alloc([2, 4], F32, "lgp")
    for hh in range(2):
        for dr in range(2):
            P.dma("sp", lgp.t[hh * 64:(hh + 1) * 64, dr, :], I["ret_logit_pair"][dr, hh:hh + 1, :].partition_broadcast(64), r=[lgp], w=[lgp])
    iop = AR.alloc([2], F32, "iop")
    P.dma("sp", iop.t, I["iota_p"][:, :], w=[iop])
    iof = AR.alloc([2, 128], F32, "iof")
    P.dma("sp", iof.t, I["iota_f"][:, :, :], w=[iof])
    rpos = AR.alloc([2, 128], F32, "rpos")
    P.dma("sp", rpos.t, I["ret_pos"][:, :, :], w=[rpos])
    rmsk = AR.alloc([2, 128], F32, "rmsk")
    P.dma("sp", rmsk.t, I["ret_msk"][:, :, :], w=[rmsk])
    for t_ in (lg, lgp):
        P.op("act", lambda e, t_=t_: e.activation(out=t_.t, in_=t_.t, func=AF.Sigmoid), r=[t_], w=[t_])
        P.op("act", lambda e, t_=t_: e.activation(out=t_.t, in_=t_.t, func=AF.Ln), r=[t_], w=[t_])
    wk = CA.alloc([2, 8], F32, "wk") if not hasattr(CA, "l0") else CA.l0["wk"]
    dmat = CA.alloc([8, 128], F32, "dmat") if not hasattr(CA, "l0") else CA.l0["dmat"]
    wq = CA.alloc([2, 4, 128], F32, "wq") if not hasattr(CA, "l0") else CA.l0["wq"]
    wqm = CA.alloc([4, 4, 128], F32, "wqm") if not hasattr(CA, "l0") else CA.l0["wqm"]
    hmask = CA.alloc([2], F32, "hmask") if not hasattr(CA, "l0") else CA.l0["hmask"]
    dec = CA.alloc([2, 4], F32, "dec") if not hasattr(CA, "l0") else CA.l0["dec"]
    wsT = CA.alloc([8, 128], BF16, "wsT") if not hasattr(CA, "l0") else CA.l0["wsT"]
    first = not hasattr(CA, "l0")
    CA.l0 = {"wk": wk, "dmat": dmat, "wq": wq, "dec": dec, "wsT": wsT, "wqm": wqm, "hmask": hmask}
    if first:
        for dr in range(2):
            P.op("dve", lambda e, dr=dr: e.tensor_scalar(out=wk.t[:, dr, :], in0=lg.t[:, dr * 8:(dr + 1) * 8], scalar1=iop.t[:, dr:dr + 1], scalar2=None, op0=ALU.mult), r=[lg, iop], w=[wk])
        P.op("act", lambda e: e.activation(out=wk.t, in_=wk.t, func=AF.Exp), r=[wk], w=[wk])
        tmpd = AR.alloc([2, 128], F32, "tmpd")
        for h in range(8):
            for dr in range(2):
                P.op("act", lambda e, h=h, dr=dr: e.activation(out=tmpd.t[:, dr, :], in_=rpos.t[:, dr, :], func=AF.Exp, scale=lg.t[:, dr * 8 + h:dr * 8 + h + 1]), r=[rpos, lg, tmpd], w=[tmpd])
            P.op("dve", lambda e: e.tensor_tensor(out=tmpd.t, in0=tmpd.t, in1=rmsk.t, op=ALU.mult), r=[tmpd, rmsk], w=[tmpd])
            P.op("dve", lambda e, h=h: e.tensor_tensor(out=dmat.t[:, h, :], in0=tmpd.t[:, 0, :], in1=tmpd.t[:, 1, :], op=ALU.add), r=[tmpd, dmat], w=[dmat])
        P.op("dve", lambda e: e.tensor_scalar(out=dmat.t, in0=dmat.t, scalar1=0.125, scalar2=None, op0=ALU.mult), r=[dmat], w=[dmat])
        for dr in range(2):
            for p_ in range(4):
                P.op("act", lambda e, dr=dr, p_=p_: e.activation(out=wq.t[:, dr, p_, :], in_=iof.t[:, dr, :], func=AF.Exp, scale=lgp.t[:, dr, p_:p_ + 1]), r=[iof, lgp, wq], w=[wq])
        P.op("dve", lambda e: e.tensor_scalar(out=wq.t, in0=wq.t, scalar1=0.125, scalar2=None, op0=ALU.mult), r=[wq], w=[wq])
        P.op("act", lambda e: e.activation(out=dec.t, in_=lgp.t, func=AF.Exp, scale=128.0), r=[lgp], w=[dec])
        P.dma("sp", hmask.t, I["half_mask"][:, :], w=[hmask])
        for dr in range(2):
            for hh in range(2):
                P.op("dve", lambda e, dr=dr, hh=hh: e.tensor_scalar(out=wqm.t[:, dr * 2 + hh, :, :], in0=wq.t[:, dr, :, :], scalar1=hmask.t[:, hh:hh + 1], scalar2=None, op0=ALU.mult), r=[wq, hmask, wqm], w=[wqm])
        wsf = AR.alloc([8, 128], F32, "wsf")
        P.dma("sp", wsf.t, I["gm_wsT"][:, :, :], w=[wsf])
        P.op("dve", lambda e: e.tensor_copy(out=wsT.t, in_=wsf.t), r=[wsf], w=[wsT])

    sF = AR.alloc([4, 128], F32, "sF")
    kT_ring = Ring([AR.alloc([4, 128], BF16, "kTc") for _ in range(2)])
    v_ring = Ring([AR.alloc([1024], BF16, "vc") for _ in range(2)])
    kf_ring = Ring([AR.alloc([512], BF16, "kf") for _ in range(2)])
    sb16_ring = Ring([AR.alloc([4, 128], BF16, "s16") for _ in range(3)])
    for dr in range(2):
        P.op("dve", lambda e: e.memset(sF.t, 0.0), r=[sF], w=[sF])
        order = range(N) if dr == 0 else range(N - 1, -1, -1)
        s_d = sF_d if dr == 0 else sB_d
        skey = "sF" if dr == 0 else "sB"
        for n in order:
            kTc = kT_ring.next()
            vc = v_ring.next()
            P.dma("sp", kTc.t, kT_d[:, :, n * 128:(n + 1) * 128].rearrange("c p t -> p c t"), r=[DB["kT"][n]], w=[kTc])
            P.dma("sp", vc.t, v_d[n * 128:(n + 1) * 128, :], r=[DB["v"][n]], w=[vc])
            s16 = sb16_ring.next()
            copy_op("act", s16.t, sF.t, [sF], [s16])
            P.dma("pool", s_d[n, :, :], s16.t.rearrange("p a b -> p (a b)"), r=[s16], w=[DB[skey][n]])
            bank = PSR.next()
            pb = bank.t.bitcast(BF16)
            for c in range(4):
                P.op("pe", lambda e, c=c, pb=pb, kTc=kTc: e.transpose(out=pb[:, c * 128:(c + 1) * 128], in_=kTc.t[:, c, :], identity=identb.t), r=[kTc, identb], w=[bank])
            kf = kf_ring.next()
            P.op("dve", lambda e, pb=pb, kf=kf, dr=dr: e.tensor_tensor(out=kf.t.rearrange("p (h d) -> p h d", h=8), in0=pb[:, 0:512].rearrange("p (h d) -> p h d", h=8),
                                                                 in1=bc(wk.t[:, dr, :].unsqueeze(2), [128, 8, 64]), op=ALU.mult), r=[bank, wk], w=[kf])
            ub = [PSR.next(), PSR.next()]
            for p_ in range(4):
                P.op("pe", lambda e, p_=p_, kf=kf, vc=vc, ub=ub: e.matmul(ub[p_ // 2].t[:, (p_ % 2) * 256:(p_ % 2 + 1) * 256], lhsT=kf.t[:, p_ * 128:(p_ + 1) * 128],
                                                                      rhs=vc.t[:, p_ * 256:(p_ + 1) * 256], start=True, stop=True), r=[kf, vc], w=[ub[p_ // 2]])
            for p_ in range(4):
                for hh in range(2):
                    rows = slice(hh * 64, (hh + 1) * 64)
                    c0 = (p_ % 2) * 256 + hh * 128
                    P.op("dve", lambda e, p_=p_, rows=rows, c0=c0, ub=ub, dr=dr: e.scalar_tensor_tensor(out=sF.t[rows, p_, :], in0=sF.t[rows, p_, :], scalar=dec.t[rows, dr, p_:p_ + 1],
                                                                                                 in1=ub[p_ // 2].t[rows, c0:c0 + 128], op0=ALU.mult, op1=ALU.add),
                         r=[sF, dec, ub[p_ // 2]], w=[sF])

    if DBG_LEVEL < 3:
        return
    barrier()
    AR.reset()
    wring = Ring([AR.alloc([16, 512], BF16, "w") for _ in range(3)])
    gm_w = AR.alloc([1024], F32, "gmw")
    load_bc(gm_w, I["gm_ln_w"][0:1, :])
    gm_b = AR.alloc([1024], F32, "gmb")
    load_bc(gm_b, I["gm_ln_b"][0:1, :])
    gbs = AR.alloc([8], F32, "gbs")
    P.dma("sp", gbs.t, I["gm_bs"][:, :], w=[gbs])
    ymixT = AR.alloc([16, T], BF16, "ymixT")
    qT_ring = Ring([AR.alloc([4, 128], BF16, "qTc") for _ in range(2)])
    kT_ring = Ring([AR.alloc([4, 128], BF16, "kTc") for _ in range(2)])
    v_ring = Ring([AR.alloc([1024], BF16, "vc") for _ in range(2)])
    sF_ring = Ring([AR.alloc([4, 128], BF16, "sFc") for _ in range(2)])
    sB_ring = Ring([AR.alloc([4, 128], BF16, "sBc") for _ in range(2)])
    sg_ring = Ring([AR.alloc([1024], F32, "sgc") for _ in range(2)])
    gu_ring = Ring([AR.alloc([1024], F32, "guc") for _ in range(2)])
    gv_ring = Ring([AR.alloc([1024], F32, "gvc") for _ in range(2)])
    SM = AR.alloc([8, 128], BF16, "SM")
    qfm = AR.alloc([4, 4, 128], BF16, "qfm")
    kTm = AR.alloc([2, 4, 128], BF16, "kTm")
    sq = AR.alloc([1024], F32, "sq")
    yn = AR.alloc([1024], F32, "yn")
    st8 = AR.alloc([6, 8], F32, "st8")
    ymix = AR.alloc([2048], BF16, "ymix")
    gvn = AR.alloc([1024], BF16, "gvn")
    gtmp = AR.alloc([1024], F32, "gtmp")
    lnst = AR.alloc([8], F32, "lnst")
    res_ring = Ring([AR.alloc([512], F32, "res") for _ in range(3)])
    out_ring = Ring([AR.alloc([512], F32, "out") for _ in range(3)])
    for t0 in range(0, S, T):
        for j in range(NS):
            n = t0 // 128 + j
            cs = slice(n * 128, (n + 1) * 128)
            qTc, kTc, vc, sFc, sBc, sgc, guc, gvc = (r_.next() for r_ in (qT_ring, kT_ring, v_ring, sF_ring, sB_ring, sg_ring, gu_ring, gv_ring))
            P.dma("sp", qTc.t, qT_d[:, :, cs].rearrange("c p t -> p c t"), r=[DB["qT"][n]], w=[qTc])
            P.dma("sp", kTc.t, kT_d[:, :, cs].rearrange("c p t -> p c t"), r=[DB["kT"][n]], w=[kTc])
            P.dma("sp", vc.t, v_d[cs, :], r=[DB["v"][n]], w=[vc])
            P.dma("sp", sFc.t.rearrange("p a b -> p (a b)"), sF_d[n, :, :], r=[DB["sF"][n]], w=[sFc])
            P.dma("sp", sBc.t.rearrange("p a b -> p (a b)"), sB_d[n, :, :], r=[DB["sB"][n]], w=[sBc])
            P.dma("sp", sgc.t, sg_d[cs, :], r=[DB["sg"][n]], w=[sgc])
            P.dma("sp", guc.t, gu_d[cs, :], r=[DB["gu"][n]], w=[guc])
            P.dma("sp", gvc.t, gv_d[cs, :], r=[DB["gv"][n]], w=[gvc])
            if K_SUB <= 1:
                continue
            sb_ = [PSR.next(), PSR.next()]
            for hh in range(2):
                P.op("dve", lambda e, hh=hh, kTc=kTc: e.tensor_scalar(out=kTm.t[:, hh, :, :], in0=kTc.t, scalar1=hmask.t[:, hh:hh + 1], scalar2=None, op0=ALU.mult), r=[kTc, hmask, kTm], w=[kTm])
            for h in range(8):
                P.op("pe", lambda e, h=h, sb_=sb_, qTc=qTc: e.matmul(sb_[h // 4].t[:, (h % 4) * 128:(h % 4 + 1) * 128], lhsT=kTm.t[:, h % 2, h // 2, :], rhs=qTc.t[:, h // 2, :], start=True, stop=True),
                     r=[kTm, qTc], w=[sb_[h // 4]])
            for b_ in range(2):
                P.op("dve", lambda e, b_=b_, sb_=sb_: e.tensor_tensor(out=SM.t[:, b_ * 4:(b_ + 1) * 4, :], in0=sb_[b_].t[:, :].rearrange("p (h i) -> p h i", h=4), in1=dmat.t[:, b_ * 4:(b_ + 1) * 4, :], op=ALU.mult),
                     r=[sb_[b_], dmat], w=[SM])
            for m_ in range(4):
                P.op("dve", lambda e, m_=m_, qTc=qTc: e.tensor_tensor(out=qfm.t[:, m_, :, :], in0=qTc.t, in1=wqm.t[:, m_, :, :], op=ALU.mult), r=[qTc, wqm, qfm], w=[qfm])
            if K_SUB <= 2:
                continue
            yb_ = [PSR.next(), PSR.next()]
            for h in range(8):
                rows = slice((h % 2) * 64, (h % 2 + 1) * 64)
                oc = slice((h % 4) * 128, (h % 4 + 1) * 128)
                P.op("pe", lambda e, h=h, oc=oc, yb_=yb_, vc=vc: e.matmul(yb_[h // 4].t[:, oc], lhsT=SM.t[:, h, :], rhs=vc.t[:, h * 128:(h + 1) * 128], start=True, stop=False), r=[SM, vc], w=[yb_[h // 4]])
                P.op("pe", lambda e, h=h, oc=oc, yb_=yb_, sFc=sFc: e.matmul(yb_[h // 4].t[:, oc], lhsT=qfm.t[:, h % 2, h // 2, :], rhs=sFc.t[:, h // 2, :], start=False, stop=False), r=[qfm, sFc], w=[yb_[h // 4]])
                P.op("pe", lambda e, h=h, oc=oc, yb_=yb_, sBc=sBc: e.matmul(yb_[h // 4].t[:, oc], lhsT=qfm.t[:, 2 + h % 2, h // 2, :], rhs=sBc.t[:, h // 2, :], start=False, stop=True), r=[qfm, sBc], w=[yb_[h // 4]])
            if K_SUB <= 2.5:
                continue
            for b_ in range(2):
                y3 = yb_[b_].t[:, :].rearrange("p (h e) -> p h e", h=4)
                P.op("dve", lambda e, b_=b_, y3=y3: e.tensor_reduce(out=st8.t[:, 0, b_ * 4:(b_ + 1) * 4], in_=y3, axis=AX.X, op=ALU.add), r=[yb_[b_], st8], w=[st8])
                P.op("act", lambda e, b_=b_, yb_=yb_: e.activation(out=sq.t[:, b_ * 512:(b_ + 1) * 512], in_=yb_[b_].t[:, :], func=AF.Square), r=[yb_[b_], sq], w=[sq])
            P.op("dve", lambda e: e.tensor_reduce(out=st8.t[:, 1, :], in_=sq.t.rearrange("p (h e) -> p h e", h=8), axis=AX.X, op=ALU.add), r=[sq, st8], w=[st8])
            if K_SUB <= 2.7:
                continue
            P.op("dve", lambda e: e.tensor_scalar(out=st8.t[:, 2, :], in0=st8.t[:, 0, :], scalar1=1.0 / 128, scalar2=None, op0=ALU.mult), r=[st8], w=[st8])
            P.op("dve", lambda e: e.tensor_tensor(out=st8.t[:, 3, :], in0=st8.t[:, 2, :], in1=st8.t[:, 2, :], op=ALU.mult), r=[st8], w=[st8])
            P.op("dve", lambda e: e.scalar_tensor_tensor(out=st8.t[:, 4, :], in0=st8.t[:, 1, :], scalar=1.0 / 128, in1=st8.t[:, 3, :], op0=ALU.mult, op1=ALU.subtract), r=[st8], w=[st8])
            P.op("act", lambda e: e.activation(out=st8.t[:, 4, :], in_=st8.t[:, 4, :], func=AF.Sqrt, bias=epsc.t[:, 0:1], scale=1.0), r=[st8, epsc], w=[st8])
            P.op("dve", lambda e: e.reciprocal(out=st8.t[:, 4, :], in_=st8.t[:, 4, :]), r=[st8], w=[st8])
            P.op("dve", lambda e: e.scalar_tensor_tensor(out=st8.t[:, 5, :], in0=st8.t[:, 2, :], scalar=-1.0, in1=st8.t[:, 4, :], op0=ALU.mult, op1=ALU.mult), r=[st8], w=[st8])
            for b_ in range(2):
                y3 = yb_[b_].t[:, :].rearrange("p (h e) -> p h e", h=4)
                yn3 = yn.t[:, b_ * 512:(b_ + 1) * 512].rearrange("p (h e) -> p h e", h=4)
                P.op("dve", lambda e, b_=b_, y3=y3, yn3=yn3: e.tensor_tensor(out=yn3, in0=y3, in1=bc(st8.t[:, 4, b_ * 4:(b_ + 1) * 4].unsqueeze(2), [128, 4, 128]), op=ALU.mult), r=[yb_[b_], st8, yn], w=[yn])
                P.op("dve", lambda e, b_=b_, yn3=yn3: e.tensor_tensor(out=yn3, in0=yn3, in1=bc(st8.t[:, 5, b_ * 4:(b_ + 1) * 4].unsqueeze(2), [128, 4, 128]), op=ALU.add), r=[yn, st8], w=[yn])
            P.op("dve", lambda e, sgc=sgc: e.tensor_tensor(out=ymix.t[:, 0:1024], in0=yn.t, in1=sgc.t, op=ALU.mult), r=[yn, sgc], w=[ymix])
            if K_SUB <= 3:
                continue
            P.op("dve", lambda e, gvc=gvc: e.tensor_reduce(out=lnst.t[:, 0:1], in_=gvc.t, axis=AX.X, op=ALU.add), r=[gvc, lnst], w=[lnst])
            P.op("act", lambda e, gvc=gvc: e.activation(out=gtmp.t, in_=gvc.t, func=AF.Square, accum_out=lnst.t[:, 1:2]), r=[gvc, lnst], w=[gtmp, lnst])
            P.op("dve", lambda e: e.tensor_scalar(out=lnst.t[:, 2:3], in0=lnst.t[:, 0:1], scalar1=1.0 / 1024, scalar2=None, op0=ALU.mult), r=[lnst], w=[lnst])
            P.op("dve", lambda e: e.tensor_tensor(out=lnst.t[:, 3:4], in0=lnst.t[:, 2:3], in1=lnst.t[:, 2:3], op=ALU.mult), r=[lnst], w=[lnst])
            P.op("dve", lambda e: e.scalar_tensor_tensor(out=lnst.t[:, 4:5], in0=lnst.t[:, 1:2], scalar=1.0 / 1024, in1=lnst.t[:, 3:4], op0=ALU.mult, op1=ALU.subtract), r=[lnst], w=[lnst])
            P.op("act", lambda e: e.activation(out=lnst.t[:, 4:5], in_=lnst.t[:, 4:5], func=AF.Sqrt, bias=epsc.t[:, 0:1], scale=1.0), r=[lnst, epsc], w=[lnst])
            P.op("dve", lambda e: e.reciprocal(out=lnst.t[:, 4:5], in_=lnst.t[:, 4:5]), r=[lnst], w=[lnst])
            P.op("dve", lambda e: e.scalar_tensor_tensor(out=lnst.t[:, 5:6], in0=lnst.t[:, 2:3], scalar=-1.0, in1=lnst.t[:, 4:5], op0=ALU.mult, op1=ALU.mult), r=[lnst], w=[lnst])
            P.op("dve", lambda e, gvc=gvc: e.tensor_scalar(out=gtmp.t, in0=gvc.t, scalar1=lnst.t[:, 4:5], scalar2=lnst.t[:, 5:6], op0=ALU.mult, op1=ALU.add), r=[gvc, lnst, gtmp], w=[gtmp])
            P.op("dve", lambda e: e.tensor_tensor(out=gtmp.t, in0=gtmp.t, in1=gm_w.t, op=ALU.mult), r=[gtmp, gm_w], w=[gtmp])
            P.op("dve", lambda e: e.tensor_tensor(out=gvn.t, in0=gtmp.t, in1=gm_b.t, op=ALU.add), r=[gtmp, gm_b], w=[gvn])
            mb_ = [PSR.next(), PSR.next()]
            for g in range(8):
                P.op("pe", lambda e, g=g, mb_=mb_: e.matmul(mb_[g // 4].t[:, (g % 4) * 128:(g % 4 + 1) * 128], lhsT=wsT.t[:, g, :], rhs=gvn.t[:, g * 128:(g + 1) * 128], start=True, stop=True), r=[wsT, gvn], w=[mb_[g // 4]])
            for b_ in range(2):
                P.op("dve", lambda e, b_=b_, mb_=mb_: e.tensor_tensor(out=gtmp.t[:, b_ * 512:(b_ + 1) * 512].rearrange("p (g c) -> p g c", g=4), in0=mb_[b_].t[:, :].rearrange("p (g c) -> p g c", g=4),
                                                                in1=bc(gbs.t[:, b_ * 4:(b_ + 1) * 4].unsqueeze(2), [128, 4, 128]), op=ALU.add), r=[mb_[b_], gbs, gtmp], w=[gtmp])
            P.op("dve", lambda e, guc=guc: e.tensor_tensor(out=ymix.t[:, 1024:2048], in0=gtmp.t, in1=guc.t, op=ALU.mult), r=[gtmp, guc, ymix], w=[ymix])
            if K_SUB <= 4:
                continue
            transposes(lambda k: ymix.t[:, k * 128:(k + 1) * 128], [ymix], 16, lambda k0, nb, j=j: ymixT.t[:, k0:k0 + nb, j * 128:(j + 1) * 128], [ymixT])
        if K_SUB <= 5:
            continue
        for og in range(4):
            wt = load_w(wring, "w_ev_out", og)
            for j in range(NS):
                bank = PSR.next()
                mm_tm(bank, ymixT, [ymixT], j, wt, 16, 512)
                res = res_ring.next()
                ot = out_ring.next()
                rows = slice(t0 + j * 128, t0 + (j + 1) * 128)
                P.dma("sp", res.t, x[rows, og * 512:(og + 1) * 512], w=[res])
                P.op("dve", lambda e, bank=bank, res=res, ot=ot: e.tensor_tensor(out=ot.t, in0=bank.t[:, :], in1=res.t, op=ALU.add), r=[bank, res], w=[ot])
                P.dma("pool", mid1[rows, og * 512:(og + 1) * 512], ot.t, r=[ot])


def layer1_mixer(PH, tag, h1, mid2, S, T):
    P, I, AR, CA, PSR, nc = PH["P"], PH["I"], PH["AR"], PH["CA"], PH["PSR"], PH["nc"]
    barrier, transposes, load_bc, norm_to_fm, load_w, mm_tm, mm_fm = (PH[k] for k in ("barrier", "transposes", "load_bc", "norm_to_fm", "load_w", "mm_tm", "mm_fm"))
    copy_op, evac_eng, identb, identf, epsc = PH["copy_op"], PH["evac_eng"], PH["identb"], PH["identf"], PH["epsc"]
    N = S // 128
    NS = T // 128
    raw_d = P.dram(f"raw_{tag}", [64, 128, S], BF16)
    sz_d = P.dram(f"sz_{tag}", [S, 4096], F32)
    G_d = P.dram(f"G_{tag}", [S, 8, 64], F32)
    GT_d = P.dram(f"GT_{tag}", [N, 64, 128], F32)
    qT_d = P.dram(f"q1T_{tag}", [16, 128, S], BF16)
    kT_d = P.dram(f"k1T_{tag}", [16, 128, S], BF16)
    k_d = P.dram(f"k1_{tag}", [S, 2048], BF16)
    v_d = P.dram(f"v1_{tag}", [S, 4096], BF16)
    o_d = [P.dram(f"o{d}_{tag}", [S, 4096], F32) for d in range(2)]
    om_d = P.dram(f"om_{tag}", [S, 4096], BF16)

    barrier()
    AR.reset()
    wring = Ring([AR.alloc([16, 512], BF16, "w") for _ in range(3)])
    hnT = AR.alloc([16, T], BF16, "hnT")
    xin_ring = Ring([AR.alloc([D], F32, "xin") for _ in range(2)])
    xn_ring = Ring([AR.alloc([D], BF16, "xn") for _ in range(2)])
    junk = AR.alloc([D], BF16, "junk")
    stat_ring = Ring([AR.alloc([4], F32, "st") for _ in range(2)])
    wbc = AR.alloc([D], F32, "wbc")
    load_bc(wbc, I["norm1_w"][1:2, :])
    dtb = AR.alloc([64], F32, "dtb")
    load_bc(dtb, I["od_dt_bias"][0:1, :])
    nrate = AR.alloc([64], F32, "nrate")
    load_bc(nrate, I["od_a_log"][0:1, :])
    P.op("act", lambda e: e.activation(out=nrate.t, in_=nrate.t, func=AF.Exp), r=[nrate], w=[nrate])
    P.op("dve", lambda e: e.tensor_scalar(out=nrate.t, in0=nrate.t, scalar1=-1.0, scalar2=None, op0=ALU.mult), r=[nrate], w=[nrate])
    cum = AR.alloc([3, 128], F32, "cum")
    P.dma("sp", cum.t, I["gdn_cum"][:, :, :], w=[cum])
    raw_ring = Ring([AR.alloc([4, T], BF16, "raw") for _ in range(2)])
    o32_ring = Ring([AR.alloc([512], F32, "o32") for _ in range(3)])
    Gt_ring = Ring([AR.alloc([8, 64], F32, "Gt") for _ in range(2)])
    gx = AR.alloc([6, 64], F32, "gx")
    gT_ring = Ring([AR.alloc([128], F32, "gT") for _ in range(2)])
    for t0 in range(0, S, T):
        norm_to_fm(h1, t0, T, wbc, hnT, xin_ring, xn_ring, junk, stat_ring)
        for g in range(16):
            wt = load_w(wring, "w_od_qkv", g)
            raw = raw_ring.next()
            for c in range(4):
                bank = PSR.next()
                mm_fm(bank, hnT, [hnT], T, wt, c, 16)
                copy_op(evac_eng(), raw.t[:, c, :], bank.t[:, 0:T], [bank], [raw])
            P.dma("pool", raw_d[g * 4:(g + 1) * 4, :, t0:t0 + T].rearrange("c p t -> p c t"), raw.t, r=[raw])
        for g in range(8):
            wt = load_w(wring, "w_od_z", g)
            for j in range(NS):
                bank = PSR.next()
                mm_tm(bank, hnT, [hnT], j, wt, 16, 512)
                o = o32_ring.next()
                P.op("act", lambda e, bank=bank, o=o: e.activation(out=o.t, in_=bank.t[:, :], func=AF.Silu), r=[bank], w=[o])
                P.dma("pool", sz_d[t0 + j * 128:t0 + (j + 1) * 128, g * 512:(g + 1) * 512], o.t, r=[o])
        wt = load_w(wring, "w_od_g", 0)
        for j in range(NS):
            n = t0 // 128 + j
            bank = PSR.next()
            mm_tm(bank, hnT, [hnT], j, wt, 16, 128)
            Gt = Gt_ring.next()
            P.op("act", lambda e, bank=bank, Gt=Gt: e.activation(out=Gt.t[:, 0, :], in_=bank.t[:, 0:64], func=AF.Sigmoid), r=[bank], w=[Gt])
            P.op("dve", lambda e, bank=bank: e.tensor_tensor(out=gx.t[:, 0, :], in0=bank.t[:, 64:128], in1=dtb.t, op=ALU.add), r=[bank, dtb, gx], w=[gx])
            P.op("dve", lambda e: e.scalar_tensor_tensor(out=gx.t[:, 1, :], in0=gx.t[:, 0, :], scalar=-1.0, in1=gx.t[:, 0, :], op0=ALU.mult, op1=ALU.max), r=[gx], w=[gx])
            P.op("act", lambda e: e.activation(out=gx.t[:, 1, :], in_=gx.t[:, 1, :], func=AF.Exp, scale=-1.0), r=[gx], w=[gx])
            P.op("act", lambda e: e.activation(out=gx.t[:, 1, :], in_=gx.t[:, 1, :], func=AF.Ln, bias=epsc.t[:, 1:2], scale=1.0), r=[gx, epsc], w=[gx])
            P.op("dve", lambda e: e.scalar_tensor_tensor(out=gx.t[:, 2, :], in0=gx.t[:, 0, :], scalar=0.0, in1=gx.t[:, 1, :], op0=ALU.max, op1=ALU.add), r=[gx], w=[gx])
            P.op("dve", lambda e: e.tensor_tensor(out=gx.t[:, 3, :], in0=gx.t[:, 2, :], in1=nrate.t, op=ALU.mult), r=[gx, nrate], w=[gx])
            P.op("act", lambda e, Gt=Gt: e.activation(out=gx.t[:, 4, :], in_=Gt.t[:, 0, :], func=AF.Ln), r=[Gt, gx], w=[gx])
            cb = PSR.next()
            cb2 = PSR.next()
            for dr in range(2):
                hs = slice(dr * 32, (dr + 1) * 32)
                P.op("pe", lambda e, dr=dr, hs=hs, cb=cb: e.matmul(cb.t[:, dr * 32:(dr + 1) * 32], lhsT=cum.t[:, dr, :], rhs=gx.t[:, 3, hs], start=True, stop=True), r=[cum, gx], w=[cb])
                P.op("pe", lambda e, dr=dr, hs=hs, cb=cb: e.matmul(cb.t[:, 64 + dr * 32:64 + (dr + 1) * 32], lhsT=cum.t[:, 2, :], rhs=gx.t[:, 3, hs], start=True, stop=True), r=[cum, gx], w=[cb])
                P.op("pe", lambda e, dr=dr, hs=hs, cb2=cb2: e.matmul(cb2.t[0:32, dr * 128:(dr + 1) * 128], lhsT=gx.t[:, 3, hs], rhs=cum.t[:, dr, :], start=True, stop=True), r=[cum, gx], w=[cb2])
            gT = gT_ring.next()
            for dr in range(2):
                P.op("dve", lambda e, dr=dr, cb2=cb2, gT=gT: e.tensor_copy(out=gT.t[0:32, :], in_=cb2.t[0:32, dr * 128:(dr + 1) * 128]), r=[cb2, gT], w=[gT])
                P.dma("pool", GT_d[n, dr * 32:(dr + 1) * 32, :], gT.t[0:32, :], r=[gT])
                gT = gT_ring.next() if dr == 0 else gT
            P.op("dve", lambda e, cb=cb, Gt=Gt: e.tensor_copy(out=Gt.t[:, 1, :], in_=cb.t[:, 0:64]), r=[cb, Gt], w=[Gt])
            P.op("dve", lambda e, Gt=Gt: e.tensor_tensor(out=Gt.t[:, 2, :], in0=Gt.t[:, 1, :], in1=gx.t[:, 4, :], op=ALU.add), r=[Gt, gx], w=[Gt])
            P.op("act", lambda e, Gt=Gt: e.activation(out=Gt.t[:, 4, :], in_=Gt.t[:, 1, :], func=AF.Exp), r=[Gt], w=[Gt])
            P.op("dve", lambda e, Gt=Gt: e.scalar_tensor_tensor(out=Gt.t[:, 3, :], in0=Gt.t[:, 0, :], scalar=-1.0, in1=Gt.t[:, 4, :], op0=ALU.mult, op1=ALU.mult), r=[Gt], w=[Gt])
            P.op("dve", lambda e, cb=cb, Gt=Gt: e.tensor_tensor(out=Gt.t[:, 5, :], in0=cb.t[:, 64:128], in1=Gt.t[:, 1, :], op=ALU.subtract), r=[cb, Gt], w=[Gt])
            P.op("act", lambda e, Gt=Gt: e.activation(out=Gt.t[:, 5, :], in_=Gt.t[:, 5, :], func=AF.Exp), r=[Gt], w=[Gt])
            P.op("act", lambda e, cb=cb, Gt=Gt: e.activation(out=Gt.t[:, 6, :], in_=cb.t[:, 64:128], func=AF.Exp), r=[cb, Gt], w=[Gt])
            P.op("dve", lambda e, Gt=Gt: e.tensor_scalar(out=Gt.t[:, 7, :], in0=Gt.t[:, 1, :], scalar1=-1.0, scalar2=None, op0=ALU.mult), r=[Gt], w=[Gt])
            P.dma("pool", G_d[n * 128:(n + 1) * 128, :, :], Gt.t, r=[Gt])

    barrier()
    AR.reset()
    cw = AR.alloc([64, 5], F32, "cw")
    P.dma("sp", cw.t, I["od_conv"][:, :, :], w=[cw])
    onesb = AR.alloc([128], BF16, "ones")
    P.op("dve", lambda e: e.memset(onesb.t, 1.0), w=[onesb])
    l2e = AR.alloc([1], F32, "l2e")
    P.op("dve", lambda e: e.memset(l2e.t, L2_EPS), w=[l2e])
    diag_ring = Ring([AR.alloc([5, 128], BF16, "diag") for _ in range(2)])
    win_ring = Ring([AR.alloc([T + 4], BF16, "win") for _ in range(3)])
    xs_ring = Ring([AR.alloc([T], F32, "xs") for _ in range(2)])
    sq_ring = Ring([AR.alloc([T], BF16, "sq") for _ in range(2)])
    rn_ring = Ring([AR.alloc([T], F32, "rn") for _ in range(2)])
    xb_ring = Ring([AR.alloc([T], BF16, "xb") for _ in range(3)])
    tm_ring = Ring([AR.alloc([NS, 128], BF16, "tm") for _ in range(3)])
    for cc in range(64):
        dg = diag_ring.next()
        for w_ in range(5):
            P.op("dve", lambda e, w_=w_, dg=dg, cc=cc: e.tensor_scalar(out=dg.t[:, w_, :], in0=identf.t, scalar1=cw.t[:, cc, w_:w_ + 1], scalar2=None, op0=ALU.mult), r=[identf, cw, dg], w=[dg])
        for t0 in range(0, S, T):
            win = win_ring.next()
            lo, hi = max(t0 - 2, 0), min(t0 + T + 2, S)
            if lo != t0 - 2 or hi != t0 + T + 2:
                P.op("dve", lambda e, win=win: e.memset(win.t, 0.0), w=[win])
            P.dma("sp", win.t[:, lo - (t0 - 2):hi - (t0 - 2)], raw_d[cc, :, lo:hi], r=[win], w=[win])
            bank = PSR.next()
            for w_ in range(5):
                P.op("pe", lambda e, w_=w_, dg=dg, win=win, bank=bank: e.matmul(bank.t[:, 0:T], lhsT=dg.t[:, w_, :], rhs=win.t[:, w_:w_ + T], start=(w_ == 0), stop=(w_ == 4)), r=[dg, win], w=[bank])
            xb = xb_ring.next()
            if cc < 32:
                xs = xs_ring.next()
                sq = sq_ring.next()
                rn = rn_ring.next()
                P.op("act", lambda e, bank=bank, xs=xs: e.activation(out=xs.t, in_=bank.t[:, 0:T], func=AF.Silu), r=[bank], w=[xs])
                P.op("dve", lambda e, xs=xs, sq=sq: e.tensor_tensor(out=sq.t, in0=xs.t, in1=xs.t, op=ALU.mult), r=[xs], w=[sq])
                b2 = PSR.next()
                P.op("pe", lambda e, sq=sq, b2=b2: e.matmul(b2.t[:, 0:T], lhsT=onesb.t, rhs=sq.t, start=True, stop=True), r=[onesb, sq], w=[b2])
                P.op("act", lambda e, b2=b2, rn=rn: e.activation(out=rn.t, in_=b2.t[:, 0:T], func=AF.Sqrt, bias=l2e.t[:, 0:1], scale=1.0), r=[b2, l2e], w=[rn])
                P.op("dve", lambda e, rn=rn: e.reciprocal(out=rn.t, in_=rn.t), r=[rn], w=[rn])
                sc = 128 ** -0.5 if cc < 16 else 1.0
                P.op("dve", lambda e, xs=xs, rn=rn, xb=xb, sc=sc: e.scalar_tensor_tensor(out=xb.t, in0=xs.t, scalar=sc, in1=rn.t, op0=ALU.mult, op1=ALU.mult), r=[xs, rn], w=[xb])
                dst = qT_d if cc < 16 else kT_d
                P.dma("pool", dst[cc % 16, :, t0:t0 + T], xb.t, r=[xb])
            else:
                P.op("act", lambda e, bank=bank, xb=xb: e.activation(out=xb.t, in_=bank.t[:, 0:T], func=AF.Silu), r=[bank], w=[xb])
            if cc >= 16:
                tm = tm_ring.next()
                transposes(lambda k, xb=xb: xb.t[:, k * 128:(k + 1) * 128], [xb], NS, lambda k0, nb, tm=tm: tm.t[:, k0:k0 + nb, :], [tm])
                dd, c0 = (k_d, (cc - 16) * 128) if cc < 32 else (v_d, (cc - 32) * 128)
                P.dma("pool", dd[t0:t0 + T, c0:c0 + 128].rearrange("(j p) e -> p j e", p=128), tm.t, r=[tm])

    barrier()
    AR.reset()
    msk = AR.alloc([2, 2, 128], F32, "msk")
    P.dma("sp", msk.t, I["gdn_msk"][:, :, :, :], w=[msk])
    Sst = AR.alloc([32, 128], F32, "S")
    Sb = AR.alloc([32, 128], BF16, "Sb")
    GB = AR.alloc([32, 128], F32, "GB")
    bv_ring = Ring([AR.alloc([4, 128], F32, "bv") for _ in range(4)])
    osb_ring = Ring([AR.alloc([4, 128], F32, "osb") for _ in range(4)])
    kT_ring = Ring([AR.alloc([16, 128], BF16, "kT") for _ in range(1)])
    qT_ring = Ring([AR.alloc([16, 128], BF16, "qT") for _ in range(1)])
    ktm_ring = Ring([AR.alloc([16, 128], BF16, "ktm") for _ in range(1)])
    v_ring = Ring([AR.alloc([32, 128], BF16, "v") for _ in range(1)])
    Gt_ring = Ring([AR.alloc([8, 64], F32, "Gt") for _ in range(1)])
    X_ring = Ring([AR.alloc([4, 128], F32, "X") for _ in range(4)])
    h16 = lambda nm, n: Ring([AR.alloc([4, 128], BF16, nm) for _ in range(n)])
    h32 = lambda nm, n: Ring([AR.alloc([4, 128], F32, nm) for _ in range(n)])
    A_ring, QK_ring, P_ring, Q_ring, Z_ring = h32("A", 4), h16("QK", 4), h32("P", 5), h32("Q", 5), h32("Z", 5)
    B_ring, QKT_ring = h32("B", 4), h16("QKT", 4)
    r_ring, vn_ring, kd_ring = h32("r", 4), h16("vn", 4), h16("kd", 4)
    t32_ring = Ring([AR.alloc([4, 128], F32, "t32") for _ in range(4)])
    b4 = lambda ap: bc(ap.unsqueeze(2), [128, 4, 128])
    for dr in range(2):
        P.op("dve", lambda e: e.memset(Sst.t, 0.0), r=[Sst], w=[Sst])
        P.op("dve", lambda e: e.memset(Sb.t, 0.0), r=[Sb], w=[Sb])
        order = range(N) if dr == 0 else range(N - 1, -1, -1)
        for n in order:
            cs = slice(n * 128, (n + 1) * 128)
            kT, qT, ktm, vv, Gt = kT_ring.next(), qT_ring.next(), ktm_ring.next(), v_ring.next(), Gt_ring.next()
            P.dma("sp", kT.t, kT_d[:, :, cs].rearrange("c p t -> p c t"), w=[kT])
            P.dma("sp", qT.t, qT_d[:, :, cs].rearrange("c p t -> p c t"), w=[qT])
            P.dma("sp", ktm.t.rearrange("p a b -> p (a b)"), k_d[cs, :], w=[ktm])
            P.dma("sp", vv.t.rearrange("p a b -> p (a b)"), v_d[cs, :], w=[vv])
            P.dma("sp", Gt.t, G_d[cs, :, :], w=[Gt])
            P.dma("sp", GB.t.rearrange("p a b -> p (a b)"), GT_d[n:n + 1, dr * 32:(dr + 1) * 32, :].rearrange("o h j -> o (h j)").partition_broadcast(128), w=[GB])
            G = lambda f, hs, Gt=Gt, dr=dr: Gt.t[:, f, dr * 32 + hs.start:dr * 32 + hs.stop]
            allh = slice(0, 32)
            def group_body(hg, kT=kT, qT=qT, ktm=ktm, vv=vv, Gt=Gt, G=G, dr=dr, cs=cs):
                hs = slice(hg * 4, hg * 4 + 4)
                bv, osb = bv_ring.next(), osb_ring.next()
                P.op("dve", lambda e: e.tensor_tensor(out=bv.t, in0=vv.t[:, hs, :], in1=b4(G(0, hs)), op=ALU.mult), r=[vv, Gt], w=[bv])
                XA, XE = X_ring.next(), X_ring.next()
                P.op("dve", lambda e, XA=XA, hs=hs, G=G: e.scalar_tensor_tensor(out=XA.t, in0=GB.t[:, hs, :], scalar=-1.0, in1=b4(G(2, hs)), op0=ALU.mult, op1=ALU.add), r=[GB, Gt], w=[XA])
                P.op("pool", lambda e, XA=XA, dr=dr: e.tensor_tensor(out=XA.t, in0=XA.t, in1=bc(msk.t[:, dr, 0:1, :], [128, 4, 128]), op=ALU.add), r=[XA, msk], w=[XA])
                P.op("act", lambda e, XA=XA: e.activation(out=XA.t, in_=XA.t, func=AF.Exp), r=[XA], w=[XA])
                P.op("dve", lambda e, XE=XE, hs=hs, G=G: e.scalar_tensor_tensor(out=XE.t, in0=GB.t[:, hs, :], scalar=-1.0, in1=b4(G(1, hs)), op0=ALU.mult, op1=ALU.add), r=[GB, Gt], w=[XE])
                P.op("pool", lambda e, XE=XE, dr=dr: e.tensor_tensor(out=XE.t, in0=XE.t, in1=bc(msk.t[:, dr, 1:2, :], [128, 4, 128]), op=ALU.add), r=[XE, msk], w=[XE])
                P.op("act", lambda e, XE=XE: e.activation(out=XE.t, in_=XE.t, func=AF.Exp), r=[XE], w=[XE])
                kb = PSR.next()
                for q_ in range(2):
                    kh = hg * 2 + q_
                    P.op("pe", lambda e, q_=q_, kh=kh, kb=kb, kT=kT: e.matmul(kb.t[:, q_ * 128:(q_ + 1) * 128], lhsT=kT.t[:, kh, :], rhs=kT.t[:, kh, :], start=True, stop=True), r=[kT], w=[kb])
                    P.op("pe", lambda e, q_=q_, kh=kh, kb=kb, kT=kT, qT=qT: e.matmul(kb.t[:, 256 + q_ * 128:256 + (q_ + 1) * 128], lhsT=qT.t[:, kh, :], rhs=kT.t[:, kh, :], start=True, stop=True), r=[kT, qT], w=[kb])
                A, QK = A_ring.next(), QK_ring.next()
                k2 = lambda ap: bc(ap.rearrange("p (k j) -> p k j", k=2).unsqueeze(2), [128, 2, 2, 128])
                v4 = lambda t_: t_.t.rearrange("p (k r) j -> p k r j", k=2)
                P.op("dve", lambda e, A=A, XA=XA, kb=kb: e.tensor_tensor(out=v4(A), in0=k2(kb.t[:, 0:256]), in1=v4(XA), op=ALU.mult), r=[kb, XA], w=[A])
                P.op("dve", lambda e, QK=QK, XE=XE, kb=kb: e.tensor_tensor(out=v4(QK), in0=k2(kb.t[:, 256:512]), in1=v4(XE), op=ALU.mult), r=[kb, XE], w=[QK])
                yield
                Bm, QKT = B_ring.next(), QKT_ring.next()
                tb_ = PSR.next()
                for k in range(4):
                    P.op("pe", lambda e, k=k, tb_=tb_, A=A: e.transpose(out=tb_.t[:, k * 128:(k + 1) * 128], in_=A.t[:, k, :], identity=identf.t), r=[A, identf], w=[tb_])
                copy_op("act", Bm.t, tb_.t[:, :].rearrange("p (h j) -> p h j", h=4), [tb_], [Bm])
                transposes(lambda k, QK=QK: QK.t[:, k, :], [QK], 4, lambda k0, nb, QKT=QKT: QKT.t[:, k0:k0 + nb, :], [QKT])
                Z = Z_ring.next()
                P.op("dve", lambda e, Z=Z, Bm=Bm: e.tensor_tensor(out=Z.t, in0=bc(identf.t.unsqueeze(1), [128, 4, 128]), in1=Bm.t, op=ALU.subtract), r=[identf, Bm], w=[Z])
                BQ = Bm
                Pm, Qm = A, None
                Qap = lambda hh, Bm=Bm: Bm.t[:, hh, :]
                for lvl in range(6):
                    yield
                    pb_, Pn = PSR.next(), P_ring.next()
                    for hh in range(4):
                        P.op("pe", lambda e, hh=hh, pb_=pb_, Pm=Pm, Qap=Qap: e.matmul(pb_.t[:, hh * 128:(hh + 1) * 128], lhsT=Qap(hh), rhs=Pm.t[:, hh, :], start=True, stop=True), r=[Pm, BQ] + ([Qm] if Qm else []), w=[pb_])
                    copy_op("act", Pn.t, pb_.t[:, :].rearrange("p (h j) -> p h j", h=4), [pb_], [Pn])
                    if lvl < 5:
                        qb_, Qn = PSR.next(), Q_ring.next()
                        for hh in range(4):
                            P.op("pe", lambda e, hh=hh, qb_=qb_, Pm=Pm, Qap=Qap: e.matmul(qb_.t[:, hh * 128:(hh + 1) * 128], lhsT=Pm.t[:, hh, :], rhs=Qap(hh), start=True, stop=True), r=[Pm, BQ] + ([Qm] if Qm else []), w=[qb_])
                        copy_op("dve", Qn.t, qb_.t[:, :].rearrange("p (h j) -> p h j", h=4), [qb_], [Qn])
                    zb_, Zn = PSR.next(), Z_ring.next()
                    for hh in range(4):
                        P.op("pe", lambda e, hh=hh, zb_=zb_, Pn=Pn, Z=Z: e.matmul(zb_.t[:, hh * 128:(hh + 1) * 128], lhsT=Pn.t[:, hh, :], rhs=Z.t[:, hh, :], start=True, stop=True), r=[Pn, Z], w=[zb_])
                    P.op("dve", lambda e, zb_=zb_, Zn=Zn, Z=Z: e.tensor_tensor(out=Zn.t, in0=zb_.t[:, :].rearrange("p (h j) -> p h j", h=4), in1=Z.t, op=ALU.add), r=[zb_, Z], w=[Zn])
                    Z, Pm = Zn, Pn
                    if lvl < 5:
                        Qm = Qn
                        Qap = lambda hh, Qn=Qn: Qn.t[:, hh, :]
                kd = kd_ring.next()
                P.op("pool", lambda e, kd=kd, ktm=ktm, hs=hs, G=G, hg=hg: e.tensor_tensor(out=v4(kd), in0=bc(ktm.t[:, hg * 2:hg * 2 + 2, :].unsqueeze(2), [128, 2, 2, 128]),
                                                                                 in1=bc(G(5, hs).rearrange("p (k r) -> p k r", k=2).unsqueeze(3), [128, 2, 2, 128]), op=ALU.mult), r=[ktm, Gt], w=[kd])
                yield
                ab = PSR.next()
                for hh in range(4):
                    h = hg * 4 + hh
                    P.op("pe", lambda e, hh=hh, h=h, ab=ab, kT=kT: e.matmul(ab.t[:, hh * 128:(hh + 1) * 128], lhsT=kT.t[:, h // 2, :], rhs=Sb.t[:, h, :], start=True, stop=True), r=[kT, Sb], w=[ab])
                t32 = t32_ring.next()
                r_ = r_ring.next()
                P.op("dve", lambda e, ab=ab, t32=t32, hs=hs, G=G: e.tensor_tensor(out=t32.t, in0=ab.t[:, :].rearrange("p (h j) -> p h j", h=4), in1=b4(G(3, hs)), op=ALU.mult), r=[ab, Gt], w=[t32])
                P.op("dve", lambda e, t32=t32, r_=r_, hs=hs: e.tensor_tensor(out=r_.t, in0=t32.t, in1=bv.t, op=ALU.add), r=[t32, bv], w=[r_])
                yield
                cb_ = PSR.next()
                for hh in range(4):
                    P.op("pe", lambda e, hh=hh, cb_=cb_, Z=Z, r_=r_: e.matmul(cb_.t[:, hh * 128:(hh + 1) * 128], lhsT=Z.t[:, hh, :], rhs=r_.t[:, hh, :], start=True, stop=True), r=[Z, r_], w=[cb_])
                vn = vn_ring.next()
                copy_op("act", vn.t, cb_.t[:, :].rearrange("p (h j) -> p h j", h=4), [cb_], [vn])
                yield
                d1, d2 = PSR.next(), PSR.next()
                for hh in range(4):
                    h = hg * 4 + hh
                    P.op("pe", lambda e, hh=hh, h=h, d1=d1, qT=qT: e.matmul(d1.t[:, hh * 128:(hh + 1) * 128], lhsT=qT.t[:, h // 2, :], rhs=Sb.t[:, h, :], start=True, stop=True), r=[qT, Sb], w=[d1])
                    P.op("pe", lambda e, hh=hh, d2=d2, QKT=QKT, vn=vn: e.matmul(d2.t[:, hh * 128:(hh + 1) * 128], lhsT=QKT.t[:, hh, :], rhs=vn.t[:, hh, :], start=True, stop=True), r=[QKT, vn], w=[d2])
                t32b = t32_ring.next()
                P.op("dve", lambda e, d1=d1, t32b=t32b, hs=hs, G=G: e.tensor_tensor(out=t32b.t, in0=d1.t[:, :].rearrange("p (h j) -> p h j", h=4), in1=b4(G(4, hs)), op=ALU.mult), r=[d1, Gt], w=[t32b])
                P.op("dve", lambda e, d2=d2, t32b=t32b, hs=hs: e.tensor_tensor(out=osb.t, in0=d2.t[:, :].rearrange("p (h j) -> p h j", h=4), in1=t32b.t, op=ALU.add), r=[d2, t32b], w=[osb])
                yield
                eb = PSR.next()
                for hh in range(4):
                    P.op("pe", lambda e, hh=hh, eb=eb, kd=kd, vn=vn: e.matmul(eb.t[:, hh * 128:(hh + 1) * 128], lhsT=kd.t[:, hh, :], rhs=vn.t[:, hh, :], start=True, stop=True), r=[kd, vn], w=[eb])
                P.op("pool", lambda e, hs=hs, G=G: e.tensor_tensor(out=Sst.t[:, hs, :], in0=Sst.t[:, hs, :], in1=b4(G(6, hs)), op=ALU.mult), r=[Sst, Gt], w=[Sst])
                P.op("dve", lambda e, hs=hs, eb=eb: e.tensor_tensor(out=Sst.t[:, hs, :], in0=Sst.t[:, hs, :], in1=eb.t[:, :].rearrange("p (h j) -> p h j", h=4), op=ALU.add), r=[Sst, eb], w=[Sst])
                copy_op("act", Sb.t[:, hs, :], Sst.t[:, hs, :], [Sst, Sb], [Sb])
                P.dma("pool", o_d[dr][cs, hg * 512:(hg + 1) * 512], osb.t.rearrange("p a b -> p (a b)"), r=[osb])
            for pair in range(0, 8, 2):
                alive = [group_body(pair), group_body(pair + 1)]
                while alive:
                    for g_ in list(alive):
                        try:
                            next(g_)
                        except StopIteration:
                            alive.remove(g_)

    barrier()
    AR.reset()
    nw = AR.alloc([128], F32, "nw")
    load_bc(nw, I["od_norm_w"][0:1, :])
    oa_ring = Ring([AR.alloc([32, 128], F32, "oa") for _ in range(2)])
    ob_ring = Ring([AR.alloc([32, 128], F32, "ob") for _ in range(2)])
    sz_ring = Ring([AR.alloc([32, 128], F32, "sz") for _ in range(2)])
    sqt = AR.alloc([32, 128], F32, "sqt")
    om_ring = Ring([AR.alloc([32, 128], BF16, "om") for _ in range(2)])
    rs_ring = Ring([AR.alloc([32], F32, "rs") for _ in range(2)])
    for n in range(N):
        cs = slice(n * 128, (n + 1) * 128)
        oa, ob, sz, om, rs = oa_ring.next(), ob_ring.next(), sz_ring.next(), om_ring.next(), rs_ring.next()
        P.dma("sp", oa.t.rearrange("p a b -> p (a b)"), o_d[0][cs, :], w=[oa])
        P.dma("sp", ob.t.rearrange("p a b -> p (a b)"), o_d[1][cs, :], w=[ob])
        P.dma("sp", sz.t.rearrange("p a b -> p (a b)"), sz_d[cs, :], w=[sz])
        P.op("dve", lambda e, oa=oa, ob=ob: e.tensor_tensor(out=oa.t, in0=oa.t, in1=ob.t, op=ALU.add), r=[oa, ob], w=[oa])
        P.op("act", lambda e, oa=oa: e.activation(out=sqt.t, in_=oa.t, func=AF.Square), r=[oa, sqt], w=[sqt])
        P.op("dve", lambda e, rs=rs: e.tensor_reduce(out=rs.t, in_=sqt.t, axis=AX.X, op=ALU.add), r=[sqt], w=[rs])
        P.op("act", lambda e, rs=rs: e.activation(out=rs.t, in_=rs.t, func=AF.Sqrt, bias=epsc.t[:, 0:1], scale=1.0 / 128), r=[rs, epsc], w=[rs])
        P.op("dve", lambda e, rs=rs: e.reciprocal(out=rs.t, in_=rs.t), r=[rs], w=[rs])
        P.op("dve", lambda e, oa=oa, rs=rs: e.tensor_tensor(out=oa.t, in0=oa.t, in1=bc(rs.t.unsqueeze(2), [128, 32, 128]), op=ALU.mult), r=[oa, rs], w=[oa])
        P.op("pool", lambda e, oa=oa: e.tensor_tensor(out=oa.t, in0=oa.t, in1=bc(nw.t.unsqueeze(1), [128, 32, 128]), op=ALU.mult), r=[oa, nw], w=[oa])
        P.op("dve", lambda e, oa=oa, sz=sz, om=om: e.tensor_tensor(out=om.t, in0=oa.t, in1=sz.t, op=ALU.mult), r=[oa, sz], w=[om])
        P.dma("pool", om_d[cs, :], om.t.rearrange("p a b -> p (a b)"), r=[om])

    barrier()
    AR.reset()
    wring = Ring([AR.alloc([16, 512], BF16, "w") for _ in range(3)])
    omT = AR.alloc([32, T], BF16, "omT")
    om_ring = Ring([AR.alloc([4096], BF16, "om") for _ in range(2)])
    res_ring = Ring([AR.alloc([512], F32, "res") for _ in range(3)])
    out_ring = Ring([AR.alloc([512], F32, "out") for _ in range(3)])
    for t0 in range(0, S, T):
        for j in range(NS):
            om = om_ring.next()
            P.dma("sp", om.t, om_d[t0 + j * 128:t0 + (j + 1) * 128, :], w=[om])
            transposes(lambda k, om=om: om.t[:, k * 128:(k + 1) * 128], [om], 32, lambda k0, nb, j=j: omT.t[:, k0:k0 + nb, j * 128:(j + 1) * 128], [omT])
        for og in range(4):
            banks = [PSR.next() for _ in range(NS)]
            for pi, (k0, k1) in enumerate(((0, 16), (16, 32))):
                wd = load_w(wring, "w_od_out", og, k0, k1)
                for j in range(NS):
                    mm_tm(banks[j], omT, [omT], j, wd, 16, 512, first=(pi == 0), last=(pi == 1), kc_off=k0)
            for j in range(NS):
                res, ot = res_ring.next(), out_ring.next()
                rows = slice(t0 + j * 128, t0 + (j + 1) * 128)
                P.dma("sp", res.t, h1[rows, og * 512:(og + 1) * 512], w=[res])
                P.op("dve", lambda e, b=banks[j], res=res, ot=ot: e.tensor_tensor(out=ot.t, in0=b.t[:, :], in1=res.t, op=ALU.add), r=[banks[j], res], w=[ot])
                P.dma("pool", mid2[rows, og * 512:(og + 1) * 512], ot.t, r=[ot])


def prep_inputs(inp, SS, SP, ncores=8):
    f = lambda a: np.ascontiguousarray(np.asarray(a, dtype=np.float32))
    sh = {}
    sh["norm1_w"] = f(inp["norm1_w"])
    sh["norm2_w"] = f(inp["norm2_w"])
    sh["final_norm_w"] = f(inp["final_norm_w"]).reshape(1, D)
    w = f(inp["ev_w_in"])[0]
    perm = np.array([h * 64 + (d + 32) % 64 for h in range(8) for d in range(64)])
    wq, wk = w[:, 0:512], w[:, 512:1024]
    w2 = np.concatenate([wq, wq[:, perm], wk, wk[:, perm], w[:, 1024:]], axis=1)
    sh["w_ev_in"] = group_w(w2)
    sh["w_ev_out"] = group_w(f(inp["ev_w_out"])[0])
    lg = f(inp["ev_ret_decay_logit"])[0]
    sh["ret_logit"] = lg.reshape(1, 16)
    sh["ret_logit_pair"] = np.ascontiguousarray(np.stack([lg[d].reshape(4, 2).T for d in range(2)], 0))
    sh["ret_gn_w"] = f(inp["ev_ret_gn_w"]).reshape(1, 1024)
    sh["gm_ln_w"] = f(inp["ev_gm_ln_w"]).reshape(1, 1024)
    sh["gm_ln_b"] = f(inp["ev_gm_ln_b"]).reshape(1, 1024)
    sh["gm_wsT"] = np.ascontiguousarray(f(inp["ev_gm_ws"])[0].transpose(2, 0, 1))
    sh["gm_bs"] = np.ascontiguousarray(f(inp["ev_gm_bs"])[0].T)
    wo = f(inp["od_w_in"])[0]
    sh["w_od_qkv"] = group_w(wo[:, 0:8192])
    sh["w_od_z"] = group_w(wo[:, 8192:12288])
    sh["w_od_g"] = group_w(wo[:, 12288:12416])
    sh["od_conv"] = np.ascontiguousarray(f(inp["od_conv_w"])[0].reshape(5, 64, 128).transpose(2, 1, 0))
    sh["od_a_log"] = f(inp["od_a_log"]).reshape(1, 64)
    sh["od_dt_bias"] = f(inp["od_dt_bias"]).reshape(1, 64)
    sh["od_norm_w"] = f(inp["od_norm_w"]).reshape(1, 128)
    sh["w_od_out"] = group_w(f(inp["od_w_out"])[0])
    for l in range(2):
        sh[f"w_gate{l}"] = group_w(f(inp["ffn_w_gate"])[l])
        sh[f"w_up{l}"] = group_w(f(inp["ffn_w_up"])[l])
        sh[f"w_down{l}"] = group_w(f(inp["ffn_w_down"])[l])
    sh.update(host_consts(max(SS, SP)))
    xs = np.asarray(inp["x_sample"], dtype=np.float32)
    xp = np.asarray(inp["x_prompt"], dtype=np.float32)
    maps = []
    for c in range(ncores):
        m = dict(sh)
        m["x_s"] = np.ascontiguousarray(xs[c % xs.shape[0]])
        m["x_p"] = np.ascontiguousarray(xp[c % xp.shape[0]])
        maps.append(m)
    return maps


_NC_CACHE = {}


def run(inp, SS, SP, nlayers=2, trace=False):
    key = (SS, SP, nlayers)
    if key not in _NC_CACHE:
        _NC_CACHE[key] = build_program(SS, SP, nlayers)
    nc = _NC_CACHE[key]
    maps = prep_inputs(inp, SS, SP)
    res = run_bass_kernel_spmd(nc, maps, core_ids=list(range(8)), trace=trace)
    ys = np.stack([res.results[c]["y_s"] for c in range(8)], 0)
    yp = np.stack([res.results[c]["y_p"] for c in range(4)], 0)
    return yp, ys, res


def kernel(**inputs):
    SS = inputs["x_sample"].shape[1]
    SP = inputs["x_prompt"].shape[1]
    yp, ys, _ = run(inputs, SS, SP)
    return (yp.astype(np.float32), ys.astype(np.float32))
```
